# Optimizing a Trainium2 kernel written in Bass

```python
import math
import jax
import jax.numpy as jnp
from jax import lax
import numpy as np

D_MODEL = 1024
BATCH = 8
SEQ = 2048
DEPTH = 2

LRU_WIDTH = D_MODEL
LRU_BLOCKS = 16
LRU_BLOCK = LRU_WIDTH // LRU_BLOCKS
LRU_C = 8.0
CONV_WIDTH = 4
RET_HEADS = 4
RET_QK_DIM = D_MODEL // RET_HEADS
RET_V_DIM = D_MODEL // RET_HEADS
RET_CHUNK = 128
ROPE_BASE = 10000.0
EVEN_IN = 2 * LRU_WIDTH + RET_HEADS * (2 * RET_QK_DIM + 2 * RET_V_DIM)
EVEN_MIX = LRU_WIDTH + RET_HEADS * RET_V_DIM
SSM_INNER = 2 * D_MODEL
SSM_HEAD_DIM = 64
SSM_HEADS = SSM_INNER // SSM_HEAD_DIM
SSM_GROUPS = 4
SSM_STATE = 128
SSM_CHUNK = 128
SSM_CONV_DIM = SSM_INNER + 2 * SSM_GROUPS * SSM_STATE
SSM_IN = SSM_INNER + SSM_CONV_DIM + SSM_HEADS
PEER_HEADS = 8
PEER_KEYS = 128
PEER_EXPERTS = PEER_KEYS * PEER_KEYS
PEER_TOPK = 16
PEER_KEY_DIM = 128
PEER_TOKEN_BLOCK = 128

EPS = 1e-6
N_EVEN = (DEPTH + 1) // 2
N_ODD = DEPTH // 2

kernel_name = 'hybrid_lru_retention_ssd_peer'


def rmsnorm(x, g):
    xf = x.astype(jnp.float32)
    y = xf * lax.rsqrt(jnp.mean(xf * xf, axis=-1, keepdims=True) + EPS)
    return (y * g.astype(jnp.float32)).astype(x.dtype)


def causal_conv(x, w, b):
    y = lax.conv_general_dilated(x, w[:, None, :].astype(x.dtype), window_strides=(1,),
                                 padding=[(CONV_WIDTH - 1, 0)],
                                 dimension_numbers=('NWC', 'WIO', 'NWC'),
                                 feature_group_count=x.shape[-1])
    return y + b.astype(x.dtype)


def rope(x):
    s_, d_ = x.shape[2], x.shape[3]
    half = d_ // 2
    inv = ROPE_BASE ** (-jnp.arange(half, dtype=jnp.float32) * (2.0 / d_))
    ang = jnp.arange(s_, dtype=jnp.float32)[:, None] * inv[None, :]
    cos, sin = jnp.cos(ang), jnp.sin(ang)
    xf = x.astype(jnp.float32)
    x1, x2 = xf[..., :half], xf[..., half:]
    return jnp.concatenate([x1 * cos - x2 * sin, x1 * sin + x2 * cos], axis=-1)


def rg_lru(x, w_a, b_a, w_i, b_i, lam):
    b_, s_, _ = x.shape
    xb = x.reshape(b_, s_, LRU_BLOCKS, LRU_BLOCK)
    r = jax.nn.sigmoid(jnp.einsum('bshi,hij->bshj', xb, w_a).reshape(b_, s_, LRU_WIDTH) + b_a)
    i = jax.nn.sigmoid(jnp.einsum('bshi,hij->bshj', xb, w_i).reshape(b_, s_, LRU_WIDTH) + b_i)
    log_a = (-LRU_C * r.astype(jnp.float32)) * jax.nn.softplus(-lam.astype(jnp.float32))
    a = jnp.exp(log_a)
    u = jnp.sqrt(-jnp.expm1(2.0 * log_a)) * (i * x).astype(jnp.float32)

    def combine(left, right):
        a1, u1 = left
        a2, u2 = right
        return a1 * a2, a2 * u1 + u2

    _, h = lax.associative_scan(combine, (a, u), axis=1)
    return h.astype(x.dtype)


def retention(q, k, v):
    b_, h_, s_, dk = q.shape
    dv = v.shape[-1]
    c = RET_CHUNK
    n = s_ // c
    log_gamma = jnp.log1p(-jnp.exp2(-5.0 - jnp.arange(h_, dtype=jnp.float32)))
    idx = jnp.arange(c, dtype=jnp.float32)
    rel = idx[:, None] - idx[None, :]
    decay_in = jnp.where(rel >= 0, jnp.exp(log_gamma[:, None, None] * jnp.maximum(rel, 0.0)), 0.0)
    qc = q.reshape(b_, h_, n, c, dk) * (dk ** -0.5)
    kc = k.reshape(b_, h_, n, c, dk)
    vc = v.astype(jnp.float32).reshape(b_, h_, n, c, dv)
    scores = jnp.einsum('bhnid,bhnjd->bhnij', qc, kc) * decay_in[None, :, None]
    y_in = jnp.einsum('bhnij,bhnje->bhnie', scores, vc)
    k_decay = jnp.exp(log_gamma[:, None] * (c - 1.0 - idx)[None, :])
    chunk_kv = jnp.einsum('bhnjd,hj,bhnje->nbhde', kc, k_decay, vc)
    chunk_decay = jnp.exp(log_gamma * c)

    def step(state, kv):
        return state * chunk_decay[None, :, None, None] + kv, state

    _, prev = lax.scan(step, jnp.zeros((b_, h_, dk, dv), jnp.float32), chunk_kv)
    q_decay = jnp.exp(log_gamma[:, None] * (idx + 1.0)[None, :])
    y_cross = jnp.einsum('bhnid,hi,nbhde->bhnie', qc, q_decay, prev)
    y = (y_in + y_cross).reshape(b_, h_, s_, dv)
    y = y * lax.rsqrt(jnp.mean(y * y, axis=-1, keepdims=True) + EPS)
    return y.transpose(0, 2, 1, 3).reshape(b_, s_, h_ * dv)


def lru_retention_mixer(h, w_in, lru_conv_w, lru_conv_b, lru_w_a, lru_b_a, lru_w_i, lru_b_i,
                        lru_lambda, w_out):
    b_, s_, _ = h.shape
    proj = h @ w_in
    o1 = LRU_WIDTH
    o2 = o1 + LRU_WIDTH
    o3 = o2 + RET_HEADS * RET_QK_DIM
    o4 = o3 + RET_HEADS * RET_QK_DIM
    o5 = o4 + RET_HEADS * RET_V_DIM
    gate_a, xa = proj[..., :o1], proj[..., o1:o2]
    q, k, v, gate_b = proj[..., o2:o3], proj[..., o3:o4], proj[..., o4:o5], proj[..., o5:]
    xa = causal_conv(xa, lru_conv_w, lru_conv_b)
    ya = jax.nn.gelu(gate_a) * rg_lru(xa, lru_w_a, lru_b_a, lru_w_i, lru_b_i, lru_lambda)
    qh = rope(q.reshape(b_, s_, RET_HEADS, RET_QK_DIM).transpose(0, 2, 1, 3))
    kh = rope(k.reshape(b_, s_, RET_HEADS, RET_QK_DIM).transpose(0, 2, 1, 3))
    vh = v.reshape(b_, s_, RET_HEADS, RET_V_DIM).transpose(0, 2, 1, 3)
    yb = (jax.nn.silu(gate_b.astype(jnp.float32)) * retention(qh, kh, vh)).astype(h.dtype)
    return jnp.concatenate([ya.astype(h.dtype), yb], axis=-1) @ w_out


def ssd(x, dt, a_neg, bm, cm):
    b_, s_, h_, p_ = x.shape
    c = SSM_CHUNK
    n = s_ // c
    g = SSM_GROUPS
    r = h_ // g
    xc = (x * dt[..., None]).reshape(b_, n, c, g, r, p_)
    a = (dt * a_neg).reshape(b_, n, c, g, r)
    a_cum = jnp.cumsum(a, axis=2)
    bc = bm.reshape(b_, n, c, g, SSM_STATE)
    cc = cm.reshape(b_, n, c, g, SSM_STATE)
    mask = jnp.tril(jnp.ones((c, c), dtype=bool))
    seg = a_cum[:, :, :, None] - a_cum[:, :, None, :]
    lmat = jnp.exp(jnp.where(mask[None, None, :, :, None, None], seg, -jnp.inf))
    cb = jnp.einsum('bnlgk,bnsgk->bnlsg', cc, bc)
    y_diag = jnp.einsum('bnlsg,bnlsgr,bnsgrp->bnlgrp', cb, lmat, xc)
    decay_states = jnp.exp(a_cum[:, :, -1:] - a_cum)
    states = jnp.einsum('bnsgk,bnsgr,bnsgrp->nbgrpk', bc, decay_states, xc)
    chunk_decay = jnp.moveaxis(jnp.exp(a_cum[:, :, -1]), 1, 0)

    def step(state, inp):
        dec, st = inp
        return state * dec[..., None, None] + st, state

    init = jnp.zeros((b_, g, r, p_, SSM_STATE), jnp.float32)
    _, prev = lax.scan(step, init, (chunk_decay, states))
    y_off = jnp.einsum('bnlgk,nbgrpk,bnlgr->bnlgrp', cc, prev, jnp.exp(a_cum))
    return (y_diag + y_off).reshape(b_, s_, h_, p_)


def mamba2_mixer(h, w_in, conv_w, conv_b, dt_bias, a_log, d_skip, norm_g, w_out):
    b_, s_, _ = h.shape
    proj = h @ w_in
    z = proj[..., :SSM_INNER]
    xbc = proj[..., SSM_INNER:SSM_INNER + SSM_CONV_DIM]
    dt_raw = proj[..., SSM_INNER + SSM_CONV_DIM:]
    xbc = jax.nn.silu(causal_conv(xbc, conv_w, conv_b)).astype(jnp.float32)
    xs = xbc[..., :SSM_INNER].reshape(b_, s_, SSM_HEADS, SSM_HEAD_DIM)
    bm = xbc[..., SSM_INNER:SSM_INNER + SSM_GROUPS * SSM_STATE].reshape(b_, s_, SSM_GROUPS, SSM_STATE)
    cm = xbc[..., SSM_INNER + SSM_GROUPS * SSM_STATE:].reshape(b_, s_, SSM_GROUPS, SSM_STATE)
    dt = jax.nn.softplus(dt_raw.astype(jnp.float32) + dt_bias.astype(jnp.float32))
    a_neg = -jnp.exp(a_log.astype(jnp.float32))
    y = ssd(xs, dt, a_neg, bm, cm) + d_skip.astype(jnp.float32)[:, None] * xs
    y = y.reshape(b_, s_, SSM_INNER) * jax.nn.silu(z.astype(jnp.float32))
    y = rmsnorm(y, norm_g).astype(h.dtype)
    return y @ w_out


def peer(h, w_q, sub_keys, u_emb, v_emb):
    b_, s_, d_ = h.shape
    t = h.reshape(-1, d_)
    n_tok = t.shape[0]
    q = (t @ w_q).reshape(n_tok, PEER_HEADS, 2, PEER_KEY_DIM // 2)
    scores = jnp.einsum('thpk,hpnk->thpn', q, sub_keys).astype(jnp.float32)
    s_top, i_top = lax.top_k(scores, PEER_TOPK)
    cand = s_top[:, :, 0, :, None] + s_top[:, :, 1, None, :]
    cand_idx = i_top[:, :, 0, :, None] * PEER_KEYS + i_top[:, :, 1, None, :]
    cand = cand.reshape(n_tok, PEER_HEADS, PEER_TOPK * PEER_TOPK)
    cand_idx = cand_idx.reshape(n_tok, PEER_HEADS, PEER_TOPK * PEER_TOPK)
    best, pos = lax.top_k(cand, PEER_TOPK)
    expert = jnp.take_along_axis(cand_idx, pos, axis=-1)
    gates = jax.nn.softmax(best, axis=-1).astype(h.dtype)
    nb = n_tok // PEER_TOKEN_BLOCK

    def block(args):
        tb, eb, gb = args
        u = u_emb[eb]
        act = jax.nn.gelu(jnp.einsum('td,thkd->thk', tb, u)) * gb
        return jnp.einsum('thk,thkd->td', act, v_emb[eb])

    out = lax.map(block, (t.reshape(nb, PEER_TOKEN_BLOCK, d_),
                          expert.reshape(nb, PEER_TOKEN_BLOCK, PEER_HEADS, PEER_TOPK),
                          gates.reshape(nb, PEER_TOKEN_BLOCK, PEER_HEADS, PEER_TOPK)))
    return out.reshape(b_, s_, d_).astype(h.dtype)


def setup_inputs(seed: int = 0) -> dict:
    key = jax.random.key(seed)
    keys = jax.random.split(key, 40)
    cnt = [0]
    f32 = jnp.float32

    def nk():
        k_ = keys[cnt[0]]
        cnt[0] += 1
        return k_

    def nrm(shape, scale):
        return jax.random.normal(nk(), shape, f32) * scale

    def gain(shape):
        return 1.0 + 0.02 * jax.random.normal(nk(), shape, f32)

    x = nrm((BATCH, SEQ, D_MODEL), 1.0)
    mix_norm = gain((DEPTH, D_MODEL))
    ffn_norm = gain((DEPTH, D_MODEL))
    final_norm = gain((D_MODEL,))
    even_w_in = nrm((N_EVEN, D_MODEL, EVEN_IN), D_MODEL ** -0.5)
    lru_conv_w = nrm((N_EVEN, CONV_WIDTH, LRU_WIDTH), CONV_WIDTH ** -0.5)
    lru_conv_b = nrm((N_EVEN, LRU_WIDTH), 0.01)
    lru_w_a = nrm((N_EVEN, LRU_BLOCKS, LRU_BLOCK, LRU_BLOCK), LRU_BLOCK ** -0.5)
    lru_b_a = nrm((N_EVEN, LRU_WIDTH), 0.01)
    lru_w_i = nrm((N_EVEN, LRU_BLOCKS, LRU_BLOCK, LRU_BLOCK), LRU_BLOCK ** -0.5)
    lru_b_i = nrm((N_EVEN, LRU_WIDTH), 0.01)
    a_pow_c = jax.random.uniform(nk(), (N_EVEN, LRU_WIDTH), f32, 0.9, 0.999)
    p = a_pow_c ** (1.0 / LRU_C)
    lru_lambda = jnp.log(p) - jnp.log1p(-p)
    even_w_out = nrm((N_EVEN, EVEN_MIX, D_MODEL), EVEN_MIX ** -0.5)
    ssm_w_in = nrm((N_ODD, D_MODEL, SSM_IN), D_MODEL ** -0.5)
    ssm_conv_w = nrm((N_ODD, CONV_WIDTH, SSM_CONV_DIM), CONV_WIDTH ** -0.5)
    ssm_conv_b = nrm((N_ODD, SSM_CONV_DIM), 0.01)
    dt0 = jnp.exp(jax.random.uniform(nk(), (N_ODD, SSM_HEADS), f32, math.log(1e-3), math.log(1e-1)))
    ssm_dt_bias = dt0 + jnp.log(-jnp.expm1(-dt0))
    ssm_a_log = jnp.log(jax.random.uniform(nk(), (N_ODD, SSM_HEADS), f32, 1.0, 16.0))
    ssm_d = gain((N_ODD, SSM_HEADS))
    ssm_norm = gain((N_ODD, SSM_INNER))
    ssm_w_out = nrm((N_ODD, SSM_INNER, D_MODEL), SSM_INNER ** -0.5)
    peer_w_q = nrm((DEPTH, D_MODEL, PEER_HEADS * PEER_KEY_DIM), D_MODEL ** -0.5)
    peer_sub_keys = nrm((DEPTH, PEER_HEADS, 2, PEER_KEYS, PEER_KEY_DIM // 2), (PEER_KEY_DIM // 2) ** -0.5)
    peer_u = nrm((DEPTH, PEER_EXPERTS, D_MODEL), D_MODEL ** -0.5)
    peer_v = nrm((DEPTH, PEER_EXPERTS, D_MODEL), (PEER_HEADS * PEER_TOPK) ** -0.5)
    return {'x': x, 'mix_norm': mix_norm, 'ffn_norm': ffn_norm, 'final_norm': final_norm,
            'even_w_in': even_w_in, 'lru_conv_w': lru_conv_w, 'lru_conv_b': lru_conv_b,
            'lru_w_a': lru_w_a, 'lru_b_a': lru_b_a, 'lru_w_i': lru_w_i, 'lru_b_i': lru_b_i,
            'lru_lambda': lru_lambda, 'even_w_out': even_w_out,
            'ssm_w_in': ssm_w_in, 'ssm_conv_w': ssm_conv_w, 'ssm_conv_b': ssm_conv_b,
            'ssm_dt_bias': ssm_dt_bias, 'ssm_a_log': ssm_a_log, 'ssm_d': ssm_d,
            'ssm_norm': ssm_norm, 'ssm_w_out': ssm_w_out,
            'peer_w_q': peer_w_q, 'peer_sub_keys': peer_sub_keys, 'peer_u': peer_u, 'peer_v': peer_v}


def reference(x, mix_norm, ffn_norm, final_norm, even_w_in, lru_conv_w, lru_conv_b, lru_w_a,
              lru_b_a, lru_w_i, lru_b_i, lru_lambda, even_w_out, ssm_w_in, ssm_conv_w, ssm_conv_b,
              ssm_dt_bias, ssm_a_log, ssm_d, ssm_norm, ssm_w_out, peer_w_q, peer_sub_keys,
              peer_u, peer_v):
    for layer in range(DEPTH):
        j = layer // 2
        hn = rmsnorm(x, mix_norm[layer])
        if layer % 2 == 0:
            x = x + lru_retention_mixer(hn, even_w_in[j], lru_conv_w[j], lru_conv_b[j], lru_w_a[j],
                                        lru_b_a[j], lru_w_i[j], lru_b_i[j], lru_lambda[j],
                                        even_w_out[j])
        else:
            x = x + mamba2_mixer(hn, ssm_w_in[j], ssm_conv_w[j], ssm_conv_b[j], ssm_dt_bias[j],
                                 ssm_a_log[j], ssm_d[j], ssm_norm[j], ssm_w_out[j])
        hn = rmsnorm(x, ffn_norm[layer])
        x = x + peer(hn, peer_w_q[layer], peer_sub_keys[layer], peer_u[layer], peer_v[layer])
    return rmsnorm(x, final_norm)
```

```python
import numpy as np
from contextlib import ExitStack
import concourse.bass as bass
import concourse.mybir as mybir
from concourse.bass_types import AP
from concourse.bass_utils import run_bass_kernel_spmd

F32 = mybir.dt.float32
BF16 = mybir.dt.bfloat16
AF = mybir.ActivationFunctionType
ALU = mybir.AluOpType
AX = mybir.AxisListType

S = 2048
D = 1024
NT = 16
EPS = 1e-6
NEG = -1.0e30


class Buf:
    __slots__ = ("w", "r", "name")

    def __init__(self, name=""):
        self.w = None
        self.r = {}
        self.name = name


class Ctx:
    def __init__(self, nc, es):
        self.nc = nc
        self.es = es
        self.eng = {"pe": nc.tensor, "act": nc.scalar, "dve": nc.vector, "pool": nc.gpsimd, "sp": nc.sync}
        self.semobj = {}
        self.val = {}
        self.seen = {k: {} for k in self.eng}
        for k in self.eng:
            self.newsem(k)

    def newsem(self, name):
        self.semobj[name] = self.es.enter_context(self.nc.semaphore("s_" + name))
        self.val[name] = 0
        return name

    def sbuf(self, name, shape, dt):
        return self.es.enter_context(self.nc.sbuf_tensor(name, list(shape), dt))

    def _deps(self, e, reads, writes):
        need = {}

        def add(s, v):
            if s == "pe" and e == "pe":
                return
            if need.get(s, 0) < v:
                need[s] = v

        for b in reads:
            if b.w is not None:
                add(*b.w)
        for b in writes:
            if b.w is not None:
                add(*b.w)
            for s, v in b.r.items():
                add(s, v)
        for s, v in need.items():
            if self.seen[e].get(s, 0) < v:
                self.eng[e].wait_ge(self.semobj[s], v)
                self.seen[e][s] = v

    def op(self, e, fn, reads=(), writes=()):
        self._deps(e, reads, writes)
        self.val[e] += 1
        v = self.val[e]
        fn(self.eng[e]).then_inc(self.semobj[e], 1)
        for b in reads:
            if b.r.get(e, 0) < v:
                b.r[e] = v
        for b in writes:
            b.w = (e, v)
            b.r = {}

    def dma(self, q, sem, out, in_, reads=(), writes=(), **kw):
        self._deps(q, reads, writes)
        self.val[sem] += 16
        v = self.val[sem]
        self.eng[q].dma_start(out=out, in_=in_, **kw).then_inc(self.semobj[sem], 16)
        for b in reads:
            b.r[sem] = v
        for b in writes:
            b.w = (sem, v)
            b.r = {}

    def barrier(self):
        for e in self.eng:
            for s, v in self.val.items():
                if s == e or v == 0:
                    continue
                if self.seen[e].get(s, 0) < v:
                    self.eng[e].wait_ge(self.semobj[s], v)
                    self.seen[e][s] = v


def bcast_last(ap2d, n):
    return ap2d.unsqueeze(2).broadcast_to([ap2d.shape[0], ap2d.shape[1], n])


class Kern:
    def __init__(self, nc, es, dram):
        self.nc = nc
        self.c = Ctx(nc, es)
        self.es = es
        self.dr = dram
        c = self.c
        self.xres = c.sbuf("xres", [128, NT, D], F32)
        self.xb = [Buf("x%d" % i) for i in range(NT)]
        self.hnT = c.sbuf("hnT", [128, 8, S], BF16)
        self.hnTb = [Buf("hnT%d" % i) for i in range(NT)]
        self.ident = c.sbuf("ident", [128, 128], BF16)
        self.identb = Buf("ident")
        self.ps = es.enter_context(nc.psum_tensor("ps", [128, 4096], F32))
        self.pb = [Buf("bank%d" % i) for i in range(8)]
        self.stat = c.sbuf("stat", [128, 64], F32)
        self.statb = Buf("stat")
        self.gbc = c.sbuf("gbc", [128, D], F32)
        self.gbcb = Buf("gbc")
        self.hn = [c.sbuf("hn%d" % i, [128, D], BF16) for i in range(2)]
        self.hnb = [Buf("hn%d" % i) for i in range(2)]
        for n in ("ld0", "ld1", "ld2", "ld3", "ld4", "ld5", "io"):
            c.newsem(n)
        self.make_ident()

    def bank(self, i, n=512):
        return self.ps[:, i * 512:i * 512 + n]

    def bank_bf(self, i):
        return self.ps[:, i * 512:(i + 1) * 512].bitcast(BF16)

    def make_ident(self):
        c = self.c
        tmp = c.sbuf("identf", [128, 128], F32)
        tb = Buf()
        c.op("pool", lambda e: e.memset(tmp[:], 1.0), writes=[tb])
        c.op("pool", lambda e: e.affine_select(out=tmp[:], in_=tmp[:], pattern=[[1, 128]], compare_op=ALU.is_equal,
                                               fill=0.0, base=0, channel_multiplier=-1), reads=[tb], writes=[tb])
        c.op("dve", lambda e: e.tensor_copy(out=self.ident[:], in_=tmp[:]), reads=[tb], writes=[self.identb])
        self.identf = tmp
        self.identfb = tb

    def load_x(self):
        c = self.c
        x = self.dr["x"]
        for tt in range(NT):
            c.dma("sp", "io", self.xres[:, tt, :], x[tt * 128:(tt + 1) * 128, :], writes=[self.xb[tt]])
        for tt in range(NT):
            self.xb[tt].w = ("io", c.val["io"])

    def store_x(self, name="y"):
        c = self.c
        y = self.dr[name]
        for tt in range(NT):
            c.dma("sp", "io", y[tt * 128:(tt + 1) * 128, :], self.xres[:, tt, :], reads=[self.xb[tt]])
        c.eng["sp"].wait_ge(c.semobj["io"], c.val["io"])

    def norm_stats(self):
        c = self.c
        junk = self.hn[0]
        for tt in range(NT):
            c.op("act", lambda e, tt=tt: e.activation(out=junk[:], in_=self.xres[:, tt, :], func=AF.Square,
                                                     accum_out=self.stat[:, tt:tt + 1]),
                 reads=[self.xb[tt]], writes=[self.hnb[0], self.statb])
        c.op("act", lambda e: e.activation(out=self.stat[:, 32:48], in_=self.stat[:, 0:16], func=AF.Sqrt, bias=EPS,
                                           scale=1.0 / D), reads=[self.statb], writes=[self.statb])
        c.op("dve", lambda e: e.reciprocal(out=self.stat[:, 16:32], in_=self.stat[:, 32:48]), reads=[self.statb],
             writes=[self.statb])

    def load_gain(self, gain_ap):
        self.c.dma("sp", "ld5", self.gbc[:], gain_ap.partition_broadcast(128), writes=[self.gbcb])

    def norm_to_hnT(self, gain_ap):
        c = self.c
        self.load_gain(gain_ap)
        self.norm_stats()
        for tt in range(NT):
            k = tt % 2
            hn = self.hn[k]
            c.op("dve", lambda e, tt=tt, hn=hn: e.scalar_tensor_tensor(out=hn[:], in0=self.xres[:, tt, :],
                                                                      scalar=self.stat[:, 16 + tt:17 + tt],
                                                                      in1=self.gbc[:], op0=ALU.mult, op1=ALU.mult),
                 reads=[self.xb[tt], self.statb, self.gbcb], writes=[self.hnb[k]])
            bk = 6 + k
            for dc in range(8):
                c.op("pe", lambda e, dc=dc, hn=hn, bk=bk: e.transpose(out=self.bank_bf(bk)[:, dc * 128:(dc + 1) * 128],
                                                                     in_=hn[:, dc * 128:(dc + 1) * 128],
                                                                     identity=self.ident[:]),
                     reads=[self.hnb[k], self.identb], writes=[self.pb[bk]])
            src = self.bank_bf(bk).rearrange("p (a b) -> p a b", a=8)
            c.op("act", lambda e, tt=tt, src=src: e.activation(out=self.hnT[:, :, tt * 128:(tt + 1) * 128], in_=src,
                                                              func=AF.Copy),
                 reads=[self.pb[bk]], writes=[self.hnTb[tt]])

    def peer(self, l):
        c = self.c
        nc = self.nc
        es = ExitStack()
        self.es.enter_context(es)
        sb = lambda name, shape, dt: es.enter_context(nc.sbuf_tensor(name + "_%d" % l, list(shape), dt))
        dr = self.dr
        self.norm_to_hnT(dr["ffn_norm"][l])

        qT = sb("qT", [128, 8, S], BF16)
        qTb = [Buf() for _ in range(8)]
        thr = sb("thr", [128, NT, 8], F32)
        negc = sb("negc", [128, NT, 8], F32)
        thrb = [Buf() for _ in range(NT)]
        sk1T = sb("sk1T", [64, 8, 128], BF16)
        sk1Tb = Buf()
        KX = [sb("KX%d" % i, [128, 8, 512], BF16) for i in range(2)]
        KXlo = [Buf() for _ in range(2)]
        KXhi = Buf()

        with ExitStack() as pes:
            pb_ = lambda name, shape, dt: pes.enter_context(nc.sbuf_tensor(name + "_%d" % l, list(shape), dt))
            Wq = pb_("Wq", [128, 8, D], BF16)
            Wqb = Buf()
            c.dma("pool", "ld0", Wq[:], dr["peer_w_q"][l].rearrange("(dc p) f -> p dc f", p=128), writes=[Wqb])
            skn = pb_("skn", [128, 16, 64], BF16)
            sknb = Buf()
            c.dma("pool", "ld1", skn[:], dr["peer_sub_keys"][l].rearrange("h p n k -> n (h p) k"), writes=[sknb])
            SKBD = pb_("SKBD", [128, 8, 256], BF16)
            SKBDb = Buf()
            c.op("pool", lambda e: e.memset(SKBD[:], 0.0), writes=[SKBDb])
            for h in range(8):
                bk = h % 2
                c.op("pe", lambda e, h=h, bk=bk: e.transpose(out=self.bank_bf(bk)[:, 0:128],
                                                            in_=skn[:, 2 * h:2 * h + 2, :].rearrange("p a b -> p (a b)"),
                                                            identity=self.ident[:]),
                     reads=[sknb, self.identb], writes=[self.pb[bk]])
                c.op("dve", lambda e, h=h, bk=bk: e.tensor_copy(out=SKBD[0:64, h, 0:128], in_=self.bank_bf(bk)[0:64, 0:128]),
                     reads=[self.pb[bk]], writes=[SKBDb])
                c.op("dve", lambda e, h=h, bk=bk: e.tensor_copy(out=SKBD[64:128, h, 128:256],
                                                               in_=self.bank_bf(bk)[64:128, 0:128]),
                     reads=[self.pb[bk]], writes=[SKBDb])
                c.op("act", lambda e, h=h, bk=bk: e.activation(out=sk1T[:, h, :], in_=self.bank_bf(bk)[0:64, 0:128],
                                                              func=AF.Copy),
                     reads=[self.pb[bk]], writes=[sk1Tb])
                for kb in range(2):
                    c.op("act", lambda e, h=h, bk=bk, kb=kb: e.activation(
                        out=KX[kb][64:128, h, :].rearrange("p (a b) -> p a b", a=4),
                        in_=self.bank_bf(bk)[64:128, 0:128].unsqueeze(1).to_broadcast([64, 4, 128]), func=AF.Copy),
                         reads=[self.pb[bk]], writes=[KXhi])
            for h in range(8):
                for tb in range(4):
                    bk = 2 + (h * 4 + tb) % 4
                    for dc in range(8):
                        c.op("pe", lambda e, h=h, tb=tb, dc=dc, bk=bk: e.matmul(
                            self.bank(bk), lhsT=Wq[:, dc, h * 128:(h + 1) * 128], rhs=self.hnT[:, dc, tb * 512:(tb + 1) * 512],
                            start=(dc == 0), stop=(dc == 7)),
                             reads=[Wqb] + self.hnTb[tb * 4:tb * 4 + 4], writes=[self.pb[bk]])
                    eng = "act" if tb % 2 == 0 else "dve"
                    if eng == "act":
                        c.op("act", lambda e, h=h, tb=tb, bk=bk: e.activation(out=qT[:, h, tb * 512:(tb + 1) * 512],
                                                                             in_=self.bank(bk), func=AF.Copy),
                             reads=[self.pb[bk]], writes=[qTb[h]])
                    else:
                        c.op("dve", lambda e, h=h, tb=tb, bk=bk: e.tensor_copy(out=qT[:, h, tb * 512:(tb + 1) * 512],
                                                                              in_=self.bank(bk)),
                             reads=[self.pb[bk]], writes=[qTb[h]])
            SC = [pb_("SC%d" % i, [128, 16, 128], F32) for i in range(2)]
            SCb = [Buf() for _ in range(2)]
            TOP = pb_("TOP", [128, 16, 24], F32)
            TOPb = Buf()
            CAND = pb_("CAND", [128, 8, 289], F32)
            CANDb = Buf()
            CT = pb_("CT", [128, 8, 24], F32)
            CTb = Buf()
            sm = pb_("sm", [128, 8, 16], F32)
            smb = Buf()
            for tt in range(NT):
                k = tt % 2
                sc = SC[k]
                for h in range(8):
                    bk = 2 * k + h // 4
                    bk = 4 * k + h // 2
                    c.op("pe", lambda e, h=h, tt=tt, bk=bk: e.matmul(
                        self.ps[:, bk * 512 + (h % 2) * 256: bk * 512 + (h % 2) * 256 + 256],
                        lhsT=qT[:, h, tt * 128:(tt + 1) * 128], rhs=SKBD[:, h, :], start=True, stop=True),
                         reads=[qTb[h], SKBDb], writes=[self.pb[bk]])
                for q in range(4):
                    bk = 4 * k + q
                    c.op("act", lambda e, q=q, bk=bk, sc=sc: e.activation(
                        out=sc[:, 4 * q:4 * q + 4, :].rearrange("p a b -> p (a b)"), in_=self.bank(bk), func=AF.Copy),
                         reads=[self.pb[bk]], writes=[SCb[k]])
                for g in range(16):
                    c.op("dve", lambda e, g=g, sc=sc: e.max(out=TOP[:, g, 0:8], in_=sc[:, g, :]), reads=[SCb[k]], writes=[TOPb])
                    c.op("dve", lambda e, g=g, sc=sc: e.match_replace(out=sc[:, g, :], in_to_replace=TOP[:, g, 0:8],
                                                                     in_values=sc[:, g, :], imm_value=NEG),
                         reads=[TOPb, SCb[k]], writes=[SCb[k]])
                    c.op("dve", lambda e, g=g, sc=sc: e.max(out=TOP[:, g, 8:16], in_=sc[:, g, :]), reads=[SCb[k]], writes=[TOPb])
                    c.op("dve", lambda e, g=g, sc=sc: e.match_replace(out=sc[:, g, :], in_to_replace=TOP[:, g, 8:16],
                                                                     in_values=sc[:, g, :], imm_value=NEG),
                         reads=[TOPb, SCb[k]], writes=[SCb[k]])
                c.op("dve", lambda e, sc=sc: e.tensor_reduce(out=TOP[:, :, 16:17], in_=sc[:], axis=AX.X, op=ALU.max),
                     reads=[SCb[k]], writes=[TOPb])
                in0 = AP(TOP, 0, [[384, 128], [48, 8], [1, 17], [0, 17]])
                in1 = AP(TOP, 24, [[384, 128], [48, 8], [0, 17], [1, 17]])
                c.op("dve", lambda e, in0=in0, in1=in1: e.tensor_tensor(
                    out=CAND[:].rearrange("p h (a b) -> p h a b", a=17), in0=in0, in1=in1, op=ALU.add),
                     reads=[TOPb], writes=[CANDb])
                for h in range(8):
                    c.op("dve", lambda e, h=h: e.max(out=CT[:, h, 0:8], in_=CAND[:, h, :]), reads=[CANDb], writes=[CTb])
                    c.op("dve", lambda e, h=h: e.match_replace(out=CAND[:, h, :], in_to_replace=CT[:, h, 0:8],
                                                              in_values=CAND[:, h, :], imm_value=NEG),
                         reads=[CTb, CANDb], writes=[CANDb])
                    c.op("dve", lambda e, h=h: e.max(out=CT[:, h, 8:16], in_=CAND[:, h, :]), reads=[CANDb], writes=[CTb])
                    c.op("dve", lambda e, h=h: e.match_replace(out=CAND[:, h, :], in_to_replace=CT[:, h, 8:16],
                                                              in_values=CAND[:, h, :], imm_value=NEG),
                         reads=[CTb, CANDb], writes=[CANDb])
                c.op("dve", lambda e: e.tensor_reduce(out=CT[:, :, 16:17], in_=CAND[:], axis=AX.X, op=ALU.max),
                     reads=[CANDb], writes=[CTb])
                c.op("dve", lambda e, tt=tt: e.tensor_tensor(out=thr[:, tt, :].unsqueeze(2), in0=CT[:, :, 15:16],
                                                            in1=CT[:, :, 16:17], op=ALU.add),
                     reads=[CTb], writes=[thrb[tt]])
                c.op("dve", lambda e, tt=tt: e.tensor_scalar(out=thr[:, tt, :], in0=thr[:, tt, :], scalar1=0.5,
                                                            scalar2=None, op0=ALU.mult),
                     reads=[thrb[tt]], writes=[thrb[tt]])
                c.op("dve", lambda e: e.tensor_tensor(out=sm[:], in0=CT[:, :, 0:16],
                                                      in1=CT[:, :, 0:1].to_broadcast([128, 8, 16]), op=ALU.subtract),
                     reads=[CTb], writes=[smb])
                c.op("act", lambda e: e.activation(out=sm[:], in_=sm[:], func=AF.Exp), reads=[smb], writes=[smb])
                c.op("dve", lambda e: e.tensor_reduce(out=CT[:, :, 17:18], in_=sm[:], axis=AX.X, op=ALU.add),
                     reads=[smb], writes=[CTb])
                c.op("act", lambda e: e.activation(out=CT[:, :, 18:19], in_=CT[:, :, 17:18], func=AF.Ln),
                     reads=[CTb], writes=[CTb])
                c.op("dve", lambda e, tt=tt: e.scalar_tensor_tensor(out=negc[:, tt, :].unsqueeze(2), in0=CT[:, :, 18:19],
                                                                   scalar=-1.0, in1=CT[:, :, 0:1], op0=ALU.mult,
                                                                   op1=ALU.subtract),
                     reads=[CTb], writes=[thrb[tt]])
            c.barrier()
        u = dr["peer_u"][l]
        v = dr["peer_v"][l]
        Ubf = sb("Ubf", [128, 4, D], BF16)
        Ubfb = Buf()
        UT = [sb("UT%d" % i, [128, 8, 512], BF16) for i in range(2)]
        UTb = [Buf() for _ in range(2)]
        Vb = [sb("Vb%d" % i, [128, 4, D], BF16) for i in range(2)]
        Vbb = [Buf() for _ in range(2)]
        GA = [sb("GA%d" % i, [128, 512], BF16) for i in range(2)]
        GAb = [Buf() for _ in range(2)]
        E = [sb("E%d" % i, [128, 512], BF16) for i in range(2)]
        Eb = [Buf() for _ in range(2)]
        G = [sb("G%d" % i, [128, 512], BF16) for i in range(3)]
        Gb = [Buf() for _ in range(3)]
        WTs = [sb("WTs%d" % i, [128, 512], BF16) for i in range(2)]
        WTsb = [Buf() for _ in range(2)]
        PT = [sb("PT%d" % i, [128, 512], BF16) for i in range(2)]
        PTb = [Buf() for _ in range(2)]
        NEB = 32
        B_S = (0, 1)
        B_AT = 2
        B_WT = (3, 7)
        B_O = (4, 5)
        B_UT = 6

        def load_eb(eb):
            c.dma("pool", "ld2", Ubf[:], u[eb * 512:(eb + 1) * 512, :].rearrange("(ec p) d -> p ec d", p=128),
                  writes=[Ubfb])
            c.dma("pool", "ld3" if eb % 2 == 0 else "ld4", Vb[eb % 2][:],
                  v[eb * 512:(eb + 1) * 512, :].rearrange("(ec p) d -> p ec d", p=128), writes=[Vbb[eb % 2]])
            kx = KX[eb % 2]
            c.op("pool", lambda e: e.tensor_copy(
                out=kx[0:64, :, :].rearrange("p h (a b) -> p h a b", a=4),
                in_=sk1T[:, :, eb * 4:eb * 4 + 4].unsqueeze(3).broadcast_to([64, 8, 4, 128])),
                 reads=[sk1Tb], writes=[KXlo[eb % 2]])

        def transp_eb(eb):
            ut = UT[eb % 2]
            for dcp in range(4):
                for j in range(2):
                    dc = dcp * 2 + j
                    for ec in range(4):
                        c.op("pe", lambda e, dc=dc, ec=ec, j=j: e.transpose(
                            out=self.bank_bf(B_UT)[:, j * 512 + ec * 128: j * 512 + (ec + 1) * 128],
                            in_=Ubf[:, ec, dc * 128:(dc + 1) * 128], identity=self.ident[:]),
                             reads=[Ubfb, self.identb], writes=[self.pb[B_UT]])
                c.op("act", lambda e, dcp=dcp: e.activation(
                    out=ut[:, 2 * dcp:2 * dcp + 2, :].rearrange("p a b -> p (a b)"), in_=self.bank_bf(B_UT), func=AF.Copy),
                     reads=[self.pb[B_UT]], writes=[UTb[eb % 2]])

        def emit_AT(eb, tt, un):
            ut = UT[eb % 2]
            for ec in range(4):
                for dc in range(8):
                    c.op("pe", lambda e, ec=ec, dc=dc: e.matmul(
                        self.ps[:, B_AT * 512 + ec * 128: B_AT * 512 + (ec + 1) * 128],
                        lhsT=ut[:, dc, ec * 128:(ec + 1) * 128], rhs=self.hnT[:, dc, tt * 128:(tt + 1) * 128],
                        start=(dc == 0), stop=(dc == 7)),
                         reads=[UTb[eb % 2], self.hnTb[tt]], writes=[self.pb[B_AT]])
            ga = GA[un % 2]
            c.op("act", lambda e: e.activation(out=ga[:], in_=self.bank(B_AT), func=self.gelu_func),
                 reads=[self.pb[B_AT]], writes=[GAb[un % 2]])

        def emit_S(eb, tt, h, si):
            bk = B_S[si % 2]
            kx = KX[eb % 2]
            c.op("pe", lambda e: e.matmul(self.bank(bk), lhsT=qT[:, h, tt * 128:(tt + 1) * 128], rhs=kx[:, h, :],
                                          start=True, stop=True),
                 reads=[qTb[h], KXlo[eb % 2], KXhi], writes=[self.pb[bk]])

        un = 0
        si = 0
        load_eb(0)
        transp_eb(0)
        for eb in range(NEB):
            if eb + 1 < NEB:
                load_eb(eb + 1)
            for tt in range(NT):
                if tt == 0:
                    emit_AT(eb, tt, un)
                s_base = si
                emit_S(eb, tt, 0, s_base)
                emit_S(eb, tt, 1, s_base + 1)
                wtb = B_WT[un % 2]
                for h in range(8):
                    bk = B_S[(s_base + h) % 2]
                    ei = (s_base + h) % 3
                    e2 = (s_base + h) % 2
                    c.op("act", lambda e, h=h, bk=bk, e2=e2: e.activation(out=E[e2][:], in_=self.bank(bk), func=AF.Exp,
                                                                         bias=negc[:, tt, h:h + 1]),
                         reads=[self.pb[bk], thrb[tt]], writes=[Eb[e2]])
                    c.op("dve", lambda e, h=h, bk=bk, ei=ei, e2=e2: e.scalar_tensor_tensor(
                        out=G[ei][:], in0=self.bank(bk), scalar=thr[:, tt, h:h + 1], in1=E[e2][:], op0=ALU.is_gt,
                        op1=ALU.mult), reads=[self.pb[bk], thrb[tt], Eb[e2]], writes=[Gb[ei]])
                    for ec in range(4):
                        c.op("pe", lambda e, h=h, ec=ec, ei=ei: e.matmul(
                            self.ps[:, wtb * 512 + ec * 128: wtb * 512 + (ec + 1) * 128],
                            lhsT=G[ei][:, ec * 128:(ec + 1) * 128], rhs=self.ident[:],
                            start=(h == 0 and ec == 0), stop=(h == 7), skip_group_check=True),
                             reads=[Gb[ei], self.identb], writes=[self.pb[wtb]])
                    if h + 2 < 8:
                        emit_S(eb, tt, h + 2, s_base + h + 2)
                si = s_base + 8
                nxt = (eb, tt + 1) if tt + 1 < NT else ((eb + 1, 0) if eb + 1 < NEB else None)
                if tt == 8 and eb + 1 < NEB:
                    transp_eb(eb + 1)
                if nxt is not None and nxt[1] != 0:
                    emit_AT(nxt[0], nxt[1], un + 1)
                c.op("act", lambda e: e.activation(out=WTs[un % 2][:], in_=self.bank(wtb), func=AF.Copy),
                     reads=[self.pb[wtb]], writes=[WTsb[un % 2]])
                c.op("pool", lambda e: e.tensor_tensor(out=PT[un % 2][:], in0=WTs[un % 2][:], in1=GA[un % 2][:],
                                                       op=ALU.mult),
                     reads=[WTsb[un % 2], GAb[un % 2]], writes=[PTb[un % 2]])
                vb = Vb[eb % 2]
                for dh in range(2):
                    for ec in range(4):
                        c.op("pe", lambda e, dh=dh, ec=ec: e.matmul(
                            self.bank(B_O[dh]), lhsT=PT[un % 2][:, ec * 128:(ec + 1) * 128],
                            rhs=vb[:, ec, dh * 512:(dh + 1) * 512], start=(ec == 0), stop=(ec == 3)),
                             reads=[PTb[un % 2], Vbb[eb % 2]], writes=[self.pb[B_O[dh]]])
                for dh in range(2):
                    c.op("dve", lambda e, dh=dh: e.tensor_tensor(out=self.xres[:, tt, dh * 512:(dh + 1) * 512],
                                                                in0=self.bank(B_O[dh]),
                                                                in1=self.xres[:, tt, dh * 512:(dh + 1) * 512], op=ALU.add),
                         reads=[self.pb[B_O[dh]], self.xb[tt]], writes=[self.xb[tt]])
                un += 1
        c.barrier()
        es.close()

    gelu_func = AF.Gelu

    def ring_open(self, es, nslots, tag):
        self.ring = [es.enter_context(self.nc.sbuf_tensor("ring%s_%d" % (tag, i), [128, 8, 256], BF16)) for i in range(nslots)]
        self.ringb = [Buf() for _ in range(nslots)]
        self.ringsem = []
        for i in range(nslots):
            n = "rg%d" % i
            if n not in self.c.semobj:
                self.c.newsem(n)
            self.ringsem.append(n)
        self.ringi = 0

    def wload_cols(self, w2d, cols):
        i = self.ringi % len(self.ring)
        self.ringi += 1
        t, b = self.ring[i], self.ringb[i]
        off = 0
        for (c0, n) in cols:
            self.c.dma("pool", self.ringsem[i], t[:, :, off:off + n],
                       w2d[:, c0:c0 + n].rearrange("(dc p) f -> p dc f", p=128), writes=[b])
            off += n
        return t, b

    def wload_rows(self, w2d, row0):
        i = self.ringi % len(self.ring)
        self.ringi += 1
        t, b = self.ring[i], self.ringb[i]
        v = t[:].rearrange("p a b -> p (a b)").rearrange("p (a b) -> p a b", a=2)
        self.c.dma("pool", self.ringsem[i], v, w2d[row0:row0 + 256, :].rearrange("(a p) d -> p a d", p=128), writes=[b])
        return v, b

    def even_mixer(self, j=0):
        c = self.c
        nc = self.nc
        dr = self.dr
        es = ExitStack()
        self.es.enter_context(es)
        sb = lambda name, shape, dt: es.enter_context(nc.sbuf_tensor(name + "_e%d" % j, list(shape), dt))
        self.norm_to_hnT(dr["mix_norm"][2 * j])
        win = dr["even_w_in"][j]
        wout = dr["even_w_out"][j]
        self.ring_open(es, 6, "e")
        PB = self.pb

        prm = sb("prm", [128, 8, 8], F32)
        prmb = Buf()
        kw = dict(allow_slow_non_contiguous=True)
        for k in range(4):
            c.dma("sp", "ld0", prm[:, :, k:k + 1], dr["lru_conv_w"][j][k].rearrange("(c p o) -> p c o", p=128, o=1),
                  writes=[prmb], **kw)
        for idx, nm in ((4, "lru_conv_b"), (5, "lru_b_a"), (6, "lru_b_i"), (7, "lru_lambda")):
            c.dma("sp", "ld0", prm[:, :, idx:idx + 1], dr[nm][j].rearrange("(c p o) -> p c o", p=128, o=1), writes=[prmb], **kw)
        c.op("act", lambda e: e.activation(out=prm[:, :, 7:8], in_=prm[:, :, 7:8], func=AF.Exp, scale=-1.0), reads=[prmb], writes=[prmb])
        c.op("act", lambda e: e.activation(out=prm[:, :, 7:8], in_=prm[:, :, 7:8], func=AF.Ln, bias=1.0), reads=[prmb], writes=[prmb])
        c.op("dve", lambda e: e.tensor_scalar(out=prm[:, :, 7:8], in0=prm[:, :, 7:8], scalar1=-8.0, scalar2=None, op0=ALU.mult),
             reads=[prmb], writes=[prmb])
        WBD = sb("WBD", [128, 2, 8, 128], BF16)
        WBDb = Buf()
        c.op("pool", lambda e: e.memset(WBD[:], 0.0), writes=[WBDb])
        for gi, nm in enumerate(("lru_w_a", "lru_w_i")):
            for hl in range(2):
                c.dma("pool", "ld1", WBD[hl * 64:(hl + 1) * 64, gi, :, hl * 64:(hl + 1) * 64],
                      dr[nm][j].rearrange("(c two) i j -> two i c j", two=2)[hl], writes=[WBDb])

        YT = sb("YT", [128, 8, S], BF16)
        YTb = [Buf() for _ in range(8)]
        with ExitStack() as les:
            lb = lambda name, shape, dt: les.enter_context(nc.sbuf_tensor(name + "_e%d" % j, list(shape), dt))
            T0 = lb("T0", [128, 3 + S], F32)
            T0b = Buf()
            T = [lb("T%d" % i, [128, 1024], F32) for i in range(1, 5)]
            Tb = [Buf() for _ in range(4)]
            XCb = lb("XCb", [128, 1024], BF16)
            XCbb = Buf()
            GAg = lb("GAg", [128, 1024], BF16)
            GAgb = Buf()
            hl_ = lb("hlast", [128, 2], F32)
            hlb = Buf()
            c.op("pool", lambda e: e.memset(T0[:, 0:3], 0.0), writes=[T0b])
            for ch in range(8):
                wt, wb = self.wload_cols(win, [(ch * 128, 128), (1024 + ch * 128, 128)])
                for hf in range(2):
                    t0 = hf * 1024
                    for q in range(2):
                        for dc in range(8):
                            c.op("pe", lambda e: e.matmul(self.bank(q), lhsT=wt[:, dc, 128:256],
                                                          rhs=self.hnT[:, dc, t0 + q * 512:t0 + (q + 1) * 512],
                                                          start=(dc == 0), stop=(dc == 7)),
                                 reads=[wb] + self.hnTb[hf * 8 + q * 4: hf * 8 + q * 4 + 4], writes=[PB[q]])
                    for q in range(2):
                        for dc in range(8):
                            c.op("pe", lambda e: e.matmul(self.bank(2 + q), lhsT=wt[:, dc, 0:128],
                                                          rhs=self.hnT[:, dc, t0 + q * 512:t0 + (q + 1) * 512],
                                                          start=(dc == 0), stop=(dc == 7)),
                                 reads=[wb] + self.hnTb[hf * 8 + q * 4: hf * 8 + q * 4 + 4], writes=[PB[2 + q]])
                    c.op("act", lambda e: e.activation(out=T0[:, 3 + t0:3 + t0 + 1024], in_=self.ps[:, 0:1024], func=AF.Copy),
                         reads=[PB[0], PB[1]], writes=[T0b])
                    XC = T[0]
                    c.op("dve", lambda e: e.tensor_scalar(out=XC[:], in0=T0[:, 3 + t0:3 + t0 + 1024], scalar1=prm[:, ch, 3:4],
                                                          scalar2=prm[:, ch, 4:5], op0=ALU.mult, op1=ALU.add),
                         reads=[T0b, prmb], writes=[Tb[0]])
                    for k in range(3):
                        c.op("dve", lambda e: e.scalar_tensor_tensor(out=XC[:], in0=T0[:, k + t0:k + t0 + 1024],
                                                                     scalar=prm[:, ch, k:k + 1], in1=XC[:], op0=ALU.mult,
                                                                     op1=ALU.add), reads=[T0b, prmb, Tb[0]], writes=[Tb[0]])
                    c.op("pool", lambda e: e.tensor_copy(out=XCb[:], in_=XC[:]), reads=[Tb[0]], writes=[XCbb])
                    for gi in range(2):
                        for q in range(2):
                            bk = 4 + gi * 2 + q
                            c.op("pe", lambda e: e.matmul(self.bank(bk), lhsT=WBD[:, gi, ch, :], rhs=XCb[:, q * 512:(q + 1) * 512],
                                                          start=True, stop=True), reads=[WBDb, XCbb], writes=[PB[bk]])
                    A, M, IS = T[1], T[2], T[3]
                    c.op("act", lambda e: e.activation(out=A[:], in_=self.ps[:, 2048:3072], func=AF.Sigmoid, bias=prm[:, ch, 5:6]),
                         reads=[PB[4], PB[5], prmb], writes=[Tb[1]])
                    c.op("act", lambda e: e.activation(out=IS[:], in_=self.ps[:, 3072:4096], func=AF.Sigmoid, bias=prm[:, ch, 6:7]),
                         reads=[PB[6], PB[7], prmb], writes=[Tb[3]])
                    c.op("act", lambda e: e.activation(out=A[:], in_=A[:], func=AF.Exp, scale=prm[:, ch, 7:8]),
                         reads=[Tb[1], prmb], writes=[Tb[1]])
                    c.op("act", lambda e: e.activation(out=M[:], in_=A[:], func=AF.Square), reads=[Tb[1]], writes=[Tb[2]])
                    c.op("act", lambda e: e.activation(out=M[:], in_=M[:], func=AF.Sqrt, scale=-1.0, bias=1.0), reads=[Tb[2]], writes=[Tb[2]])
                    c.op("dve", lambda e: e.tensor_tensor(out=IS[:], in0=IS[:], in1=M[:], op=ALU.mult), reads=[Tb[3], Tb[2]], writes=[Tb[3]])
                    c.op("dve", lambda e: e.tensor_tensor(out=IS[:], in0=IS[:], in1=XC[:], op=ALU.mult), reads=[Tb[3], Tb[0]], writes=[Tb[3]])
                    init = 0.0 if hf == 0 else hl_[:, 0:1]
                    c.op("dve", lambda e: e.tensor_tensor_scan(out=M[:], data0=A[:], data1=IS[:], initial=init, op0=ALU.mult,
                                                               op1=ALU.add), reads=[Tb[1], Tb[3], hlb], writes=[Tb[2]])
                    if hf == 0:
                        c.op("dve", lambda e: e.tensor_copy(out=hl_[:, 0:1], in_=M[:, 1023:1024]), reads=[Tb[2]], writes=[hlb])
                    c.op("act", lambda e: e.activation(out=GAg[:], in_=self.ps[:, 1024:2048], func=self.gelu_func),
                         reads=[PB[2], PB[3]], writes=[GAgb])
                    c.op("dve", lambda e: e.tensor_tensor(out=YT[:, ch, t0:t0 + 1024], in0=M[:], in1=GAg[:], op=ALU.mult),
                         reads=[Tb[2], GAgb], writes=[YTb[ch]])
        wo = [self.wload_rows(wout, r * 256) for r in range(4)]
        self.out_proj(YT, YTb, wo, 8)
        c.barrier()
        es.close()
        self.retention(j)

    def retention(self, j):
        import math
        c = self.c
        nc = self.nc
        dr = self.dr
        PB = self.pb
        win = dr["even_w_in"][j]
        wout = dr["even_w_out"][j]
        es = ExitStack()
        self.es.enter_context(es)
        rb = lambda name, shape, dt: es.enter_context(nc.sbuf_tensor(name + "_r%d" % j, list(shape), dt))
        self.ring_open(es, 6, "r")
        COS = rb("COS", [128, S], F32)
        SIN = rb("SIN", [128, S], F32)
        tabb = Buf()
        DB = rb("DB", [128, 4, 128], F32)
        DM = rb("DM", [128, 4, 128], F32)
        decb = Buf()
        PI = math.pi
        with ExitStack() as tes:
            tb_ = lambda name, shape, dt: tes.enter_context(nc.sbuf_tensor(name + "_r%d" % j, list(shape), dt))
            invf = tb_("invf", [128, 4], F32)
            ANG = tb_("ANG", [128, S], F32)
            KI = tb_("KI", [128, S], mybir.dt.int32)
            KF = tb_("KF", [128, S], F32)
            tmb = Buf()
            c.op("pool", lambda e: e.iota(out=invf[:, 0:1], pattern=[[0, 1]], base=0, channel_multiplier=1,
                                          allow_small_or_imprecise_dtypes=True), writes=[tmb])
            c.op("dve", lambda e: e.tensor_scalar(out=invf[:, 1:2], in0=invf[:, 0:1], scalar1=-1.0 / 128, scalar2=None,
                                                  op0=ALU.mult), reads=[tmb], writes=[tmb])
            c.op("pool", lambda e: e.memset(invf[:, 2:3], 10000.0), reads=[tmb], writes=[tmb])
            c.op("pool", lambda e: e.tensor_tensor(out=invf[:, 3:4], in0=invf[:, 2:3], in1=invf[:, 1:2], op=ALU.pow),
                 reads=[tmb], writes=[tmb])
            c.op("pool", lambda e: e.iota(out=ANG[:], pattern=[[1, S]], base=0, channel_multiplier=0,
                                          allow_small_or_imprecise_dtypes=True), reads=[tmb], writes=[tmb])
            c.op("dve", lambda e: e.tensor_scalar(out=ANG[:], in0=ANG[:], scalar1=invf[:, 3:4], scalar2=None, op0=ALU.mult),
                 reads=[tmb], writes=[tmb])
            for dst, shift in ((SIN, 0.0), (COS, PI / 2)):
                c.op("dve", lambda e: e.tensor_scalar(out=KF[:], in0=ANG[:], scalar1=shift + PI, scalar2=1.0 / (2 * PI),
                                                      op0=ALU.add, op1=ALU.mult), reads=[tmb], writes=[tmb])
                c.op("dve", lambda e: e.tensor_copy(out=KI[:], in_=KF[:]), reads=[tmb], writes=[tmb])
                c.op("dve", lambda e: e.tensor_copy(out=KF[:], in_=KI[:]), reads=[tmb], writes=[tmb])
                c.op("dve", lambda e: e.scalar_tensor_tensor(out=KF[:], in0=KF[:], scalar=-2 * PI, in1=ANG[:], op0=ALU.mult,
                                                             op1=ALU.add), reads=[tmb], writes=[tmb])
                if shift != 0.0:
                    c.op("dve", lambda e: e.tensor_scalar(out=KF[:], in0=KF[:], scalar1=shift, scalar2=None, op0=ALU.add),
                         reads=[tmb], writes=[tmb])
                c.op("dve", lambda e: e.tensor_scalar(out=dst[:], in0=KF[:], scalar1=-PI, scalar2=2 * PI, op0=ALU.is_lt,
                                                      op1=ALU.mult), reads=[tmb], writes=[tabb])
                c.op("dve", lambda e: e.tensor_tensor(out=dst[:], in0=dst[:], in1=KF[:], op=ALU.add), reads=[tmb, tabb], writes=[tabb])
                c.op("dve", lambda e: e.tensor_scalar(out=dst[:], in0=dst[:], scalar1=-3.14159, scalar2=3.14159, op0=ALU.max,
                                                      op1=ALU.min), reads=[tabb], writes=[tabb])
                c.op("act", lambda e: e.activation(out=dst[:], in_=dst[:], func=AF.Sin), reads=[tabb], writes=[tabb])
            c.op("pool", lambda e: e.iota(out=KF[:, 0:128], pattern=[[1, 128]], base=0, channel_multiplier=-1,
                                          allow_small_or_imprecise_dtypes=True), reads=[tmb], writes=[tmb])
            for h in range(4):
                lg = math.log1p(-2.0 ** (-5 - h))
                c.op("act", lambda e: e.activation(out=DB[:, h, :], in_=KF[:, 0:128], func=AF.Exp, scale=lg), reads=[tmb], writes=[decb])
                c.op("pool", lambda e: e.affine_select(out=DM[:, h, :], in_=DB[:, h, :], pattern=[[1, 128]], compare_op=ALU.is_ge,
                                                       fill=0.0, base=0, channel_multiplier=-1), reads=[decb], writes=[decb])
            c.barrier()
        qT = rb("qTr", [128, 2, S], BF16)
        kT = rb("kTr", [128, 2, S], BF16)
        qkb = [Buf(), Buf()]
        VH = rb("VH", [128, NT, 256], BF16)
        VHb = Buf()
        YBT = rb("YBT", [128, 2, S], BF16)
        YBTb = [Buf(), Buf()]
        R = [rb("R%d" % i, [128, 512], F32) for i in range(4)]
        Rb = [Buf() for _ in range(4)]
        SC = [rb("SCr%d" % i, [128, 128], BF16) for i in range(4)]
        SCb = [Buf() for _ in range(4)]
        SG = [rb("SG%d" % i, [128, 256], F32) for i in range(2)]
        SGb = [Buf() for _ in range(2)]
        yb = [rb("yb%d" % i, [128, 256], BF16) for i in range(2)]
        ybb = [Buf() for _ in range(2)]
        junk = rb("junkr", [128, 256], BF16)
        junkb = Buf()
        ssr = rb("ssr", [128, 4], F32)
        ssrb = Buf()
        sreg = [Buf() for _ in range(8)]
        cnt = 0
        pj = 0
        for h in range(4):
            lg = math.log1p(-2.0 ** (-5 - h))
            wq = self.wload_cols(win, [(2048 + h * 256, 256)])
            wk = self.wload_cols(win, [(3072 + h * 256, 256)])
            wv = self.wload_cols(win, [(4096 + h * 256, 256)])
            wg = self.wload_cols(win, [(5120 + h * 256, 256)])
            wo = self.wload_rows(wout, 1024 + h * 256)
            for jq, ((wt, wb), dst) in enumerate(((wq, qT), (wk, kT))):
                for tb in range(4):
                    ba, bb = (4, 5) if pj % 2 == 0 else (6, 7)
                    pj += 1
                    ts = slice(tb * 512, (tb + 1) * 512)
                    for half, bk in ((0, ba), (1, bb)):
                        for dc in range(8):
                            c.op("pe", lambda e: e.matmul(self.bank(bk), lhsT=wt[:, dc, half * 128:(half + 1) * 128],
                                                          rhs=self.hnT[:, dc, ts], start=(dc == 0), stop=(dc == 7)),
                                 reads=[wb] + self.hnTb[tb * 4:tb * 4 + 4], writes=[PB[bk]])
                    c.op("dve", lambda e: e.tensor_tensor(out=R[0][:], in0=self.bank(ba), in1=COS[:, ts], op=ALU.mult),
                         reads=[PB[ba], tabb], writes=[Rb[0]])
                    c.op("dve", lambda e: e.tensor_tensor(out=R[1][:], in0=self.bank(bb), in1=SIN[:, ts], op=ALU.mult),
                         reads=[PB[bb], tabb], writes=[Rb[1]])
                    c.op("pool", lambda e: e.tensor_tensor(out=dst[:, 0, ts], in0=R[0][:], in1=R[1][:], op=ALU.subtract),
                         reads=[Rb[0], Rb[1]], writes=[qkb[jq]])
                    c.op("dve", lambda e: e.tensor_tensor(out=R[2][:], in0=self.bank(ba), in1=SIN[:, ts], op=ALU.mult),
                         reads=[PB[ba], tabb], writes=[Rb[2]])
                    c.op("dve", lambda e: e.tensor_tensor(out=R[3][:], in0=self.bank(bb), in1=COS[:, ts], op=ALU.mult),
                         reads=[PB[bb], tabb], writes=[Rb[3]])
                    c.op("pool", lambda e: e.tensor_tensor(out=dst[:, 1, ts], in0=R[2][:], in1=R[3][:], op=ALU.add),
                         reads=[Rb[2], Rb[3]], writes=[qkb[jq]])
            for tt in range(NT):
                bk = 4 + (tt // 2) % 4
                reg = self.ps[:, bk * 512 + (tt % 2) * 256: bk * 512 + (tt % 2) * 256 + 256]
                for dc in range(8):
                    c.op("pe", lambda e: e.matmul(reg, lhsT=self.hnT[:, dc, tt * 128:(tt + 1) * 128], rhs=wv[0][:, dc, :],
                                                  start=(dc == 0), stop=(dc == 7)), reads=[wv[1], self.hnTb[tt]], writes=[PB[bk]])
                c.op("act", lambda e: e.activation(out=VH[:, tt, :], in_=reg, func=AF.Copy), reads=[PB[bk]], writes=[VHb])
            for ci in range(NT):
                k2 = ci % 2
                ybk = 2 + k2
                Y = self.ps[:, ybk * 512: ybk * 512 + 256]
                GP = self.ps[:, ybk * 512 + 256: ybk * 512 + 512]
                for cj in range(ci + 1):
                    r = cnt % 8
                    cnt += 1
                    st = self.ps[:, r * 128:(r + 1) * 128]
                    for dch in range(2):
                        c.op("pe", lambda e: e.matmul(st, lhsT=kT[:, dch, cj * 128:(cj + 1) * 128],
                                                      rhs=qT[:, dch, ci * 128:(ci + 1) * 128], start=(dch == 0), stop=(dch == 1)),
                             reads=[qkb[0], qkb[1]], writes=[sreg[r]])
                    const = math.exp(lg * 128.0 * (ci - cj)) / 16.0
                    dmat = DM[:, h, :] if cj == ci else DB[:, h, :]
                    sc = SC[r % 4]
                    c.op("dve", lambda e: e.scalar_tensor_tensor(out=sc[:], in0=st, scalar=const, in1=dmat, op0=ALU.mult,
                                                                 op1=ALU.mult), reads=[sreg[r], decb], writes=[SCb[r % 4]])
                    c.op("pe", lambda e: e.matmul(Y, lhsT=sc[:], rhs=VH[:, cj, :], start=(cj == 0), stop=(cj == ci)),
                         reads=[SCb[r % 4], VHb], writes=[PB[ybk]])
                for dc in range(8):
                    c.op("pe", lambda e: e.matmul(GP, lhsT=self.hnT[:, dc, ci * 128:(ci + 1) * 128], rhs=wg[0][:, dc, :],
                                                  start=(dc == 0), stop=(dc == 7), skip_group_check=True),
                         reads=[wg[1], self.hnTb[ci]], writes=[PB[ybk]])
                c.op("act", lambda e: e.activation(out=SG[k2][:], in_=GP, func=AF.Silu), reads=[PB[ybk]], writes=[SGb[k2]])
                c.op("act", lambda e: e.activation(out=junk[:], in_=Y, func=AF.Square, accum_out=ssr[:, 0:1]),
                     reads=[PB[ybk]], writes=[junkb, ssrb])
                c.op("act", lambda e: e.activation(out=ssr[:, 1:2], in_=ssr[:, 0:1], func=AF.Sqrt, bias=EPS, scale=1.0 / 256),
                     reads=[ssrb], writes=[ssrb])
                c.op("dve", lambda e: e.reciprocal(out=ssr[:, 2:3], in_=ssr[:, 1:2]), reads=[ssrb], writes=[ssrb])
                c.op("dve", lambda e: e.scalar_tensor_tensor(out=yb[k2][:], in0=Y, scalar=ssr[:, 2:3], in1=SG[k2][:],
                                                             op0=ALU.mult, op1=ALU.mult),
                     reads=[PB[ybk], ssrb, SGb[k2]], writes=[ybb[k2]])
                tbk = 6 + k2
                for eh in range(2):
                    c.op("pe", lambda e: e.transpose(out=self.bank_bf(tbk)[:, eh * 128:(eh + 1) * 128],
                                                     in_=yb[k2][:, eh * 128:(eh + 1) * 128], identity=self.ident[:]),
                         reads=[ybb[k2], self.identb], writes=[PB[tbk]])
                c.op("act", lambda e: e.activation(out=YBT[:, :, ci * 128:(ci + 1) * 128],
                                                   in_=self.bank_bf(tbk)[:, 0:256].rearrange("p (a b) -> p a b", a=2), func=AF.Copy),
                     reads=[PB[tbk]], writes=[YBTb[0], YBTb[1]])
            self.out_proj(YBT, YBTb, [wo], 2)
        c.barrier()
        es.close()

    def out_proj(self, YT, YTb, wo, nchunks, rstd=None, rstdb=None):
        c = self.c
        for tt in range(NT):
            for dh in range(2):
                bk = (tt * 2 + dh) % 4
                for ch in range(nchunks):
                    wv, wb = wo[ch // 2]
                    c.op("pe", lambda e: e.matmul(self.bank(bk), lhsT=YT[:, ch, tt * 128:(tt + 1) * 128],
                                                  rhs=wv[:, ch % 2, dh * 512:(dh + 1) * 512], start=(ch == 0),
                                                  stop=(ch == nchunks - 1)), reads=[YTb[ch], wb], writes=[self.pb[bk]])
                xs = self.xres[:, tt, dh * 512:(dh + 1) * 512]
                if rstd is None:
                    c.op("dve", lambda e: e.tensor_tensor(out=xs, in0=self.bank(bk), in1=xs, op=ALU.add),
                         reads=[self.pb[bk], self.xb[tt]], writes=[self.xb[tt]])
                else:
                    c.op("dve", lambda e: e.scalar_tensor_tensor(out=xs, in0=self.bank(bk), scalar=rstd[:, tt:tt + 1], in1=xs,
                                                                 op0=ALU.mult, op1=ALU.add),
                         reads=[self.pb[bk], self.xb[tt], rstdb], writes=[self.xb[tt]])

    def odd_mixer(self, j=0):
        c = self.c
        nc = self.nc
        dr = self.dr
        PB = self.pb
        win = dr["ssm_w_in"][j]
        wout = dr["ssm_w_out"][j]
        ygs = dr["ygs"]
        es = ExitStack()
        self.es.enter_context(es)
        ob = lambda name, shape, dt: es.enter_context(nc.sbuf_tensor(name + "_o%d" % j, list(shape), dt))
        self.norm_to_hnT(dr["mix_norm"][2 * j + 1])
        self.ring_open(es, 4, "o")
        kw = dict(allow_slow_non_contiguous=True)
        TRI = ob("TRI", [128, 128], F32)
        ONES = ob("ONES", [128, 128], F32)
        MNEG = ob("MNEG", [128, 128], F32)
        cstb = Buf()
        c.op("pool", lambda e: e.memset(ONES[:], 1.0), writes=[cstb])
        c.op("pool", lambda e: e.affine_select(out=TRI[:], in_=ONES[:], pattern=[[1, 128]], compare_op=ALU.is_ge, fill=0.0,
                                               base=0, channel_multiplier=-1), reads=[cstb], writes=[cstb])
        c.op("pool", lambda e: e.memset(MNEG[:], 0.0), reads=[cstb], writes=[cstb])
        c.op("pool", lambda e: e.affine_select(out=MNEG[:], in_=MNEG[:], pattern=[[1, 128]], compare_op=ALU.is_ge, fill=NEG,
                                               base=0, channel_multiplier=-1), reads=[cstb], writes=[cstb])
        prm = ob("prm", [128, 24, 8], F32)
        prmb = Buf()
        for k in range(4):
            c.dma("sp", "ld0", prm[:, :, k:k + 1], dr["ssm_conv_w"][j][k].rearrange("(c p o) -> p c o", p=128, o=1), writes=[prmb], **kw)
        c.dma("sp", "ld0", prm[:, :, 4:5], dr["ssm_conv_b"][j].rearrange("(c p o) -> p c o", p=128, o=1), writes=[prmb], **kw)
        GN = ob("GN", [128, 16], F32)
        c.dma("sp", "ld0", GN[:].unsqueeze(2), dr["ssm_norm"][j].rearrange("(c p o) -> p c o", p=128, o=1), writes=[prmb], **kw)
        hp = ob("hp", [128, 4, 32], F32)
        hpb = Buf()
        for idx, nm in ((0, "ssm_dt_bias"), (1, "ssm_a_log"), (2, "ssm_d")):
            c.dma("sp", "ld1", hp[:, idx, :], dr[nm][j].partition_broadcast(128), writes=[hpb])
        c.op("act", lambda e: e.activation(out=hp[:, 1, :], in_=hp[:, 1, :], func=AF.Exp), reads=[hpb], writes=[hpb])
        c.op("dve", lambda e: e.tensor_scalar(out=hp[:, 1, :], in0=hp[:, 1, :], scalar1=-1.0, scalar2=None, op0=ALU.mult),
             reads=[hpb], writes=[hpb])
        fam = {n: ob(n, [128, NT, 32], F32) for n in ("DT", "ADT", "ACUM", "NACUM", "EA", "DEC", "CD")}
        famb = Buf()
        SS = ob("SS", [128, NT, 4], F32)
        SSb = Buf()
        wdt, wdtb = self.wload_cols(win, [(5120, 32)])
        for tt in range(NT):
            for dc in range(8):
                c.op("pe", lambda e: e.matmul(self.ps[:, tt * 32:(tt + 1) * 32], lhsT=self.hnT[:, dc, tt * 128:(tt + 1) * 128],
                                              rhs=wdt[:, dc, 0:32], start=(dc == 0), stop=(dc == 7)),
                     reads=[wdtb, self.hnTb[tt]], writes=[PB[0]])
        f3 = lambda t: t[:]
        flat = lambda t: t[:].rearrange("p a b -> p (a b)")
        bc_h = lambda ap2: ap2.unsqueeze(1).to_broadcast([128, NT, 32])
        DT, ADT, ACUM, NACUM, EA, DEC, CD = (fam[n] for n in ("DT", "ADT", "ACUM", "NACUM", "EA", "DEC", "CD"))
        c.op("dve", lambda e: e.tensor_tensor(out=ADT[:], in0=self.bank(0).rearrange("p (a b) -> p a b", a=NT), in1=bc_h(hp[:, 0, :]),
                                              op=ALU.add), reads=[PB[0], hpb], writes=[famb])
        c.op("act", lambda e: e.activation(out=flat(EA), in_=flat(ADT), func=AF.Abs), reads=[famb], writes=[famb])
        c.op("act", lambda e: e.activation(out=flat(EA), in_=flat(EA), func=AF.Exp, scale=-1.0), reads=[famb], writes=[famb])
        c.op("act", lambda e: e.activation(out=flat(EA), in_=flat(EA), func=AF.Ln, bias=1.0), reads=[famb], writes=[famb])
        c.op("dve", lambda e: e.scalar_tensor_tensor(out=flat(DT), in0=flat(ADT), scalar=0.0, in1=flat(EA), op0=ALU.max, op1=ALU.add),
             reads=[famb], writes=[famb])
        c.op("dve", lambda e: e.tensor_tensor(out=ADT[:], in0=DT[:], in1=bc_h(hp[:, 1, :]), op=ALU.mult), reads=[famb, hpb], writes=[famb])
        c.op("pe", lambda e: e.matmul(self.bank(1), lhsT=TRI[:], rhs=flat(ADT), start=True, stop=True), reads=[cstb, famb], writes=[PB[1]])
        c.op("pe", lambda e: e.matmul(self.bank(2), lhsT=ONES[:], rhs=flat(ADT), start=True, stop=True), reads=[cstb, famb], writes=[PB[2]])
        c.op("act", lambda e: e.activation(out=flat(ACUM), in_=self.bank(1), func=AF.Copy), reads=[PB[1]], writes=[famb])
        c.op("act", lambda e: e.activation(out=flat(EA), in_=self.bank(1), func=AF.Exp), reads=[PB[1]], writes=[famb])
        c.op("act", lambda e: e.activation(out=flat(CD), in_=self.bank(2), func=AF.Exp), reads=[PB[2]], writes=[famb])
        c.op("dve", lambda e: e.tensor_scalar(out=flat(NACUM), in0=flat(ACUM), scalar1=-1.0, scalar2=None, op0=ALU.mult),
             reads=[famb], writes=[famb])
        c.op("dve", lambda e: e.tensor_tensor(out=flat(DEC), in0=self.bank(2), in1=flat(ACUM), op=ALU.subtract),
             reads=[PB[2], famb], writes=[famb])
        c.op("act", lambda e: e.activation(out=flat(DEC), in_=flat(DEC), func=AF.Exp), reads=[famb], writes=[famb])
        c.barrier()
        with ExitStack() as ges:
            gb_ = lambda name, shape, dt: ges.enter_context(nc.sbuf_tensor(name + "_o%d" % j, list(shape), dt))
            BT = gb_("BT", [128, S], BF16)
            CT = gb_("CT", [128, S], BF16)
            BCb = [Buf(), Buf()]
            XS = gb_("XS", [128, NT, 512], BF16)
            XSb = Buf()
            STG = gb_("STG", [128, 3 + S], F32)
            STGb = Buf()
            TC = gb_("TC", [128, 1024], F32)
            TCb = Buf()
            XSF = gb_("XSF", [128, 1024], BF16)
            XSFb = Buf()
            STATE = gb_("STATE", [128, 8, 64], F32)
            STb = Buf()
            STATEb = gb_("STATEb", [128, 512], BF16)
            STbb = Buf()
            LA = gb_("LA", [128, 8, 128], BF16)
            LAb = Buf()
            MT = gb_("MT", [128, 8, 128], BF16)
            MTb = Buf()
            CBm = gb_("CBm", [128, 128], BF16)
            CBmb = Buf()
            BTOK = gb_("BTOK", [128, 128], BF16)
            BTOKb = Buf()
            RH = [gb_("RH%d" % i, [128, 128], F32) for i in range(2)]
            RHb = [Buf(), Buf()]
            XDT = gb_("XDT", [128, 8, 64], BF16)
            XDTb = Buf()
            XDD = gb_("XDD", [128, 8, 64], BF16)
            XDDb = Buf()
            SZ = gb_("SZ", [128, 512], F32)
            SZb = Buf()
            W1 = gb_("W1", [128, 8, 64], F32)
            W1b = Buf()
            W2 = gb_("W2", [128, 8, 64], F32)
            W2b = Buf()
            YG = [gb_("YG%d" % i, [128, 512], BF16) for i in range(2)]
            YGb = [Buf(), Buf()]
            junk = gb_("junko", [128, 512], BF16)
            junkb = Buf()
            c.op("pool", lambda e: e.memset(STG[:, 0:3], 0.0), writes=[STGb])
            ygi = 0
            for g in range(4):
                wbc = self.wload_cols(win, [(4096 + g * 128, 128), (4608 + g * 128, 128)])
                wxs = [self.wload_cols(win, [(2048 + g * 512 + i * 256, 256)]) for i in range(2)]
                hs = slice(8 * g, 8 * g + 8)
                for fc in range(6):
                    if fc < 2:
                        wt, wb, co, q = wbc[0], wbc[1], fc * 128, 16 + 4 * fc + g
                    else:
                        jx = fc - 2
                        wt, wb, co, q = wxs[jx // 2][0], wxs[jx // 2][1], (jx % 2) * 128, g * 4 + jx
                    for hf in range(2):
                        t0 = hf * 1024
                        for qq in range(2):
                            for dc in range(8):
                                c.op("pe", lambda e: e.matmul(self.bank(qq), lhsT=wt[:, dc, co:co + 128],
                                                              rhs=self.hnT[:, dc, t0 + qq * 512:t0 + (qq + 1) * 512],
                                                              start=(dc == 0), stop=(dc == 7)),
                                     reads=[wb] + self.hnTb[hf * 8 + qq * 4:hf * 8 + qq * 4 + 4], writes=[PB[qq]])
                        c.op("act", lambda e: e.activation(out=STG[:, 3 + t0:3 + t0 + 1024], in_=self.ps[:, 0:1024], func=AF.Copy),
                             reads=[PB[0], PB[1]], writes=[STGb])
                        c.op("dve", lambda e: e.tensor_scalar(out=TC[:], in0=STG[:, 3 + t0:3 + t0 + 1024], scalar1=prm[:, q, 3:4],
                                                              scalar2=prm[:, q, 4:5], op0=ALU.mult, op1=ALU.add),
                             reads=[STGb, prmb], writes=[TCb])
                        for k in range(3):
                            c.op("dve", lambda e: e.scalar_tensor_tensor(out=TC[:], in0=STG[:, k + t0:k + t0 + 1024],
                                                                         scalar=prm[:, q, k:k + 1], in1=TC[:], op0=ALU.mult,
                                                                         op1=ALU.add), reads=[STGb, prmb, TCb], writes=[TCb])
                        if fc < 2:
                            dst = (BT, CT)[fc]
                            c.op("act", lambda e: e.activation(out=dst[:, t0:t0 + 1024], in_=TC[:], func=AF.Silu),
                                 reads=[TCb], writes=[BCb[fc]])
                        else:
                            c.op("act", lambda e: e.activation(out=XSF[:], in_=TC[:], func=AF.Silu), reads=[TCb], writes=[XSFb])
                            tbk = 2 + (fc * 2 + hf) % 2
                            for k in range(8):
                                c.op("pe", lambda e: e.transpose(out=self.bank_bf(tbk)[:, k * 128:(k + 1) * 128],
                                                                 in_=XSF[:, k * 128:(k + 1) * 128], identity=self.ident[:]),
                                     reads=[XSFb, self.identb], writes=[PB[tbk]])
                            c.op("act", lambda e: e.activation(out=XS[:, hf * 8:hf * 8 + 8, (fc - 2) * 128:(fc - 1) * 128],
                                                               in_=self.bank_bf(tbk).rearrange("p (a b) -> p a b", a=8), func=AF.Copy),
                                 reads=[PB[tbk]], writes=[XSb])
                wz = [self.wload_cols(win, [(g * 512 + i * 256, 256)]) for i in range(2)]
                c.op("pool", lambda e: e.memset(STATE[:], 0.0), reads=[STb], writes=[STb])
                c.op("pool", lambda e: e.memset(STATEb[:], 0.0), reads=[STbb], writes=[STbb])
                for tt in range(NT):
                    tsl = slice(tt * 128, (tt + 1) * 128)
                    c.op("pe", lambda e: e.matmul(self.ps[:, 4 * 512:4 * 512 + 128], lhsT=BT[:, tsl], rhs=CT[:, tsl], start=True, stop=True),
                         reads=BCb, writes=[PB[4]])
                    c.op("act", lambda e: e.activation(out=CBm[:], in_=self.ps[:, 4 * 512:4 * 512 + 128], func=AF.Copy),
                         reads=[PB[4]], writes=[CBmb])
                    c.op("pe", lambda e: e.transpose(out=self.bank_bf(5)[:, 0:128], in_=BT[:, tsl], identity=self.ident[:]),
                         reads=[BCb[0], self.identb], writes=[PB[5]])
                    c.op("act", lambda e: e.activation(out=BTOK[:], in_=self.bank_bf(5)[:, 0:128], func=AF.Copy),
                         reads=[PB[5]], writes=[BTOKb])
                    for h in range(8):
                        hh = 8 * g + h
                        rh = RH[h % 2]
                        bk = 6 + (h // 4) % 2
                        reg = self.ps[:, bk * 512 + (h % 4) * 128: bk * 512 + (h % 4 + 1) * 128]
                        c.op("dve", lambda e: e.tensor_scalar(out=rh[:], in0=TRI[:], scalar1=ADT[:, tt, hh:hh + 1], scalar2=None,
                                                              op0=ALU.mult), reads=[cstb, famb], writes=[RHb[h % 2]])
                        c.op("pe", lambda e: e.matmul(reg, lhsT=ONES[:], rhs=rh[:], start=True, stop=False),
                             reads=[cstb, RHb[h % 2]], writes=[PB[bk]])
                        c.op("pe", lambda e: e.matmul(reg, lhsT=self.identf[:], rhs=MNEG[:], start=False, stop=True),
                             reads=[cstb, self.identfb], writes=[PB[bk]])
                        c.op("act", lambda e: e.activation(out=LA[:, h, :], in_=reg, func=AF.Exp, bias=NACUM[:, tt, hh:hh + 1]),
                             reads=[PB[bk], famb], writes=[LAb])
                    c.op("dve", lambda e: e.tensor_tensor(out=MT[:], in0=LA[:], in1=CBm[:].unsqueeze(1).to_broadcast([128, 8, 128]),
                                                          op=ALU.mult), reads=[LAb, CBmb], writes=[MTb])
                    xs3 = XS[:, tt, :].rearrange("p (a b) -> p a b", a=8)
                    bc_p = lambda ap2: ap2.unsqueeze(2).to_broadcast([128, 8, 64])
                    c.op("dve", lambda e: e.tensor_tensor(out=XDT[:], in0=xs3, in1=bc_p(DT[:, tt, hs]), op=ALU.mult),
                         reads=[XSb, famb], writes=[XDTb])
                    c.op("dve", lambda e: e.tensor_tensor(out=XDD[:], in0=XDT[:], in1=bc_p(DEC[:, tt, hs]), op=ALU.mult),
                         reads=[XDTb, famb], writes=[XDDb])
                    for h in range(8):
                        c.op("pe", lambda e: e.matmul(self.ps[:, 2 * 512 + h * 64: 2 * 512 + (h + 1) * 64], lhsT=MT[:, h, :],
                                                      rhs=XDT[:, h, :], start=True, stop=True), reads=[MTb, XDTb], writes=[PB[2]])
                    c.op("pe", lambda e: e.matmul(self.bank(3), lhsT=CT[:, tsl], rhs=STATEb[:], start=True, stop=True),
                         reads=[BCb[1], STbb], writes=[PB[3]])
                    c.op("pe", lambda e: e.matmul(self.bank(0), lhsT=BTOK[:], rhs=XDD[:].rearrange("p a b -> p (a b)"), start=True, stop=True),
                         reads=[BTOKb, XDDb], writes=[PB[0]])
                    for i in range(2):
                        for dc in range(8):
                            c.op("pe", lambda e: e.matmul(self.ps[:, 512 + i * 256:512 + (i + 1) * 256], lhsT=self.hnT[:, dc, tsl],
                                                          rhs=wz[i][0][:, dc, :], start=(dc == 0), stop=(dc == 7)),
                                 reads=[wz[i][1], self.hnTb[tt]], writes=[PB[1]])
                    c.op("act", lambda e: e.activation(out=SZ[:], in_=self.bank(1), func=AF.Silu), reads=[PB[1]], writes=[SZb])
                    c.op("dve", lambda e: e.tensor_tensor(out=W1[:], in0=self.bank(3).rearrange("p (a b) -> p a b", a=8),
                                                          in1=bc_p(EA[:, tt, hs]), op=ALU.mult), reads=[PB[3], famb], writes=[W1b])
                    c.op("dve", lambda e: e.tensor_tensor(out=W1[:], in0=self.bank(2).rearrange("p (a b) -> p a b", a=8), in1=W1[:],
                                                          op=ALU.add), reads=[PB[2], W1b], writes=[W1b])
                    c.op("dve", lambda e: e.tensor_tensor(out=W2[:], in0=xs3, in1=bc_p(hp[:, 2, hs]), op=ALU.mult),
                         reads=[XSb, hpb], writes=[W2b])
                    c.op("pool", lambda e: e.tensor_tensor(out=W1[:], in0=W1[:], in1=W2[:], op=ALU.add), reads=[W1b, W2b], writes=[W1b])
                    yg = YG[ygi % 2]
                    c.op("dve", lambda e: e.tensor_tensor(out=yg[:], in0=W1[:].rearrange("p a b -> p (a b)"), in1=SZ[:], op=ALU.mult),
                         reads=[W1b, SZb], writes=[YGb[ygi % 2]])
                    c.op("act", lambda e: e.activation(out=junk[:], in_=yg[:], func=AF.Square, accum_out=SS[:, tt, g:g + 1]),
                         reads=[YGb[ygi % 2]], writes=[junkb, SSb])
                    c.dma("sp", "ld2" if ygi % 2 == 0 else "ld3", ygs[tt * 128:(tt + 1) * 128, g * 512:(g + 1) * 512], yg[:],
                          reads=[YGb[ygi % 2]])
                    ygi += 1
                    c.op("dve", lambda e: e.tensor_tensor(out=STATE[:], in0=STATE[:], in1=bc_p(CD[:, tt, hs]), op=ALU.mult),
                         reads=[STb, famb], writes=[STb])
                    c.op("dve", lambda e: e.tensor_tensor(out=STATE[:], in0=self.bank(0).rearrange("p (a b) -> p a b", a=8), in1=STATE[:],
                                                          op=ALU.add), reads=[PB[0], STb], writes=[STb])
                    c.op("act", lambda e: e.activation(out=STATEb[:], in_=STATE[:].rearrange("p a b -> p (a b)"), func=AF.Copy),
                         reads=[STb], writes=[STbb])
            c.barrier()
        RS = ob("RS", [128, 3, NT], F32)
        RSb = Buf()
        c.op("dve", lambda e: e.tensor_reduce(out=RS[:, 0, :], in_=SS[:], axis=AX.X, op=ALU.add), reads=[SSb], writes=[RSb])
        c.op("act", lambda e: e.activation(out=RS[:, 1, :], in_=RS[:, 0, :], func=AF.Sqrt, bias=EPS, scale=1.0 / 2048), reads=[RSb], writes=[RSb])
        c.op("dve", lambda e: e.reciprocal(out=RS[:, 2, :], in_=RS[:, 1, :]), reads=[RSb], writes=[RSb])
        WO = ob("WO", [128, 16, D], BF16)
        WOb = Buf()
        for r in range(8):
            c.dma("pool", "ld4", WO[:, 2 * r:2 * r + 2, :], wout[r * 256:(r + 1) * 256, :].rearrange("(a p) d -> p a d", p=128), writes=[WOb])
        YGF = [ob("YGF%d" % i, [128, 2048], BF16) for i in range(2)]
        YGFb = [Buf(), Buf()]
        YGT = [ob("YGT%d" % i, [128, 16, 128], BF16) for i in range(2)]
        YGTb = [Buf(), Buf()]
        for tt in range(NT):
            k = tt % 2
            c.dma("sp", "ld2" if k == 0 else "ld3", YGF[k][:], ygs[tt * 128:(tt + 1) * 128, :], writes=[YGFb[k]])
            b0 = 4 + 2 * k
            for ch in range(16):
                bk = b0 + ch // 8
                c.op("pe", lambda e: e.transpose(out=self.bank_bf(bk)[:, (ch % 8) * 128:(ch % 8 + 1) * 128],
                                                 in_=YGF[k][:, ch * 128:(ch + 1) * 128], identity=self.ident[:]),
                     reads=[YGFb[k], self.identb], writes=[PB[bk]])
            src = self.ps[:, b0 * 512:(b0 + 2) * 512].bitcast(BF16).rearrange("p (a b) -> p a b", a=16)
            c.op("dve", lambda e: e.tensor_tensor(out=YGT[k][:], in0=src, in1=GN[:].unsqueeze(2).to_broadcast([128, 16, 128]), op=ALU.mult),
                 reads=[PB[b0], PB[b0 + 1], prmb], writes=[YGTb[k]])
            for dh in range(2):
                bk = 2 * k + dh
                for ch in range(16):
                    c.op("pe", lambda e: e.matmul(self.bank(bk), lhsT=YGT[k][:, ch, :], rhs=WO[:, ch, dh * 512:(dh + 1) * 512],
                                                  start=(ch == 0), stop=(ch == 15)), reads=[YGTb[k], WOb], writes=[PB[bk]])
                xs_ = self.xres[:, tt, dh * 512:(dh + 1) * 512]
                c.op("dve", lambda e: e.scalar_tensor_tensor(out=xs_, in0=self.bank(bk), scalar=RS[:, 2, tt:tt + 1], in1=xs_,
                                                             op0=ALU.mult, op1=ALU.add),
                     reads=[PB[bk], RSb, self.xb[tt]], writes=[self.xb[tt]])
        c.barrier()
        es.close()

    def final_norm(self):
        c = self.c
        self.load_gain(self.dr["final_norm"])
        self.norm_stats()
        for tt in range(NT):
            c.op("dve", lambda e: e.scalar_tensor_tensor(out=self.xres[:, tt, :], in0=self.xres[:, tt, :],
                                                         scalar=self.stat[:, 16 + tt:17 + tt], in1=self.gbc[:], op0=ALU.mult,
                                                         op1=ALU.mult), reads=[self.xb[tt], self.statb, self.gbcb], writes=[self.xb[tt]])


LAST_INPUT_NAMES = []


def build(stages=("peer0",), out_name="y"):
    nc = bass.Bass("TRN2", target_bir_lowering=False)
    dram = {}

    def din(name, shape):
        dram[name] = nc.dram_tensor(name, list(shape), F32, kind="ExternalInput").ap()
        if name not in LAST_INPUT_NAMES:
            LAST_INPUT_NAMES.append(name)

    din("x", [S, D])
    din("ffn_norm", [2, D])
    din("mix_norm", [2, D])
    din("final_norm", [D])
    din("even_w_in", [1, D, 6144])
    din("lru_conv_w", [1, 4, D])
    din("lru_conv_b", [1, D])
    din("lru_w_a", [1, 16, 64, 64])
    din("lru_b_a", [1, D])
    din("lru_w_i", [1, 16, 64, 64])
    din("lru_b_i", [1, D])
    din("lru_lambda", [1, D])
    din("even_w_out", [1, 2048, D])
    din("ssm_w_in", [1, D, 5152])
    din("ssm_conv_w", [1, 4, 3072])
    din("ssm_conv_b", [1, 3072])
    din("ssm_dt_bias", [1, 32])
    din("ssm_a_log", [1, 32])
    din("ssm_d", [1, 32])
    din("ssm_norm", [1, 2048])
    din("ssm_w_out", [1, 2048, D])
    dram["ygs"] = nc.dram_tensor("ygs", [S, 2048], BF16, kind="Internal").ap()
    din("peer_w_q", [2, D, D])
    din("peer_sub_keys", [2, 8, 2, 128, 64])
    din("peer_u", [2, 16384, D])
    din("peer_v", [2, 16384, D])
    dram["y"] = nc.dram_tensor("y", [S, D], F32, kind="ExternalOutput").ap()
    with ExitStack() as es:
        k = Kern(nc, es, dram)
        k.load_x()
        for st in stages:
            if st == "peer0":
                k.peer(0)
            elif st == "peer1":
                k.peer(1)
            elif st == "odd":
                k.odd_mixer(0)
            elif st == "even":
                k.even_mixer(0)
            elif st == "final":
                k.final_norm()
        k.store_x("y")
    return nc


ALL_STAGES = ("even", "peer0", "odd", "peer1", "final")


def kernel(**inputs):
    nc = build(ALL_STAGES)
    names = [n for n in LAST_INPUT_NAMES if n != "x"]
    shared = {n: np.ascontiguousarray(np.asarray(inputs[n], dtype=np.float32)) for n in names}
    x = np.asarray(inputs["x"], dtype=np.float32)
    nb = x.shape[0]
    maps = [dict(shared, x=np.ascontiguousarray(x[b])) for b in range(nb)]
    res = run_bass_kernel_spmd(nc, maps, core_ids=list(range(nb)))
    return np.stack([np.asarray(res.results[b]["y"], dtype=np.float32) for b in range(nb)], axis=0)
```

```python
import numpy as np
from contextlib import ExitStack
import concourse.bass as bass
import concourse.mybir as mybir
from concourse.bass_types import AP
from concourse.bass_utils import run_bass_kernel_spmd

F32 = mybir.dt.float32
BF16 = mybir.dt.bfloat16
AF = mybir.ActivationFunctionType
ALU = mybir.AluOpType
AX = mybir.AxisListType

S = 2048
D = 1024
NT = 16
EPS = 1e-6
NEG = -1.0e30
PEER_NEB = 32
NORM_SCOPED = True
PEER_MAXSTEPS = 10 ** 9
NORM_BARRIER = False


class Buf:
    __slots__ = ("w", "r", "name")

    def __init__(self, name=""):
        self.w = None
        self.r = {}
        self.name = name


class Ctx:
    def __init__(self, nc, es, needed=None):
        self.nc = nc
        self.es = es
        self.eng = {"pe": nc.tensor, "act": nc.scalar, "dve": nc.vector, "pool": nc.gpsimd, "sp": nc.sync}
        self.semobj = {}
        self.val = {}
        self.real = {}
        self.v2r = {}
        self.needed = needed
        self.waited = set()
        self.seen = {k: {} for k in self.eng}
        for k in self.eng:
            self.newsem(k)

    def newsem(self, name):
        self.semobj[name] = self.es.enter_context(self.nc.semaphore("s_" + name))
        self.val[name] = 0
        self.real[name] = 0
        return name

    def sbuf(self, name, shape, dt):
        return self.es.enter_context(self.nc.sbuf_tensor(name, list(shape), dt))

    def _wait(self, e, s, v):
        self.waited.add((s, v))
        if self.needed is None or s not in self.eng:
            rv = v
        else:
            rv = self.v2r[(s, v)]
        self.eng[e].wait_ge(self.semobj[s], rv)
        self.seen[e][s] = v

    def _deps(self, e, reads, writes):
        need = {}

        def add(s, v):
            if s == "pe" and e == "pe":
                return
            if need.get(s, 0) < v:
                need[s] = v

        for b in reads:
            if b.w is not None:
                add(*b.w)
        for b in writes:
            if b.w is not None:
                add(*b.w)
            for s, v in b.r.items():
                add(s, v)
        for s, v in need.items():
            if self.seen[e].get(s, 0) < v:
                self._wait(e, s, v)

    def op(self, e, fn, reads=(), writes=()):
        self._deps(e, reads, writes)
        self.val[e] += 1
        v = self.val[e]
        ins = fn(self.eng[e])
        if self.needed is None:
            ins.then_inc(self.semobj[e], 1)
        elif (e, v) in self.needed:
            self.real[e] += 1
            self.v2r[(e, v)] = self.real[e]
            ins.then_inc(self.semobj[e], 1)
        for b in reads:
            if b.r.get(e, 0) < v:
                b.r[e] = v
        for b in writes:
            b.w = (e, v)
            b.r = {}

    def dma(self, q, sem, out, in_, reads=(), writes=(), **kw):
        sem = q + "_" + sem
        if sem not in self.semobj:
            self.newsem(sem)
        self._deps(q, reads, writes)
        self.val[sem] += 16
        v = self.val[sem]
        self.eng[q].dma_start(out=out, in_=in_, **kw).then_inc(self.semobj[sem], 16)
        for b in reads:
            b.r[sem] = v
        for b in writes:
            b.w = (sem, v)
            b.r = {}

    def barrier(self, engines=None):
        for e in (engines or self.eng):
            for s, v in self.val.items():
                if s == e or v == 0:
                    continue
                if self.seen[e].get(s, 0) < v:
                    self._wait(e, s, v)


def bcast_last(ap2d, n):
    return ap2d.unsqueeze(2).broadcast_to([ap2d.shape[0], ap2d.shape[1], n])


class Kern:
    def __init__(self, nc, es, dram, needed=None):
        self.nc = nc
        self.c = Ctx(nc, es, needed)
        self.es = es
        self.dr = dram
        c = self.c
        self.xres = c.sbuf("xres", [128, NT, D], F32)
        self.xb = [Buf("x%d" % i) for i in range(NT)]
        self.hnT = c.sbuf("hnT", [128, 8, S], BF16)
        self.hnTb = [Buf("hnT%d" % i) for i in range(NT)]
        self.ident = c.sbuf("ident", [128, 128], BF16)
        self.identb = Buf("ident")
        self.ps = es.enter_context(nc.psum_tensor("ps", [128, 4096], F32))
        self.pb = [Buf("bank%d" % i) for i in range(8)]
        self.stat = c.sbuf("stat", [128, 64], F32)
        self.statb = Buf("stat")
        self.nscope = 0
        for n in ("sp_io", "sp_ld0", "sp_ld1", "sp_ld2", "sp_ld3", "sp_ld5", "pool_ld0", "pool_ld1", "pool_ld2", "pool_ld3",
                  "pool_ld4", "pool_rg0", "pool_rg1", "pool_rg2", "pool_rg3", "pool_rg4", "pool_rg5"):
            c.newsem(n)
        self.make_ident()

    def bank(self, i, n=512):
        return self.ps[:, i * 512:i * 512 + n]

    def bank_bf(self, i):
        return self.ps[:, i * 512:(i + 1) * 512].bitcast(BF16)

    def make_ident(self):
        c = self.c
        tmp = c.sbuf("identf", [128, 128], F32)
        tb = Buf()
        c.op("pool", lambda e: e.memset(tmp[:], 1.0), writes=[tb])
        c.op("pool", lambda e: e.affine_select(out=tmp[:], in_=tmp[:], pattern=[[1, 128]], compare_op=ALU.is_equal,
                                               fill=0.0, base=0, channel_multiplier=-1), reads=[tb], writes=[tb])
        c.op("dve", lambda e: e.tensor_copy(out=self.ident[:], in_=tmp[:]), reads=[tb], writes=[self.identb])
        self.identf = tmp
        self.identfb = tb

    def load_x(self):
        c = self.c
        x = self.dr["x"]
        for tt in range(NT):
            c.dma("sp", "io", self.xres[:, tt, :], x[tt * 128:(tt + 1) * 128, :], writes=[self.xb[tt]])
        for tt in range(NT):
            self.xb[tt].w = ("sp_io", c.val["sp_io"])

    def store_x(self, name="y"):
        c = self.c
        y = self.dr[name]
        for tt in range(NT):
            c.dma("sp", "io", y[tt * 128:(tt + 1) * 128, :], self.xres[:, tt, :], reads=[self.xb[tt]])
        c._wait("sp", "sp_io", c.val["sp_io"])

    def norm_open(self):
        es = ExitStack()
        self.nscope += 1
        t = "_n%d" % self.nscope
        self.gbc = es.enter_context(self.nc.sbuf_tensor("gbc" + t, [128, D], F32))
        self.gbcb = Buf("gbc")
        self.hn = [es.enter_context(self.nc.sbuf_tensor("hn%d%s" % (i, t), [128, D], BF16)) for i in range(2)]
        self.hnb = [Buf("hn%d" % i) for i in range(2)]
        return es

    def norm_stats(self):
        c = self.c
        junk = self.hn[0]
        for tt in range(NT):
            c.op("act", lambda e, tt=tt: e.activation(out=junk[:], in_=self.xres[:, tt, :], func=AF.Square,
                                                     accum_out=self.stat[:, tt:tt + 1]),
                 reads=[self.xb[tt]], writes=[self.hnb[0], self.statb])
        c.op("act", lambda e: e.activation(out=self.stat[:, 32:48], in_=self.stat[:, 0:16], func=AF.Sqrt, bias=EPS,
                                           scale=1.0 / D), reads=[self.statb], writes=[self.statb])
        c.op("dve", lambda e: e.reciprocal(out=self.stat[:, 16:32], in_=self.stat[:, 32:48]), reads=[self.statb],
             writes=[self.statb])

    def load_gain(self, gain_ap):
        self.c.dma("sp", "ld5", self.gbc[:], gain_ap.partition_broadcast(128), writes=[self.gbcb])

    def norm_to_hnT(self, gain_ap, free=False):
        c = self.c
        nes = self.norm_open()
        self.load_gain(gain_ap)
        self.norm_stats()
        for tt in range(NT):
            k = tt % 2
            hn = self.hn[k]
            c.op("dve", lambda e, tt=tt, hn=hn: e.scalar_tensor_tensor(out=hn[:], in0=self.xres[:, tt, :],
                                                                      scalar=self.stat[:, 16 + tt:17 + tt],
                                                                      in1=self.gbc[:], op0=ALU.mult, op1=ALU.mult),
                 reads=[self.xb[tt], self.statb, self.gbcb], writes=[self.hnb[k]])
            bk = 6 + k
            for dc in range(8):
                c.op("pe", lambda e, dc=dc, hn=hn, bk=bk: e.transpose(out=self.bank_bf(bk)[:, dc * 128:(dc + 1) * 128],
                                                                     in_=hn[:, dc * 128:(dc + 1) * 128],
                                                                     identity=self.ident[:]),
                     reads=[self.hnb[k], self.identb], writes=[self.pb[bk]])
            src = self.bank_bf(bk).rearrange("p (a b) -> p a b", a=8)
            c.op("act", lambda e, tt=tt, src=src: e.activation(out=self.hnT[:, :, tt * 128:(tt + 1) * 128], in_=src,
                                                              func=AF.Copy),
                 reads=[self.pb[bk]], writes=[self.hnTb[tt]])
        if free:
            nes.close()
        else:
            return nes
        return None

    def peer(self, l):
        c = self.c
        nc = self.nc
        es = ExitStack()
        self.es.enter_context(es)
        sb = lambda name, shape, dt: es.enter_context(nc.sbuf_tensor(name + "_%d" % l, list(shape), dt))
        dr = self.dr
        self.norm_to_hnT(dr["ffn_norm"][l], free=True)

        qT = sb("qT", [128, 8, S], BF16)
        qTb = [Buf() for _ in range(8)]
        thr = sb("thr", [128, NT, 8], F32)
        negc = sb("negc", [128, NT, 8], F32)
        thrb = [Buf() for _ in range(NT)]
        sk1T = sb("sk1T", [64, 8, 128], BF16)
        sk1Tb = Buf()
        KX = [sb("KX%d" % i, [128, 8, 512], BF16) for i in range(2)]
        KXlo = [Buf() for _ in range(2)]
        KXhi = Buf()

        with ExitStack() as pes:
            pb_ = lambda name, shape, dt: pes.enter_context(nc.sbuf_tensor(name + "_%d" % l, list(shape), dt))
            Wq = pb_("Wq", [128, 8, D], BF16)
            Wqb = Buf()
            c.dma("pool", "ld0", Wq[:], dr["peer_w_q"][l].rearrange("(dc p) f -> p dc f", p=128), writes=[Wqb])
            skn = pb_("skn", [128, 16, 64], BF16)
            sknb = Buf()
            c.dma("pool", "ld1", skn[:], dr["peer_sub_keys"][l].rearrange("h p n k -> n (h p) k"), writes=[sknb])
            SKBD = pb_("SKBD", [128, 8, 256], BF16)
            SKBDb = Buf()
            c.op("pool", lambda e: e.memset(SKBD[:], 0.0), writes=[SKBDb])
            for h in range(8):
                bk = h % 2
                c.op("pe", lambda e, h=h, bk=bk: e.transpose(out=self.bank_bf(bk)[:, 0:128],
                                                            in_=skn[:, 2 * h:2 * h + 2, :].rearrange("p a b -> p (a b)"),
                                                            identity=self.ident[:]),
                     reads=[sknb, self.identb], writes=[self.pb[bk]])
                c.op("dve", lambda e, h=h, bk=bk: e.tensor_copy(out=SKBD[0:64, h, 0:128], in_=self.bank_bf(bk)[0:64, 0:128]),
                     reads=[self.pb[bk]], writes=[SKBDb])
                c.op("dve", lambda e, h=h, bk=bk: e.tensor_copy(out=SKBD[64:128, h, 128:256],
                                                               in_=self.bank_bf(bk)[64:128, 0:128]),
                     reads=[self.pb[bk]], writes=[SKBDb])
                c.op("act", lambda e, h=h, bk=bk: e.activation(out=sk1T[:, h, :], in_=self.bank_bf(bk)[0:64, 0:128],
                                                              func=AF.Copy),
                     reads=[self.pb[bk]], writes=[sk1Tb])
                for kb in range(2):
                    c.op("act", lambda e, h=h, bk=bk, kb=kb: e.activation(
                        out=KX[kb][64:128, h, :].rearrange("p (a b) -> p a b", a=4),
                        in_=self.bank_bf(bk)[64:128, 0:128].unsqueeze(1).to_broadcast([64, 4, 128]), func=AF.Copy),
                         reads=[self.pb[bk]], writes=[KXhi])
            for h in range(8):
                for tb in range(4):
                    bk = 2 + (h * 4 + tb) % 4
                    for dc in range(8):
                        c.op("pe", lambda e, h=h, tb=tb, dc=dc, bk=bk: e.matmul(
                            self.bank(bk), lhsT=Wq[:, dc, h * 128:(h + 1) * 128], rhs=self.hnT[:, dc, tb * 512:(tb + 1) * 512],
                            start=(dc == 0), stop=(dc == 7)),
                             reads=[Wqb] + self.hnTb[tb * 4:tb * 4 + 4], writes=[self.pb[bk]])
                    eng = "act" if tb % 2 == 0 else "dve"
                    if eng == "act":
                        c.op("act", lambda e, h=h, tb=tb, bk=bk: e.activation(out=qT[:, h, tb * 512:(tb + 1) * 512],
                                                                             in_=self.bank(bk), func=AF.Copy),
                             reads=[self.pb[bk]], writes=[qTb[h]])
                    else:
                        c.op("dve", lambda e, h=h, tb=tb, bk=bk: e.tensor_copy(out=qT[:, h, tb * 512:(tb + 1) * 512],
                                                                              in_=self.bank(bk)),
                             reads=[self.pb[bk]], writes=[qTb[h]])
            SC = [pb_("SC%d" % i, [128, 16, 128], F32) for i in range(2)]
            SCb = [Buf() for _ in range(2)]
            TOP = pb_("TOP", [128, 16, 24], F32)
            TOPb = Buf()
            CAND = pb_("CAND", [128, 8, 289], F32)
            CANDb = Buf()
            CT = pb_("CT", [128, 8, 24], F32)
            CTb = Buf()
            sm = pb_("sm", [128, 8, 16], F32)
            smb = Buf()
            for tt in range(NT):
                k = tt % 2
                sc = SC[k]
                for h in range(8):
                    bk = 2 * k + h // 4
                    bk = 4 * k + h // 2
                    c.op("pe", lambda e, h=h, tt=tt, bk=bk: e.matmul(
                        self.ps[:, bk * 512 + (h % 2) * 256: bk * 512 + (h % 2) * 256 + 256],
                        lhsT=qT[:, h, tt * 128:(tt + 1) * 128], rhs=SKBD[:, h, :], start=True, stop=True),
                         reads=[qTb[h], SKBDb], writes=[self.pb[bk]])
                for q in range(4):
                    bk = 4 * k + q
                    c.op("act", lambda e, q=q, bk=bk, sc=sc: e.activation(
                        out=sc[:, 4 * q:4 * q + 4, :].rearrange("p a b -> p (a b)"), in_=self.bank(bk), func=AF.Copy),
                         reads=[self.pb[bk]], writes=[SCb[k]])
                for g in range(16):
                    c.op("dve", lambda e, g=g, sc=sc: e.max(out=TOP[:, g, 0:8], in_=sc[:, g, :]), reads=[SCb[k]], writes=[TOPb])
                    c.op("dve", lambda e, g=g, sc=sc: e.match_replace(out=sc[:, g, :], in_to_replace=TOP[:, g, 0:8],
                                                                     in_values=sc[:, g, :], imm_value=NEG),
                         reads=[TOPb, SCb[k]], writes=[SCb[k]])
                    c.op("dve", lambda e, g=g, sc=sc: e.max(out=TOP[:, g, 8:16], in_=sc[:, g, :]), reads=[SCb[k]], writes=[TOPb])
                    c.op("dve", lambda e, g=g, sc=sc: e.match_replace(out=sc[:, g, :], in_to_replace=TOP[:, g, 8:16],
                                                                     in_values=sc[:, g, :], imm_value=NEG),
                         reads=[TOPb, SCb[k]], writes=[SCb[k]])
                c.op("dve", lambda e, sc=sc: e.tensor_reduce(out=TOP[:, :, 16:17], in_=sc[:], axis=AX.X, op=ALU.max),
                     reads=[SCb[k]], writes=[TOPb])
                in0 = AP(TOP, 0, [[384, 128], [48, 8], [1, 17], [0, 17]])
                in1 = AP(TOP, 24, [[384, 128], [48, 8], [0, 17], [1, 17]])
                c.op("dve", lambda e, in0=in0, in1=in1: e.tensor_tensor(
                    out=CAND[:].rearrange("p h (a b) -> p h a b", a=17), in0=in0, in1=in1, op=ALU.add),
                     reads=[TOPb], writes=[CANDb])
                for h in range(8):
                    c.op("dve", lambda e, h=h: e.max(out=CT[:, h, 0:8], in_=CAND[:, h, :]), reads=[CANDb], writes=[CTb])
                    c.op("dve", lambda e, h=h: e.match_replace(out=CAND[:, h, :], in_to_replace=CT[:, h, 0:8],
                                                              in_values=CAND[:, h, :], imm_value=NEG),
                         reads=[CTb, CANDb], writes=[CANDb])
                    c.op("dve", lambda e, h=h: e.max(out=CT[:, h, 8:16], in_=CAND[:, h, :]), reads=[CANDb], writes=[CTb])
                    c.op("dve", lambda e, h=h: e.match_replace(out=CAND[:, h, :], in_to_replace=CT[:, h, 8:16],
                                                              in_values=CAND[:, h, :], imm_value=NEG),
                         reads=[CTb, CANDb], writes=[CANDb])
                c.op("dve", lambda e: e.tensor_reduce(out=CT[:, :, 16:17], in_=CAND[:], axis=AX.X, op=ALU.max),
                     reads=[CANDb], writes=[CTb])
                c.op("dve", lambda e, tt=tt: e.tensor_tensor(out=thr[:, tt, :].unsqueeze(2), in0=CT[:, :, 15:16],
                                                            in1=CT[:, :, 16:17], op=ALU.add),
                     reads=[CTb], writes=[thrb[tt]])
                c.op("dve", lambda e, tt=tt: e.tensor_scalar(out=thr[:, tt, :], in0=thr[:, tt, :], scalar1=0.5,
                                                            scalar2=None, op0=ALU.mult),
                     reads=[thrb[tt]], writes=[thrb[tt]])
                c.op("dve", lambda e: e.tensor_tensor(out=sm[:], in0=CT[:, :, 0:16],
                                                      in1=CT[:, :, 0:1].to_broadcast([128, 8, 16]), op=ALU.subtract),
                     reads=[CTb], writes=[smb])
                c.op("act", lambda e: e.activation(out=sm[:], in_=sm[:], func=AF.Exp), reads=[smb], writes=[smb])
                c.op("dve", lambda e: e.tensor_reduce(out=CT[:, :, 17:18], in_=sm[:], axis=AX.X, op=ALU.add),
                     reads=[smb], writes=[CTb])
                c.op("act", lambda e: e.activation(out=CT[:, :, 18:19], in_=CT[:, :, 17:18], func=AF.Ln),
                     reads=[CTb], writes=[CTb])
                c.op("dve", lambda e, tt=tt: e.scalar_tensor_tensor(out=negc[:, tt, :].unsqueeze(2), in0=CT[:, :, 18:19],
                                                                   scalar=-1.0, in1=CT[:, :, 0:1], op0=ALU.mult,
                                                                   op1=ALU.subtract),
                     reads=[CTb], writes=[thrb[tt]])
            c.barrier()
        if PEER_NEB == 0:
            c.barrier()
            es.close()
            return
        u = dr["peer_u"][l]
        v = dr["peer_v"][l]
        Ubf = sb("Ubf", [128, 4, D], BF16)
        Ubfb = Buf()
        UT1 = sb("UT", [128, 8, 512], BF16)
        UT = [UT1, UT1]
        UTb1 = Buf()
        UTb = [UTb1, UTb1]
        Vb = [sb("Vb%d" % i, [128, 4, D], BF16) for i in range(2)]
        Vbb = [Buf() for _ in range(2)]
        GA = sb("GA", [128, NT, 512], BF16)
        GAb = [Buf() for _ in range(NT)]
        NS = 3
        E = [sb("E%d" % i, [128, 512], BF16) for i in range(2)]
        Eb = [Buf() for _ in range(2)]
        G = [sb("G%d" % i, [128, 512], BF16) for i in range(4)]
        Gb = [Buf() for _ in range(4)]
        PT = [sb("PT%d" % i, [128, 512], BF16) for i in range(2)]
        PTb = [Buf() for _ in range(2)]
        NEB = PEER_NEB
        NU = NEB * NT
        B_S = (0, 1, 2)
        B_AT = 3
        B_WT = (4, 5)
        B_O = (6, 7)

        def load_U(eb):
            c.dma("pool", "ld2", Ubf[:], u[eb * 512:(eb + 1) * 512, :].rearrange("(ec p) d -> p ec d", p=128),
                  writes=[Ubfb])

        def load_V(eb):
            c.dma("pool", "ld3" if eb % 2 == 0 else "ld4", Vb[eb % 2][:],
                  v[eb * 512:(eb + 1) * 512, :].rearrange("(ec p) d -> p ec d", p=128), writes=[Vbb[eb % 2]])

        def build_KX(eb):
            kx = KX[eb % 2]
            c.op("pool", lambda e: e.tensor_copy(
                out=kx[0:64, :, :].rearrange("p h (a b) -> p h a b", a=4),
                in_=sk1T[:, :, eb * 4:eb * 4 + 4].unsqueeze(3).broadcast_to([64, 8, 4, 128])),
                 reads=[sk1Tb], writes=[KXlo[eb % 2]])

        def transp_eb(eb):
            ut = UT[eb % 2]
            for dcp in range(4):
                for jj in range(2):
                    dc = dcp * 2 + jj
                    for ec in range(4):
                        c.op("pe", lambda e: e.transpose(
                            out=self.bank_bf(B_AT)[:, jj * 512 + ec * 128: jj * 512 + (ec + 1) * 128],
                            in_=Ubf[:, ec, dc * 128:(dc + 1) * 128], identity=self.ident[:]),
                             reads=[Ubfb, self.identb], writes=[self.pb[B_AT]])
                c.op("act", lambda e: e.activation(
                    out=ut[:, 2 * dcp:2 * dcp + 2, :].rearrange("p a b -> p (a b)"), in_=self.bank_bf(B_AT), func=AF.Copy),
                     reads=[self.pb[B_AT]], writes=[UTb[eb % 2]])

        def emit_AT(eb, tt):
            ut = UT[eb % 2]
            for ec in range(4):
                for dc in range(8):
                    c.op("pe", lambda e: e.matmul(
                        self.ps[:, B_AT * 512 + ec * 128: B_AT * 512 + (ec + 1) * 128],
                        lhsT=ut[:, dc, ec * 128:(ec + 1) * 128], rhs=self.hnT[:, dc, tt * 128:(tt + 1) * 128],
                        start=(dc == 0), stop=(dc == 7)),
                         reads=[UTb[eb % 2], self.hnTb[tt]], writes=[self.pb[B_AT]])

        def emit_Acopy(tt):
            c.op("act", lambda e: e.activation(out=GA[:, tt, :], in_=self.bank(B_AT), func=AF.Copy),
                 reads=[self.pb[B_AT]], writes=[GAb[tt]])

        def emit_gelu_batch():
            for tt in range(NT):
                c.op("act", lambda e: e.activation(out=GA[:, tt, :], in_=GA[:, tt, :], func=self.gelu_func),
                     reads=[GAb[tt]], writes=[GAb[tt]])

        def unit_of(n):
            uu = n // 8
            return uu, uu // NT, uu % NT, n % 8

        def emit_S(n):
            uu, eb, tt, h = unit_of(n)
            bk = B_S[n % NS]
            c.op("pe", lambda e: e.matmul(self.bank(bk), lhsT=qT[:, h, tt * 128:(tt + 1) * 128], rhs=KX[eb % 2][:, h, :],
                                          start=True, stop=True),
                 reads=[qTb[h], KXlo[eb % 2], KXhi], writes=[self.pb[bk]])

        def emit_EG(n):
            uu, eb, tt, h = unit_of(n)
            bk = B_S[n % NS]
            e2 = n % 2
            g4 = n % 4
            c.op("act", lambda e: e.activation(out=E[e2][:], in_=self.bank(bk), func=AF.Exp, bias=negc[:, tt, h:h + 1]),
                 reads=[self.pb[bk], thrb[tt]], writes=[Eb[e2]])
            c.op("dve", lambda e: e.scalar_tensor_tensor(out=G[g4][:], in0=self.bank(bk), scalar=thr[:, tt, h:h + 1],
                                                         in1=E[e2][:], op0=ALU.is_gt, op1=ALU.mult),
                 reads=[self.pb[bk], thrb[tt], Eb[e2]], writes=[Gb[g4]])

        def emit_T(n):
            uu, eb, tt, h = unit_of(n)
            wtb = B_WT[uu % 2]
            g4 = n % 4
            for ec in range(4):
                c.op("pe", lambda e: e.matmul(self.ps[:, wtb * 512 + ec * 128: wtb * 512 + (ec + 1) * 128],
                                              lhsT=G[g4][:, ec * 128:(ec + 1) * 128], rhs=self.ident[:],
                                              start=(h == 0 and ec == 0), stop=(h == 7), skip_group_check=True),
                     reads=[Gb[g4], self.identb], writes=[self.pb[wtb]])

        def emit_PT(uu):
            eb, tt = uu // NT, uu % NT
            wtb = B_WT[uu % 2]
            c.op("dve", lambda e: e.tensor_tensor(out=PT[uu % 2][:], in0=self.bank(wtb), in1=GA[:, tt, :], op=ALU.mult),
                 reads=[self.pb[wtb], GAb[tt]], writes=[PTb[uu % 2]])

        def emit_V(uu):
            eb, tt = uu // NT, uu % NT
            vb = Vb[eb % 2]
            for dh in range(2):
                for ec in range(4):
                    c.op("pe", lambda e: e.matmul(self.bank(B_O[dh]), lhsT=PT[uu % 2][:, ec * 128:(ec + 1) * 128],
                                                  rhs=vb[:, ec, dh * 512:(dh + 1) * 512], start=(ec == 0), stop=(ec == 3)),
                         reads=[PTb[uu % 2], Vbb[eb % 2]], writes=[self.pb[B_O[dh]]])

        def emit_xacc(uu):
            eb, tt = uu // NT, uu % NT
            for dh in range(2):
                xs = self.xres[:, tt, dh * 512:(dh + 1) * 512]
                c.op("dve", lambda e: e.tensor_tensor(out=xs, in0=self.bank(B_O[dh]), in1=xs, op=ALU.add),
                     reads=[self.pb[B_O[dh]], self.xb[tt]], writes=[self.xb[tt]])

        if NEB == 0:
            c.barrier()
            es.close()
            return
        load_U(0)
        load_V(0)
        build_KX(0)
        transp_eb(0)
        for tt in range(NT):
            emit_AT(0, tt)
            emit_Acopy(tt)
        emit_gelu_batch()
        load_U(1)
        transp_eb(1)
        NH = NU * 8
        LT = 2
        for k in range(min(NH + 8, PEER_MAXSTEPS)):
            if k < NH:
                uu, eb, tt, h = unit_of(k)
                if h == 0 and tt == 1:
                    if eb + 1 < NEB:
                        load_V(eb + 1)
                        build_KX(eb + 1)
                    if eb + 2 < NEB:
                        load_U(eb + 2)
                emit_S(k)
                emit_EG(k)
            if 0 <= k - LT < NH:
                emit_T(k - LT)
            for (lag, what) in ((LT, "PT"), (LT + 2, "V"), (LT + 4, "X")):
                n = k - lag
                if n >= 7 and n < NH and n % 8 == 7:
                    uu = n // 8
                    eb, tt = uu // NT, uu % NT
                    if what == "PT":
                        emit_PT(uu)
                    elif what == "V":
                        emit_V(uu)
                        if eb + 1 < NEB:
                            emit_AT(eb + 1, tt)
                    else:
                        emit_xacc(uu)
                        if eb + 1 < NEB:
                            emit_Acopy(tt)
                            if tt == NT - 1:
                                if eb + 2 < NEB:
                                    transp_eb(eb + 2)
                                emit_gelu_batch()
        c.barrier()
        es.close()

    gelu_func = AF.Gelu

    def ring_open(self, es, nslots, tag):
        self.ring = [es.enter_context(self.nc.sbuf_tensor("ring%s_%d" % (tag, i), [128, 8, 256], BF16)) for i in range(nslots)]
        self.ringb = [Buf() for _ in range(nslots)]
        self.ringsem = []
        for i in range(nslots):
            self.ringsem.append("rg%d" % i)
        self.ringi = 0

    def wload_cols(self, w2d, cols):
        i = self.ringi % len(self.ring)
        self.ringi += 1
        t, b = self.ring[i], self.ringb[i]
        off = 0
        for (c0, n) in cols:
            self.c.dma("pool", self.ringsem[i], t[:, :, off:off + n],
                       w2d[:, c0:c0 + n].rearrange("(dc p) f -> p dc f", p=128), writes=[b])
            off += n
        return t, b

    def wload_rows(self, w2d, row0):
        i = self.ringi % len(self.ring)
        self.ringi += 1
        t, b = self.ring[i], self.ringb[i]
        v = t[:].rearrange("p a b -> p (a b)").rearrange("p (a b) -> p a b", a=2)
        self.c.dma("pool", self.ringsem[i], v, w2d[row0:row0 + 256, :].rearrange("(a p) d -> p a d", p=128), writes=[b])
        return v, b

    def even_mixer(self, j=0):
        c = self.c
        nc = self.nc
        dr = self.dr
        es = ExitStack()
        self.es.enter_context(es)
        sb = lambda name, shape, dt: es.enter_context(nc.sbuf_tensor(name + "_e%d" % j, list(shape), dt))
        es.enter_context(self.norm_to_hnT(dr["mix_norm"][2 * j]))
        win = dr["even_w_in"][j]
        wout = dr["even_w_out"][j]
        self.ring_open(es, 6, "e")
        PB = self.pb

        prm = sb("prm", [128, 8, 8], F32)
        prmb = Buf()
        kw = dict(allow_slow_non_contiguous=True)
        for k in range(4):
            c.dma("sp", "ld0", prm[:, :, k:k + 1], dr["lru_conv_w"][j][k].rearrange("(c p o) -> p c o", p=128, o=1),
                  writes=[prmb], **kw)
        for idx, nm in ((4, "lru_conv_b"), (5, "lru_b_a"), (6, "lru_b_i"), (7, "lru_lambda")):
            c.dma("sp", "ld0", prm[:, :, idx:idx + 1], dr[nm][j].rearrange("(c p o) -> p c o", p=128, o=1), writes=[prmb], **kw)
        c.op("act", lambda e: e.activation(out=prm[:, :, 7:8], in_=prm[:, :, 7:8], func=AF.Exp, scale=-1.0), reads=[prmb], writes=[prmb])
        c.op("act", lambda e: e.activation(out=prm[:, :, 7:8], in_=prm[:, :, 7:8], func=AF.Ln, bias=1.0), reads=[prmb], writes=[prmb])
        c.op("dve", lambda e: e.tensor_scalar(out=prm[:, :, 7:8], in0=prm[:, :, 7:8], scalar1=-8.0, scalar2=None, op0=ALU.mult),
             reads=[prmb], writes=[prmb])
        WBD = sb("WBD", [128, 2, 8, 128], BF16)
        WBDb = Buf()
        c.op("pool", lambda e: e.memset(WBD[:], 0.0), writes=[WBDb])
        for gi, nm in enumerate(("lru_w_a", "lru_w_i")):
            for hl in range(2):
                c.dma("pool", "ld1", WBD[hl * 64:(hl + 1) * 64, gi, :, hl * 64:(hl + 1) * 64],
                      dr[nm][j].rearrange("(c two) i j -> two i c j", two=2)[hl], writes=[WBDb])

        YT = sb("YT", [128, 8, S], BF16)
        YTb = [Buf() for _ in range(8)]
        with ExitStack() as les:
            lb = lambda name, shape, dt: les.enter_context(nc.sbuf_tensor(name + "_e%d" % j, list(shape), dt))
            T0 = lb("T0", [128, 3 + S], F32)
            T0b = Buf()
            T = [lb("T%d" % i, [128, 1024], F32) for i in range(1, 5)]
            Tb = [Buf() for _ in range(4)]
            XCb = lb("XCb", [128, 1024], BF16)
            XCbb = Buf()
            GAg = lb("GAg", [128, 1024], BF16)
            GAgb = Buf()
            hl_ = lb("hlast", [128, 2], F32)
            hlb = Buf()
            c.op("pool", lambda e: e.memset(T0[:, 0:3], 0.0), writes=[T0b])
            for ch in range(8):
                wt, wb = self.wload_cols(win, [(ch * 128, 128), (1024 + ch * 128, 128)])
                for hf in range(2):
                    t0 = hf * 1024
                    for q in range(2):
                        for dc in range(8):
                            c.op("pe", lambda e: e.matmul(self.bank(q), lhsT=wt[:, dc, 128:256],
                                                          rhs=self.hnT[:, dc, t0 + q * 512:t0 + (q + 1) * 512],
                                                          start=(dc == 0), stop=(dc == 7)),
                                 reads=[wb] + self.hnTb[hf * 8 + q * 4: hf * 8 + q * 4 + 4], writes=[PB[q]])
                    for q in range(2):
                        for dc in range(8):
                            c.op("pe", lambda e: e.matmul(self.bank(2 + q), lhsT=wt[:, dc, 0:128],
                                                          rhs=self.hnT[:, dc, t0 + q * 512:t0 + (q + 1) * 512],
                                                          start=(dc == 0), stop=(dc == 7)),
                                 reads=[wb] + self.hnTb[hf * 8 + q * 4: hf * 8 + q * 4 + 4], writes=[PB[2 + q]])
                    c.op("act", lambda e: e.activation(out=T0[:, 3 + t0:3 + t0 + 1024], in_=self.ps[:, 0:1024], func=AF.Copy),
                         reads=[PB[0], PB[1]], writes=[T0b])
                    XC = T[0]
                    c.op("dve", lambda e: e.tensor_scalar(out=XC[:], in0=T0[:, 3 + t0:3 + t0 + 1024], scalar1=prm[:, ch, 3:4],
                                                          scalar2=prm[:, ch, 4:5], op0=ALU.mult, op1=ALU.add),
                         reads=[T0b, prmb], writes=[Tb[0]])
                    for k in range(3):
                        c.op("dve", lambda e: e.scalar_tensor_tensor(out=XC[:], in0=T0[:, k + t0:k + t0 + 1024],
                                                                     scalar=prm[:, ch, k:k + 1], in1=XC[:], op0=ALU.mult,
                                                                     op1=ALU.add), reads=[T0b, prmb, Tb[0]], writes=[Tb[0]])
                    c.op("pool", lambda e: e.tensor_copy(out=XCb[:], in_=XC[:]), reads=[Tb[0]], writes=[XCbb])
                    for gi in range(2):
                        for q in range(2):
                            bk = 4 + gi * 2 + q
                            c.op("pe", lambda e: e.matmul(self.bank(bk), lhsT=WBD[:, gi, ch, :], rhs=XCb[:, q * 512:(q + 1) * 512],
                                                          start=True, stop=True), reads=[WBDb, XCbb], writes=[PB[bk]])
                    A, M, IS = T[1], T[2], T[3]
                    c.op("act", lambda e: e.activation(out=A[:], in_=self.ps[:, 2048:3072], func=AF.Sigmoid, bias=prm[:, ch, 5:6]),
                         reads=[PB[4], PB[5], prmb], writes=[Tb[1]])
                    c.op("act", lambda e: e.activation(out=IS[:], in_=self.ps[:, 3072:4096], func=AF.Sigmoid, bias=prm[:, ch, 6:7]),
                         reads=[PB[6], PB[7], prmb], writes=[Tb[3]])
                    c.op("act", lambda e: e.activation(out=A[:], in_=A[:], func=AF.Exp, scale=prm[:, ch, 7:8]),
                         reads=[Tb[1], prmb], writes=[Tb[1]])
                    c.op("act", lambda e: e.activation(out=M[:], in_=A[:], func=AF.Square), reads=[Tb[1]], writes=[Tb[2]])
                    c.op("act", lambda e: e.activation(out=M[:], in_=M[:], func=AF.Sqrt, scale=-1.0, bias=1.0), reads=[Tb[2]], writes=[Tb[2]])
                    c.op("dve", lambda e: e.tensor_tensor(out=IS[:], in0=IS[:], in1=M[:], op=ALU.mult), reads=[Tb[3], Tb[2]], writes=[Tb[3]])
                    c.op("dve", lambda e: e.tensor_tensor(out=IS[:], in0=IS[:], in1=XC[:], op=ALU.mult), reads=[Tb[3], Tb[0]], writes=[Tb[3]])
                    init = 0.0 if hf == 0 else hl_[:, 0:1]
                    c.op("dve", lambda e: e.tensor_tensor_scan(out=M[:], data0=A[:], data1=IS[:], initial=init, op0=ALU.mult,
                                                               op1=ALU.add), reads=[Tb[1], Tb[3], hlb], writes=[Tb[2]])
                    if hf == 0:
                        c.op("dve", lambda e: e.tensor_copy(out=hl_[:, 0:1], in_=M[:, 1023:1024]), reads=[Tb[2]], writes=[hlb])
                    c.op("act", lambda e: e.activation(out=GAg[:], in_=self.ps[:, 1024:2048], func=self.gelu_func),
                         reads=[PB[2], PB[3]], writes=[GAgb])
                    c.op("dve", lambda e: e.tensor_tensor(out=YT[:, ch, t0:t0 + 1024], in0=M[:], in1=GAg[:], op=ALU.mult),
                         reads=[Tb[2], GAgb], writes=[YTb[ch]])
        wo = [self.wload_rows(wout, r * 256) for r in range(4)]
        self.out_proj(YT, YTb, wo, 8)
        c.barrier()
        es.close()
        self.retention(j)

    def retention(self, j):
        import math
        c = self.c
        nc = self.nc
        dr = self.dr
        PB = self.pb
        win = dr["even_w_in"][j]
        wout = dr["even_w_out"][j]
        es = ExitStack()
        self.es.enter_context(es)
        rb = lambda name, shape, dt: es.enter_context(nc.sbuf_tensor(name + "_r%d" % j, list(shape), dt))
        self.ring_open(es, 6, "r")
        COS = rb("COS", [128, S], F32)
        SIN = rb("SIN", [128, S], F32)
        tabb = Buf()
        DB = rb("DB", [128, 4, 128], F32)
        DM = rb("DM", [128, 4, 128], F32)
        decb = Buf()
        PI = math.pi
        with ExitStack() as tes:
            tb_ = lambda name, shape, dt: tes.enter_context(nc.sbuf_tensor(name + "_r%d" % j, list(shape), dt))
            invf = tb_("invf", [128, 4], F32)
            ANG = tb_("ANG", [128, S], F32)
            KI = tb_("KI", [128, S], mybir.dt.int32)
            KF = tb_("KF", [128, S], F32)
            tmb = Buf()
            c.op("pool", lambda e: e.iota(out=invf[:, 0:1], pattern=[[0, 1]], base=0, channel_multiplier=1,
                                          allow_small_or_imprecise_dtypes=True), writes=[tmb])
            c.op("dve", lambda e: e.tensor_scalar(out=invf[:, 1:2], in0=invf[:, 0:1], scalar1=-1.0 / 128, scalar2=None,
                                                  op0=ALU.mult), reads=[tmb], writes=[tmb])
            c.op("pool", lambda e: e.memset(invf[:, 2:3], 10000.0), reads=[tmb], writes=[tmb])
            c.op("pool", lambda e: e.tensor_tensor(out=invf[:, 3:4], in0=invf[:, 2:3], in1=invf[:, 1:2], op=ALU.pow),
                 reads=[tmb], writes=[tmb])
            c.op("pool", lambda e: e.iota(out=ANG[:], pattern=[[1, S]], base=0, channel_multiplier=0,
                                          allow_small_or_imprecise_dtypes=True), reads=[tmb], writes=[tmb])
            c.op("dve", lambda e: e.tensor_scalar(out=ANG[:], in0=ANG[:], scalar1=invf[:, 3:4], scalar2=None, op0=ALU.mult),
                 reads=[tmb], writes=[tmb])
            for dst, shift in ((SIN, 0.0), (COS, PI / 2)):
                c.op("dve", lambda e: e.tensor_scalar(out=KF[:], in0=ANG[:], scalar1=shift + PI, scalar2=1.0 / (2 * PI),
                                                      op0=ALU.add, op1=ALU.mult), reads=[tmb], writes=[tmb])
                c.op("dve", lambda e: e.tensor_copy(out=KI[:], in_=KF[:]), reads=[tmb], writes=[tmb])
                c.op("dve", lambda e: e.tensor_copy(out=KF[:], in_=KI[:]), reads=[tmb], writes=[tmb])
                c.op("dve", lambda e: e.scalar_tensor_tensor(out=KF[:], in0=KF[:], scalar=-2 * PI, in1=ANG[:], op0=ALU.mult,
                                                             op1=ALU.add), reads=[tmb], writes=[tmb])
                if shift != 0.0:
                    c.op("dve", lambda e: e.tensor_scalar(out=KF[:], in0=KF[:], scalar1=shift, scalar2=None, op0=ALU.add),
                         reads=[tmb], writes=[tmb])
                c.op("dve", lambda e: e.tensor_scalar(out=dst[:], in0=KF[:], scalar1=-PI, scalar2=2 * PI, op0=ALU.is_lt,
                                                      op1=ALU.mult), reads=[tmb], writes=[tabb])
                c.op("dve", lambda e: e.tensor_tensor(out=dst[:], in0=dst[:], in1=KF[:], op=ALU.add), reads=[tmb, tabb], writes=[tabb])
                c.op("dve", lambda e: e.tensor_scalar(out=dst[:], in0=dst[:], scalar1=-3.14159, scalar2=3.14159, op0=ALU.max,
                                                      op1=ALU.min), reads=[tabb], writes=[tabb])
                c.op("act", lambda e: e.activation(out=dst[:], in_=dst[:], func=AF.Sin), reads=[tabb], writes=[tabb])
            c.op("pool", lambda e: e.iota(out=KF[:, 0:128], pattern=[[1, 128]], base=0, channel_multiplier=-1,
                                          allow_small_or_imprecise_dtypes=True), reads=[tmb], writes=[tmb])
            for h in range(4):
                lg = math.log1p(-2.0 ** (-5 - h))
                c.op("act", lambda e: e.activation(out=DB[:, h, :], in_=KF[:, 0:128], func=AF.Exp, scale=lg), reads=[tmb], writes=[decb])
                c.op("pool", lambda e: e.affine_select(out=DM[:, h, :], in_=DB[:, h, :], pattern=[[1, 128]], compare_op=ALU.is_ge,
                                                       fill=0.0, base=0, channel_multiplier=-1), reads=[decb], writes=[decb])
            c.barrier()
        qT = rb("qTr", [128, 2, S], BF16)
        kT = rb("kTr", [128, 2, S], BF16)
        qkb = [Buf(), Buf()]
        VH = rb("VH", [128, NT, 256], BF16)
        VHb = Buf()
        YBT = rb("YBT", [128, 2, S], BF16)
        YBTb = [Buf(), Buf()]
        R = [rb("R%d" % i, [128, 512], F32) for i in range(4)]
        Rb = [Buf() for _ in range(4)]
        SC = [rb("SCr%d" % i, [128, 128], BF16) for i in range(4)]
        SCb = [Buf() for _ in range(4)]
        SG = [rb("SG%d" % i, [128, 256], F32) for i in range(2)]
        SGb = [Buf() for _ in range(2)]
        yb = [rb("yb%d" % i, [128, 256], BF16) for i in range(2)]
        ybb = [Buf() for _ in range(2)]
        junk = rb("junkr", [128, 256], BF16)
        junkb = Buf()
        ssr = rb("ssr", [128, 4], F32)
        ssrb = Buf()
        sreg = [Buf() for _ in range(8)]
        cnt = 0
        pj = 0
        for h in range(4):
            lg = math.log1p(-2.0 ** (-5 - h))
            wq = self.wload_cols(win, [(2048 + h * 256, 256)])
            wk = self.wload_cols(win, [(3072 + h * 256, 256)])
            wv = self.wload_cols(win, [(4096 + h * 256, 256)])
            wg = self.wload_cols(win, [(5120 + h * 256, 256)])
            wo = self.wload_rows(wout, 1024 + h * 256)
            for jq, ((wt, wb), dst) in enumerate(((wq, qT), (wk, kT))):
                for tb in range(4):
                    ba, bb = (4, 5) if pj % 2 == 0 else (6, 7)
                    pj += 1
                    ts = slice(tb * 512, (tb + 1) * 512)
                    for half, bk in ((0, ba), (1, bb)):
                        for dc in range(8):
                            c.op("pe", lambda e: e.matmul(self.bank(bk), lhsT=wt[:, dc, half * 128:(half + 1) * 128],
                                                          rhs=self.hnT[:, dc, ts], start=(dc == 0), stop=(dc == 7)),
                                 reads=[wb] + self.hnTb[tb * 4:tb * 4 + 4], writes=[PB[bk]])
                    c.op("dve", lambda e: e.tensor_tensor(out=R[0][:], in0=self.bank(ba), in1=COS[:, ts], op=ALU.mult),
                         reads=[PB[ba], tabb], writes=[Rb[0]])
                    c.op("dve", lambda e: e.tensor_tensor(out=R[1][:], in0=self.bank(bb), in1=SIN[:, ts], op=ALU.mult),
                         reads=[PB[bb], tabb], writes=[Rb[1]])
                    c.op("pool", lambda e: e.tensor_tensor(out=dst[:, 0, ts], in0=R[0][:], in1=R[1][:], op=ALU.subtract),
                         reads=[Rb[0], Rb[1]], writes=[qkb[jq]])
                    c.op("dve", lambda e: e.tensor_tensor(out=R[2][:], in0=self.bank(ba), in1=SIN[:, ts], op=ALU.mult),
                         reads=[PB[ba], tabb], writes=[Rb[2]])
                    c.op("dve", lambda e: e.tensor_tensor(out=R[3][:], in0=self.bank(bb), in1=COS[:, ts], op=ALU.mult),
                         reads=[PB[bb], tabb], writes=[Rb[3]])
                    c.op("pool", lambda e: e.tensor_tensor(out=dst[:, 1, ts], in0=R[2][:], in1=R[3][:], op=ALU.add),
                         reads=[Rb[2], Rb[3]], writes=[qkb[jq]])
            for tt in range(NT):
                bk = 4 + (tt // 2) % 4
                reg = self.ps[:, bk * 512 + (tt % 2) * 256: bk * 512 + (tt % 2) * 256 + 256]
                for dc in range(8):
                    c.op("pe", lambda e: e.matmul(reg, lhsT=self.hnT[:, dc, tt * 128:(tt + 1) * 128], rhs=wv[0][:, dc, :],
                                                  start=(dc == 0), stop=(dc == 7)), reads=[wv[1], self.hnTb[tt]], writes=[PB[bk]])
                c.op("act", lambda e: e.activation(out=VH[:, tt, :], in_=reg, func=AF.Copy), reads=[PB[bk]], writes=[VHb])
            for ci in range(NT):
                k2 = ci % 2
                ybk = 2 + k2
                Y = self.ps[:, ybk * 512: ybk * 512 + 256]
                GP = self.ps[:, ybk * 512 + 256: ybk * 512 + 512]
                for cj in range(ci + 1):
                    r = cnt % 8
                    cnt += 1
                    st = self.ps[:, r * 128:(r + 1) * 128]
                    for dch in range(2):
                        c.op("pe", lambda e: e.matmul(st, lhsT=kT[:, dch, cj * 128:(cj + 1) * 128],
                                                      rhs=qT[:, dch, ci * 128:(ci + 1) * 128], start=(dch == 0), stop=(dch == 1)),
                             reads=[qkb[0], qkb[1]], writes=[sreg[r]])
                    const = math.exp(lg * 128.0 * (ci - cj)) / 16.0
                    dmat = DM[:, h, :] if cj == ci else DB[:, h, :]
                    sc = SC[r % 4]
                    c.op("dve", lambda e: e.scalar_tensor_tensor(out=sc[:], in0=st, scalar=const, in1=dmat, op0=ALU.mult,
                                                                 op1=ALU.mult), reads=[sreg[r], decb], writes=[SCb[r % 4]])
                    c.op("pe", lambda e: e.matmul(Y, lhsT=sc[:], rhs=VH[:, cj, :], start=(cj == 0), stop=(cj == ci)),
                         reads=[SCb[r % 4], VHb], writes=[PB[ybk]])
                for dc in range(8):
                    c.op("pe", lambda e: e.matmul(GP, lhsT=self.hnT[:, dc, ci * 128:(ci + 1) * 128], rhs=wg[0][:, dc, :],
                                                  start=(dc == 0), stop=(dc == 7), skip_group_check=True),
                         reads=[wg[1], self.hnTb[ci]], writes=[PB[ybk]])
                c.op("act", lambda e: e.activation(out=SG[k2][:], in_=GP, func=AF.Silu), reads=[PB[ybk]], writes=[SGb[k2]])
                c.op("act", lambda e: e.activation(out=junk[:], in_=Y, func=AF.Square, accum_out=ssr[:, 0:1]),
                     reads=[PB[ybk]], writes=[junkb, ssrb])
                c.op("act", lambda e: e.activation(out=ssr[:, 1:2], in_=ssr[:, 0:1], func=AF.Sqrt, bias=EPS, scale=1.0 / 256),
                     reads=[ssrb], writes=[ssrb])
                c.op("dve", lambda e: e.reciprocal(out=ssr[:, 2:3], in_=ssr[:, 1:2]), reads=[ssrb], writes=[ssrb])
                c.op("dve", lambda e: e.scalar_tensor_tensor(out=yb[k2][:], in0=Y, scalar=ssr[:, 2:3], in1=SG[k2][:],
                                                             op0=ALU.mult, op1=ALU.mult),
                     reads=[PB[ybk], ssrb, SGb[k2]], writes=[ybb[k2]])
                tbk = 6 + k2
                for eh in range(2):
                    c.op("pe", lambda e: e.transpose(out=self.bank_bf(tbk)[:, eh * 128:(eh + 1) * 128],
                                                     in_=yb[k2][:, eh * 128:(eh + 1) * 128], identity=self.ident[:]),
                         reads=[ybb[k2], self.identb], writes=[PB[tbk]])
                c.op("act", lambda e: e.activation(out=YBT[:, :, ci * 128:(ci + 1) * 128],
                                                   in_=self.bank_bf(tbk)[:, 0:256].rearrange("p (a b) -> p a b", a=2), func=AF.Copy),
                     reads=[PB[tbk]], writes=[YBTb[0], YBTb[1]])
            self.out_proj(YBT, YBTb, [wo], 2)
        c.barrier()
        es.close()

    def out_proj(self, YT, YTb, wo, nchunks, rstd=None, rstdb=None):
        c = self.c
        for tt in range(NT):
            for dh in range(2):
                bk = (tt * 2 + dh) % 4
                for ch in range(nchunks):
                    wv, wb = wo[ch // 2]
                    c.op("pe", lambda e: e.matmul(self.bank(bk), lhsT=YT[:, ch, tt * 128:(tt + 1) * 128],
                                                  rhs=wv[:, ch % 2, dh * 512:(dh + 1) * 512], start=(ch == 0),
                                                  stop=(ch == nchunks - 1)), reads=[YTb[ch], wb], writes=[self.pb[bk]])
                xs = self.xres[:, tt, dh * 512:(dh + 1) * 512]
                if rstd is None:
                    c.op("dve", lambda e: e.tensor_tensor(out=xs, in0=self.bank(bk), in1=xs, op=ALU.add),
                         reads=[self.pb[bk], self.xb[tt]], writes=[self.xb[tt]])
                else:
                    c.op("dve", lambda e: e.scalar_tensor_tensor(out=xs, in0=self.bank(bk), scalar=rstd[:, tt:tt + 1], in1=xs,
                                                                 op0=ALU.mult, op1=ALU.add),
                         reads=[self.pb[bk], self.xb[tt], rstdb], writes=[self.xb[tt]])

    def odd_mixer(self, j=0):
        c = self.c
        nc = self.nc
        dr = self.dr
        PB = self.pb
        win = dr["ssm_w_in"][j]
        wout = dr["ssm_w_out"][j]
        ygs = dr["ygs"]
        es = ExitStack()
        self.es.enter_context(es)
        ob = lambda name, shape, dt: es.enter_context(nc.sbuf_tensor(name + "_o%d" % j, list(shape), dt))
        es.enter_context(self.norm_to_hnT(dr["mix_norm"][2 * j + 1]))
        self.ring_open(es, 4, "o")
        kw = dict(allow_slow_non_contiguous=True)
        TRI = ob("TRI", [128, 128], F32)
        ONES = ob("ONES", [128, 128], F32)
        MNEG = ob("MNEG", [128, 128], F32)
        cstb = Buf()
        c.op("pool", lambda e: e.memset(ONES[:], 1.0), writes=[cstb])
        c.op("pool", lambda e: e.affine_select(out=TRI[:], in_=ONES[:], pattern=[[1, 128]], compare_op=ALU.is_ge, fill=0.0,
                                               base=0, channel_multiplier=-1), reads=[cstb], writes=[cstb])
        c.op("pool", lambda e: e.memset(MNEG[:], 0.0), reads=[cstb], writes=[cstb])
        c.op("pool", lambda e: e.affine_select(out=MNEG[:], in_=MNEG[:], pattern=[[1, 128]], compare_op=ALU.is_ge, fill=NEG,
                                               base=0, channel_multiplier=-1), reads=[cstb], writes=[cstb])
        prm = ob("prm", [128, 24, 8], F32)
        prmb = Buf()
        for k in range(4):
            c.dma("sp", "ld0", prm[:, :, k:k + 1], dr["ssm_conv_w"][j][k].rearrange("(c p o) -> p c o", p=128, o=1), writes=[prmb], **kw)
        c.dma("sp", "ld0", prm[:, :, 4:5], dr["ssm_conv_b"][j].rearrange("(c p o) -> p c o", p=128, o=1), writes=[prmb], **kw)
        GN = ob("GN", [128, 16], F32)
        c.dma("sp", "ld0", GN[:].unsqueeze(2), dr["ssm_norm"][j].rearrange("(c p o) -> p c o", p=128, o=1), writes=[prmb], **kw)
        hp = ob("hp", [128, 4, 32], F32)
        hpb = Buf()
        for idx, nm in ((0, "ssm_dt_bias"), (1, "ssm_a_log"), (2, "ssm_d")):
            c.dma("sp", "ld1", hp[:, idx, :], dr[nm][j].partition_broadcast(128), writes=[hpb])
        c.op("act", lambda e: e.activation(out=hp[:, 1, :], in_=hp[:, 1, :], func=AF.Exp), reads=[hpb], writes=[hpb])
        c.op("dve", lambda e: e.tensor_scalar(out=hp[:, 1, :], in0=hp[:, 1, :], scalar1=-1.0, scalar2=None, op0=ALU.mult),
             reads=[hpb], writes=[hpb])
        fam = {n: ob(n, [128, NT, 32], F32) for n in ("DT", "ADT", "ACUM", "NACUM", "EA", "DEC", "CD")}
        famb = Buf()
        SS = ob("SS", [128, NT, 4], F32)
        SSb = Buf()
        wdt, wdtb = self.wload_cols(win, [(5120, 32)])
        for tt in range(NT):
            for dc in range(8):
                c.op("pe", lambda e: e.matmul(self.ps[:, tt * 32:(tt + 1) * 32], lhsT=self.hnT[:, dc, tt * 128:(tt + 1) * 128],
                                              rhs=wdt[:, dc, 0:32], start=(dc == 0), stop=(dc == 7)),
                     reads=[wdtb, self.hnTb[tt]], writes=[PB[0]])
        f3 = lambda t: t[:]
        flat = lambda t: t[:].rearrange("p a b -> p (a b)")
        bc_h = lambda ap2: ap2.unsqueeze(1).to_broadcast([128, NT, 32])
        DT, ADT, ACUM, NACUM, EA, DEC, CD = (fam[n] for n in ("DT", "ADT", "ACUM", "NACUM", "EA", "DEC", "CD"))
        c.op("dve", lambda e: e.tensor_tensor(out=ADT[:], in0=self.bank(0).rearrange("p (a b) -> p a b", a=NT), in1=bc_h(hp[:, 0, :]),
                                              op=ALU.add), reads=[PB[0], hpb], writes=[famb])
        c.op("act", lambda e: e.activation(out=flat(EA), in_=flat(ADT), func=AF.Abs), reads=[famb], writes=[famb])
        c.op("act", lambda e: e.activation(out=flat(EA), in_=flat(EA), func=AF.Exp, scale=-1.0), reads=[famb], writes=[famb])
        c.op("act", lambda e: e.activation(out=flat(EA), in_=flat(EA), func=AF.Ln, bias=1.0), reads=[famb], writes=[famb])
        c.op("dve", lambda e: e.scalar_tensor_tensor(out=flat(DT), in0=flat(ADT), scalar=0.0, in1=flat(EA), op0=ALU.max, op1=ALU.add),
             reads=[famb], writes=[famb])
        c.op("dve", lambda e: e.tensor_tensor(out=ADT[:], in0=DT[:], in1=bc_h(hp[:, 1, :]), op=ALU.mult), reads=[famb, hpb], writes=[famb])
        c.op("pe", lambda e: e.matmul(self.bank(1), lhsT=TRI[:], rhs=flat(ADT), start=True, stop=True), reads=[cstb, famb], writes=[PB[1]])
        c.op("pe", lambda e: e.matmul(self.bank(2), lhsT=ONES[:], rhs=flat(ADT), start=True, stop=True), reads=[cstb, famb], writes=[PB[2]])
        c.op("act", lambda e: e.activation(out=flat(ACUM), in_=self.bank(1), func=AF.Copy), reads=[PB[1]], writes=[famb])
        c.op("act", lambda e: e.activation(out=flat(EA), in_=self.bank(1), func=AF.Exp), reads=[PB[1]], writes=[famb])
        c.op("act", lambda e: e.activation(out=flat(CD), in_=self.bank(2), func=AF.Exp), reads=[PB[2]], writes=[famb])
        c.op("dve", lambda e: e.tensor_scalar(out=flat(NACUM), in0=flat(ACUM), scalar1=-1.0, scalar2=None, op0=ALU.mult),
             reads=[famb], writes=[famb])
        c.op("dve", lambda e: e.tensor_tensor(out=flat(DEC), in0=self.bank(2), in1=flat(ACUM), op=ALU.subtract),
             reads=[PB[2], famb], writes=[famb])
        c.op("act", lambda e: e.activation(out=flat(DEC), in_=flat(DEC), func=AF.Exp), reads=[famb], writes=[famb])
        c.barrier()
        with ExitStack() as ges:
            gb_ = lambda name, shape, dt: ges.enter_context(nc.sbuf_tensor(name + "_o%d" % j, list(shape), dt))
            BT = gb_("BT", [128, S], BF16)
            CT = gb_("CT", [128, S], BF16)
            BCb = [Buf(), Buf()]
            XS = gb_("XS", [128, NT, 512], BF16)
            XSb = Buf()
            STG = gb_("STG", [128, 3 + S], F32)
            STGb = Buf()
            TC = gb_("TC", [128, 1024], F32)
            TCb = Buf()
            XSF = gb_("XSF", [128, 1024], BF16)
            XSFb = Buf()
            STATE = gb_("STATE", [128, 8, 64], F32)
            STb = Buf()
            STATEb = gb_("STATEb", [128, 512], BF16)
            STbb = Buf()
            LA = gb_("LA", [128, 8, 128], BF16)
            LAb = Buf()
            MT = gb_("MT", [128, 8, 128], BF16)
            MTb = Buf()
            CBm = gb_("CBm", [128, 128], BF16)
            CBmb = Buf()
            BTOK = gb_("BTOK", [128, 128], BF16)
            BTOKb = Buf()
            RH = [gb_("RH%d" % i, [128, 128], F32) for i in range(2)]
            RHb = [Buf(), Buf()]
            XDT = gb_("XDT", [128, 8, 64], BF16)
            XDTb = Buf()
            XDD = gb_("XDD", [128, 8, 64], BF16)
            XDDb = Buf()
            SZ = gb_("SZ", [128, 512], F32)
            SZb = Buf()
            W1 = gb_("W1", [128, 8, 64], F32)
            W1b = Buf()
            W2 = gb_("W2", [128, 8, 64], F32)
            W2b = Buf()
            YG = [gb_("YG%d" % i, [128, 512], BF16) for i in range(2)]
            YGb = [Buf(), Buf()]
            junk = gb_("junko", [128, 512], BF16)
            junkb = Buf()
            c.op("pool", lambda e: e.memset(STG[:, 0:3], 0.0), writes=[STGb])
            ygi = 0
            for g in range(4):
                wbc = self.wload_cols(win, [(4096 + g * 128, 128), (4608 + g * 128, 128)])
                wxs = [self.wload_cols(win, [(2048 + g * 512 + i * 256, 256)]) for i in range(2)]
                hs = slice(8 * g, 8 * g + 8)
                for fc in range(6):
                    if fc < 2:
                        wt, wb, co, q = wbc[0], wbc[1], fc * 128, 16 + 4 * fc + g
                    else:
                        jx = fc - 2
                        wt, wb, co, q = wxs[jx // 2][0], wxs[jx // 2][1], (jx % 2) * 128, g * 4 + jx
                    for hf in range(2):
                        t0 = hf * 1024
                        for qq in range(2):
                            for dc in range(8):
                                c.op("pe", lambda e: e.matmul(self.bank(qq), lhsT=wt[:, dc, co:co + 128],
                                                              rhs=self.hnT[:, dc, t0 + qq * 512:t0 + (qq + 1) * 512],
                                                              start=(dc == 0), stop=(dc == 7)),
                                     reads=[wb] + self.hnTb[hf * 8 + qq * 4:hf * 8 + qq * 4 + 4], writes=[PB[qq]])
                        c.op("act", lambda e: e.activation(out=STG[:, 3 + t0:3 + t0 + 1024], in_=self.ps[:, 0:1024], func=AF.Copy),
                             reads=[PB[0], PB[1]], writes=[STGb])
                        c.op("dve", lambda e: e.tensor_scalar(out=TC[:], in0=STG[:, 3 + t0:3 + t0 + 1024], scalar1=prm[:, q, 3:4],
                                                              scalar2=prm[:, q, 4:5], op0=ALU.mult, op1=ALU.add),
                             reads=[STGb, prmb], writes=[TCb])
                        for k in range(3):
                            c.op("dve", lambda e: e.scalar_tensor_tensor(out=TC[:], in0=STG[:, k + t0:k + t0 + 1024],
                                                                         scalar=prm[:, q, k:k + 1], in1=TC[:], op0=ALU.mult,
                                                                         op1=ALU.add), reads=[STGb, prmb, TCb], writes=[TCb])
                        if fc < 2:
                            dst = (BT, CT)[fc]
                            c.op("act", lambda e: e.activation(out=dst[:, t0:t0 + 1024], in_=TC[:], func=AF.Silu),
                                 reads=[TCb], writes=[BCb[fc]])
                        else:
                            c.op("act", lambda e: e.activation(out=XSF[:], in_=TC[:], func=AF.Silu), reads=[TCb], writes=[XSFb])
                            tbk = 2 + (fc * 2 + hf) % 2
                            for k in range(8):
                                c.op("pe", lambda e: e.transpose(out=self.bank_bf(tbk)[:, k * 128:(k + 1) * 128],
                                                                 in_=XSF[:, k * 128:(k + 1) * 128], identity=self.ident[:]),
                                     reads=[XSFb, self.identb], writes=[PB[tbk]])
                            c.op("act", lambda e: e.activation(out=XS[:, hf * 8:hf * 8 + 8, (fc - 2) * 128:(fc - 1) * 128],
                                                               in_=self.bank_bf(tbk).rearrange("p (a b) -> p a b", a=8), func=AF.Copy),
                                 reads=[PB[tbk]], writes=[XSb])
                wz = [self.wload_cols(win, [(g * 512 + i * 256, 256)]) for i in range(2)]
                c.op("pool", lambda e: e.memset(STATE[:], 0.0), reads=[STb], writes=[STb])
                c.op("pool", lambda e: e.memset(STATEb[:], 0.0), reads=[STbb], writes=[STbb])
                for tt in range(NT):
                    tsl = slice(tt * 128, (tt + 1) * 128)
                    c.op("pe", lambda e: e.matmul(self.ps[:, 4 * 512:4 * 512 + 128], lhsT=BT[:, tsl], rhs=CT[:, tsl], start=True, stop=True),
                         reads=BCb, writes=[PB[4]])
                    c.op("act", lambda e: e.activation(out=CBm[:], in_=self.ps[:, 4 * 512:4 * 512 + 128], func=AF.Copy),
                         reads=[PB[4]], writes=[CBmb])
                    c.op("pe", lambda e: e.transpose(out=self.bank_bf(5)[:, 0:128], in_=BT[:, tsl], identity=self.ident[:]),
                         reads=[BCb[0], self.identb], writes=[PB[5]])
                    c.op("act", lambda e: e.activation(out=BTOK[:], in_=self.bank_bf(5)[:, 0:128], func=AF.Copy),
                         reads=[PB[5]], writes=[BTOKb])
                    for h in range(8):
                        hh = 8 * g + h
                        rh = RH[h % 2]
                        bk = 6 + (h // 4) % 2
                        reg = self.ps[:, bk * 512 + (h % 4) * 128: bk * 512 + (h % 4 + 1) * 128]
                        c.op("dve", lambda e: e.tensor_scalar(out=rh[:], in0=TRI[:], scalar1=ADT[:, tt, hh:hh + 1], scalar2=None,
                                                              op0=ALU.mult), reads=[cstb, famb], writes=[RHb[h % 2]])
                        c.op("pe", lambda e: e.matmul(reg, lhsT=ONES[:], rhs=rh[:], start=True, stop=False),
                             reads=[cstb, RHb[h % 2]], writes=[PB[bk]])
                        c.op("pe", lambda e: e.matmul(reg, lhsT=self.identf[:], rhs=MNEG[:], start=False, stop=True),
                             reads=[cstb, self.identfb], writes=[PB[bk]])
                        c.op("act", lambda e: e.activation(out=LA[:, h, :], in_=reg, func=AF.Exp, bias=NACUM[:, tt, hh:hh + 1]),
                             reads=[PB[bk], famb], writes=[LAb])
                    c.op("dve", lambda e: e.tensor_tensor(out=MT[:], in0=LA[:], in1=CBm[:].unsqueeze(1).to_broadcast([128, 8, 128]),
                                                          op=ALU.mult), reads=[LAb, CBmb], writes=[MTb])
                    xs3 = XS[:, tt, :].rearrange("p (a b) -> p a b", a=8)
                    bc_p = lambda ap2: ap2.unsqueeze(2).to_broadcast([128, 8, 64])
                    c.op("dve", lambda e: e.tensor_tensor(out=XDT[:], in0=xs3, in1=bc_p(DT[:, tt, hs]), op=ALU.mult),
                         reads=[XSb, famb], writes=[XDTb])
                    c.op("dve", lambda e: e.tensor_tensor(out=XDD[:], in0=XDT[:], in1=bc_p(DEC[:, tt, hs]), op=ALU.mult),
                         reads=[XDTb, famb], writes=[XDDb])
                    for h in range(8):
                        c.op("pe", lambda e: e.matmul(self.ps[:, 2 * 512 + h * 64: 2 * 512 + (h + 1) * 64], lhsT=MT[:, h, :],
                                                      rhs=XDT[:, h, :], start=True, stop=True), reads=[MTb, XDTb], writes=[PB[2]])
                    c.op("pe", lambda e: e.matmul(self.bank(3), lhsT=CT[:, tsl], rhs=STATEb[:], start=True, stop=True),
                         reads=[BCb[1], STbb], writes=[PB[3]])
                    c.op("pe", lambda e: e.matmul(self.bank(0), lhsT=BTOK[:], rhs=XDD[:].rearrange("p a b -> p (a b)"), start=True, stop=True),
                         reads=[BTOKb, XDDb], writes=[PB[0]])
                    for i in range(2):
                        for dc in range(8):
                            c.op("pe", lambda e: e.matmul(self.ps[:, 512 + i * 256:512 + (i + 1) * 256], lhsT=self.hnT[:, dc, tsl],
                                                          rhs=wz[i][0][:, dc, :], start=(dc == 0), stop=(dc == 7)),
                                 reads=[wz[i][1], self.hnTb[tt]], writes=[PB[1]])
                    c.op("act", lambda e: e.activation(out=SZ[:], in_=self.bank(1), func=AF.Silu), reads=[PB[1]], writes=[SZb])
                    c.op("dve", lambda e: e.tensor_tensor(out=W1[:], in0=self.bank(3).rearrange("p (a b) -> p a b", a=8),
                                                          in1=bc_p(EA[:, tt, hs]), op=ALU.mult), reads=[PB[3], famb], writes=[W1b])
                    c.op("dve", lambda e: e.tensor_tensor(out=W1[:], in0=self.bank(2).rearrange("p (a b) -> p a b", a=8), in1=W1[:],
                                                          op=ALU.add), reads=[PB[2], W1b], writes=[W1b])
                    c.op("dve", lambda e: e.tensor_tensor(out=W2[:], in0=xs3, in1=bc_p(hp[:, 2, hs]), op=ALU.mult),
                         reads=[XSb, hpb], writes=[W2b])
                    c.op("pool", lambda e: e.tensor_tensor(out=W1[:], in0=W1[:], in1=W2[:], op=ALU.add), reads=[W1b, W2b], writes=[W1b])
                    yg = YG[ygi % 2]
                    c.op("dve", lambda e: e.tensor_tensor(out=yg[:], in0=W1[:].rearrange("p a b -> p (a b)"), in1=SZ[:], op=ALU.mult),
                         reads=[W1b, SZb], writes=[YGb[ygi % 2]])
                    c.op("act", lambda e: e.activation(out=junk[:], in_=yg[:], func=AF.Square, accum_out=SS[:, tt, g:g + 1]),
                         reads=[YGb[ygi % 2]], writes=[junkb, SSb])
                    c.dma("sp", "ld2" if ygi % 2 == 0 else "ld3", ygs[tt * 128:(tt + 1) * 128, g * 512:(g + 1) * 512], yg[:],
                          reads=[YGb[ygi % 2]])
                    ygi += 1
                    c.op("dve", lambda e: e.tensor_tensor(out=STATE[:], in0=STATE[:], in1=bc_p(CD[:, tt, hs]), op=ALU.mult),
                         reads=[STb, famb], writes=[STb])
                    c.op("dve", lambda e: e.tensor_tensor(out=STATE[:], in0=self.bank(0).rearrange("p (a b) -> p a b", a=8), in1=STATE[:],
                                                          op=ALU.add), reads=[PB[0], STb], writes=[STb])
                    c.op("act", lambda e: e.activation(out=STATEb[:], in_=STATE[:].rearrange("p a b -> p (a b)"), func=AF.Copy),
                         reads=[STb], writes=[STbb])
            c.barrier()
        RS = ob("RS", [128, 3, NT], F32)
        RSb = Buf()
        c.op("dve", lambda e: e.tensor_reduce(out=RS[:, 0, :], in_=SS[:], axis=AX.X, op=ALU.add), reads=[SSb], writes=[RSb])
        c.op("act", lambda e: e.activation(out=RS[:, 1, :], in_=RS[:, 0, :], func=AF.Sqrt, bias=EPS, scale=1.0 / 2048), reads=[RSb], writes=[RSb])
        c.op("dve", lambda e: e.reciprocal(out=RS[:, 2, :], in_=RS[:, 1, :]), reads=[RSb], writes=[RSb])
        WO = ob("WO", [128, 16, D], BF16)
        WOb = Buf()
        for r in range(8):
            c.dma("pool", "ld4", WO[:, 2 * r:2 * r + 2, :], wout[r * 256:(r + 1) * 256, :].rearrange("(a p) d -> p a d", p=128), writes=[WOb])
        YGF = [ob("YGF%d" % i, [128, 2048], BF16) for i in range(2)]
        YGFb = [Buf(), Buf()]
        YGT = [ob("YGT%d" % i, [128, 16, 128], BF16) for i in range(2)]
        YGTb = [Buf(), Buf()]
        for tt in range(NT):
            k = tt % 2
            c.dma("sp", "ld2" if k == 0 else "ld3", YGF[k][:], ygs[tt * 128:(tt + 1) * 128, :], writes=[YGFb[k]])
            b0 = 4 + 2 * k
            for ch in range(16):
                bk = b0 + ch // 8
                c.op("pe", lambda e: e.transpose(out=self.bank_bf(bk)[:, (ch % 8) * 128:(ch % 8 + 1) * 128],
                                                 in_=YGF[k][:, ch * 128:(ch + 1) * 128], identity=self.ident[:]),
                     reads=[YGFb[k], self.identb], writes=[PB[bk]])
            src = self.ps[:, b0 * 512:(b0 + 2) * 512].bitcast(BF16).rearrange("p (a b) -> p a b", a=16)
            c.op("dve", lambda e: e.tensor_tensor(out=YGT[k][:], in0=src, in1=GN[:].unsqueeze(2).to_broadcast([128, 16, 128]), op=ALU.mult),
                 reads=[PB[b0], PB[b0 + 1], prmb], writes=[YGTb[k]])
            for dh in range(2):
                bk = 2 * k + dh
                for ch in range(16):
                    c.op("pe", lambda e: e.matmul(self.bank(bk), lhsT=YGT[k][:, ch, :], rhs=WO[:, ch, dh * 512:(dh + 1) * 512],
                                                  start=(ch == 0), stop=(ch == 15)), reads=[YGTb[k], WOb], writes=[PB[bk]])
                xs_ = self.xres[:, tt, dh * 512:(dh + 1) * 512]
                c.op("dve", lambda e: e.scalar_tensor_tensor(out=xs_, in0=self.bank(bk), scalar=RS[:, 2, tt:tt + 1], in1=xs_,
                                                             op0=ALU.mult, op1=ALU.add),
                     reads=[PB[bk], RSb, self.xb[tt]], writes=[self.xb[tt]])
        c.barrier()
        es.close()

    def final_norm(self):
        c = self.c
        nes = self.norm_open()
        self.es.enter_context(nes)
        self.load_gain(self.dr["final_norm"])
        self.norm_stats()
        for tt in range(NT):
            c.op("dve", lambda e: e.scalar_tensor_tensor(out=self.xres[:, tt, :], in0=self.xres[:, tt, :],
                                                         scalar=self.stat[:, 16 + tt:17 + tt], in1=self.gbc[:], op0=ALU.mult,
                                                         op1=ALU.mult), reads=[self.xb[tt], self.statb, self.gbcb], writes=[self.xb[tt]])


LAST_INPUT_NAMES = []


def _build_once(stages, needed):
    nc = bass.Bass("TRN2", target_bir_lowering=False)
    dram = {}

    def din(name, shape):
        dram[name] = nc.dram_tensor(name, list(shape), F32, kind="ExternalInput").ap()
        if name not in LAST_INPUT_NAMES:
            LAST_INPUT_NAMES.append(name)

    din("x", [S, D])
    din("ffn_norm", [2, D])
    din("mix_norm", [2, D])
    din("final_norm", [D])
    din("even_w_in", [1, D, 6144])
    din("lru_conv_w", [1, 4, D])
    din("lru_conv_b", [1, D])
    din("lru_w_a", [1, 16, 64, 64])
    din("lru_b_a", [1, D])
    din("lru_w_i", [1, 16, 64, 64])
    din("lru_b_i", [1, D])
    din("lru_lambda", [1, D])
    din("even_w_out", [1, 2048, D])
    din("ssm_w_in", [1, D, 5152])
    din("ssm_conv_w", [1, 4, 3072])
    din("ssm_conv_b", [1, 3072])
    din("ssm_dt_bias", [1, 32])
    din("ssm_a_log", [1, 32])
    din("ssm_d", [1, 32])
    din("ssm_norm", [1, 2048])
    din("ssm_w_out", [1, 2048, D])
    dram["ygs"] = nc.dram_tensor("ygs", [S, 2048], BF16, kind="Internal").ap()
    din("peer_w_q", [2, D, D])
    din("peer_sub_keys", [2, 8, 2, 128, 64])
    din("peer_u", [2, 16384, D])
    din("peer_v", [2, 16384, D])
    dram["y"] = nc.dram_tensor("y", [S, D], F32, kind="ExternalOutput").ap()
    with ExitStack() as es:
        k = Kern(nc, es, dram, needed)
        k.load_x()
        for st in stages:
            if st == "peer0":
                k.peer(0)
            elif st == "peer1":
                k.peer(1)
            elif st == "odd":
                k.odd_mixer(0)
            elif st == "even":
                k.even_mixer(0)
            elif st == "final":
                k.final_norm()
        k.store_x("y")
    return nc, k.c.waited


def build(stages=("peer0",), out_name="y"):
    _, waited = _build_once(stages, None)
    nc, waited2 = _build_once(stages, waited)
    assert waited == waited2
    return nc


ALL_STAGES = ("even", "peer0", "odd", "peer1", "final")


def kernel(**inputs):
    nc = build(ALL_STAGES)
    names = [n for n in LAST_INPUT_NAMES if n != "x"]
    shared = {n: np.ascontiguousarray(np.asarray(inputs[n], dtype=np.float32)) for n in names}
    x = np.asarray(inputs["x"], dtype=np.float32)
    nb = x.shape[0]
    maps = [dict(shared, x=np.ascontiguousarray(x[b])) for b in range(nb)]
    res = run_bass_kernel_spmd(nc, maps, core_ids=list(range(nb)))
    return np.stack([np.asarray(res.results[b]["y"], dtype=np.float32) for b in range(nb)], axis=0)
```

```python
import numpy as np
from contextlib import ExitStack
import concourse.bass as bass
import concourse.mybir as mybir
from concourse.bass_types import AP
from concourse.bass_utils import run_bass_kernel_spmd

F32 = mybir.dt.float32
BF16 = mybir.dt.bfloat16
AF = mybir.ActivationFunctionType
ALU = mybir.AluOpType
AX = mybir.AxisListType

S = 2048
D = 1024
NT = 16
EPS = 1e-6
NEG = -1.0e30
PEER_NEB = 32
NORM_SCOPED = True
PEER_MAXSTEPS = 10 ** 9
NORM_BARRIER = False


class Buf:
    __slots__ = ("w", "r", "name")

    def __init__(self, name=""):
        self.w = None
        self.r = {}
        self.name = name


class Ctx:
    def __init__(self, nc, es, needed=None):
        self.nc = nc
        self.es = es
        self.eng = {"pe": nc.tensor, "act": nc.scalar, "dve": nc.vector, "pool": nc.gpsimd, "sp": nc.sync}
        self.semobj = {}
        self.val = {}
        self.real = {}
        self.v2r = {}
        self.needed = needed
        self.waited = set()
        self.seen = {k: {} for k in self.eng}
        for k in self.eng:
            self.newsem(k)

    def newsem(self, name):
        self.semobj[name] = self.es.enter_context(self.nc.semaphore("s_" + name))
        self.val[name] = 0
        self.real[name] = 0
        return name

    def sbuf(self, name, shape, dt):
        return self.es.enter_context(self.nc.sbuf_tensor(name, list(shape), dt))

    def _wait(self, e, s, v):
        self.waited.add((s, v))
        if self.needed is None or s not in self.eng:
            rv = v
        else:
            rv = self.v2r[(s, v)]
        self.eng[e].wait_ge(self.semobj[s], rv)
        self.seen[e][s] = v

    def _deps(self, e, reads, writes):
        need = {}

        def add(s, v):
            if s == "pe" and e == "pe":
                return
            if need.get(s, 0) < v:
                need[s] = v

        for b in reads:
            if b.w is not None:
                add(*b.w)
        for b in writes:
            if b.w is not None:
                add(*b.w)
            for s, v in b.r.items():
                add(s, v)
        for s, v in need.items():
            if self.seen[e].get(s, 0) < v:
                self._wait(e, s, v)

    def op(self, e, fn, reads=(), writes=()):
        self._deps(e, reads, writes)
        self.val[e] += 1
        v = self.val[e]
        ins = fn(self.eng[e])
        if self.needed is None:
            ins.then_inc(self.semobj[e], 1)
        elif (e, v) in self.needed:
            self.real[e] += 1
            self.v2r[(e, v)] = self.real[e]
            ins.then_inc(self.semobj[e], 1)
        for b in reads:
            if b.r.get(e, 0) < v:
                b.r[e] = v
        for b in writes:
            b.w = (e, v)
            b.r = {}

    def dma(self, q, sem, out, in_, reads=(), writes=(), **kw):
        sem = q + "_" + sem
        if sem not in self.semobj:
            self.newsem(sem)
        self._deps(q, reads, writes)
        self.val[sem] += 16
        v = self.val[sem]
        self.eng[q].dma_start(out=out, in_=in_, **kw).then_inc(self.semobj[sem], 16)
        for b in reads:
            b.r[sem] = v
        for b in writes:
            b.w = (sem, v)
            b.r = {}

    def barrier(self, engines=None):
        for e in (engines or self.eng):
            for s, v in self.val.items():
                if s == e or v == 0:
                    continue
                if self.seen[e].get(s, 0) < v:
                    self._wait(e, s, v)


def bcast_last(ap2d, n):
    return ap2d.unsqueeze(2).broadcast_to([ap2d.shape[0], ap2d.shape[1], n])


class Kern:
    def __init__(self, nc, es, dram, needed=None):
        self.nc = nc
        self.c = Ctx(nc, es, needed)
        self.es = es
        self.dr = dram
        c = self.c
        self.xres = c.sbuf("xres", [128, NT, D], F32)
        self.xb = [Buf("x%d" % i) for i in range(NT)]
        self.hnT = c.sbuf("hnT", [128, 8, S], BF16)
        self.hnTb = [Buf("hnT%d" % i) for i in range(NT)]
        self.ident = c.sbuf("ident", [128, 128], BF16)
        self.identb = Buf("ident")
        self.ps = es.enter_context(nc.psum_tensor("ps", [128, 4096], F32))
        self.pb = [Buf("bank%d" % i) for i in range(8)]
        self.stat = c.sbuf("stat", [128, 64], F32)
        self.statb = Buf("stat")
        self.nscope = 0
        for n in ("sp_io", "sp_ld0", "sp_ld1", "sp_ld2", "sp_ld3", "sp_ld5", "pool_ld0", "pool_ld1", "pool_ld2", "pool_ld3",
                  "pool_ld4", "pool_rg0", "pool_rg1", "pool_rg2", "pool_rg3", "pool_rg4", "pool_rg5"):
            c.newsem(n)
        self.make_ident()

    def bank(self, i, n=512):
        return self.ps[:, i * 512:i * 512 + n]

    def bank_bf(self, i):
        return self.ps[:, i * 512:(i + 1) * 512].bitcast(BF16)

    def make_ident(self):
        c = self.c
        tmp = c.sbuf("identf", [128, 128], F32)
        tb = Buf()
        c.op("pool", lambda e: e.memset(tmp[:], 1.0), writes=[tb])
        c.op("pool", lambda e: e.affine_select(out=tmp[:], in_=tmp[:], pattern=[[1, 128]], compare_op=ALU.is_equal,
                                               fill=0.0, base=0, channel_multiplier=-1), reads=[tb], writes=[tb])
        c.op("dve", lambda e: e.tensor_copy(out=self.ident[:], in_=tmp[:]), reads=[tb], writes=[self.identb])
        self.identf = tmp
        self.identfb = tb

    def load_x(self):
        c = self.c
        x = self.dr["x"]
        for tt in range(NT):
            c.dma("sp", "io", self.xres[:, tt, :], x[tt * 128:(tt + 1) * 128, :], writes=[self.xb[tt]])
        for tt in range(NT):
            self.xb[tt].w = ("sp_io", c.val["sp_io"])

    def store_x(self, name="y"):
        c = self.c
        y = self.dr[name]
        for tt in range(NT):
            c.dma("sp", "io", y[tt * 128:(tt + 1) * 128, :], self.xres[:, tt, :], reads=[self.xb[tt]])
        c._wait("sp", "sp_io", c.val["sp_io"])

    def norm_open(self):
        es = ExitStack()
        self.nscope += 1
        t = "_n%d" % self.nscope
        self.gbc = es.enter_context(self.nc.sbuf_tensor("gbc" + t, [128, D], F32))
        self.gbcb = Buf("gbc")
        self.hn = [es.enter_context(self.nc.sbuf_tensor("hn%d%s" % (i, t), [128, D], BF16)) for i in range(2)]
        self.hnb = [Buf("hn%d" % i) for i in range(2)]
        return es

    def norm_stats(self):
        c = self.c
        junk = self.hn[0]
        for tt in range(NT):
            c.op("act", lambda e, tt=tt: e.activation(out=junk[:], in_=self.xres[:, tt, :], func=AF.Square,
                                                     accum_out=self.stat[:, tt:tt + 1]),
                 reads=[self.xb[tt]], writes=[self.hnb[0], self.statb])
        c.op("act", lambda e: e.activation(out=self.stat[:, 32:48], in_=self.stat[:, 0:16], func=AF.Sqrt, bias=EPS,
                                           scale=1.0 / D), reads=[self.statb], writes=[self.statb])
        c.op("dve", lambda e: e.reciprocal(out=self.stat[:, 16:32], in_=self.stat[:, 32:48]), reads=[self.statb],
             writes=[self.statb])

    def load_gain(self, gain_ap):
        self.c.dma("sp", "ld5", self.gbc[:], gain_ap.partition_broadcast(128), writes=[self.gbcb])

    def norm_to_hnT(self, gain_ap, free=False):
        c = self.c
        nes = self.norm_open()
        self.load_gain(gain_ap)
        self.norm_stats()
        for tt in range(NT):
            k = tt % 2
            hn = self.hn[k]
            c.op("dve", lambda e, tt=tt, hn=hn: e.scalar_tensor_tensor(out=hn[:], in0=self.xres[:, tt, :],
                                                                      scalar=self.stat[:, 16 + tt:17 + tt],
                                                                      in1=self.gbc[:], op0=ALU.mult, op1=ALU.mult),
                 reads=[self.xb[tt], self.statb, self.gbcb], writes=[self.hnb[k]])
            bk = 6 + k
            for dc in range(8):
                c.op("pe", lambda e, dc=dc, hn=hn, bk=bk: e.transpose(out=self.bank_bf(bk)[:, dc * 128:(dc + 1) * 128],
                                                                     in_=hn[:, dc * 128:(dc + 1) * 128],
                                                                     identity=self.ident[:]),
                     reads=[self.hnb[k], self.identb], writes=[self.pb[bk]])
            src = self.bank_bf(bk).rearrange("p (a b) -> p a b", a=8)
            c.op("act", lambda e, tt=tt, src=src: e.activation(out=self.hnT[:, :, tt * 128:(tt + 1) * 128], in_=src,
                                                              func=AF.Copy),
                 reads=[self.pb[bk]], writes=[self.hnTb[tt]])
        if free:
            nes.close()
        else:
            return nes
        return None

    def peer(self, l):
        c = self.c
        nc = self.nc
        es = ExitStack()
        self.es.enter_context(es)
        sb = lambda name, shape, dt: es.enter_context(nc.sbuf_tensor(name + "_%d" % l, list(shape), dt))
        dr = self.dr
        self.norm_to_hnT(dr["ffn_norm"][l], free=True)

        qT = sb("qT", [128, 8, S], BF16)
        qTb = [Buf() for _ in range(8)]
        thr = sb("thr", [128, NT, 8], F32)
        negc = sb("negc", [128, NT, 8], F32)
        thrb = [Buf() for _ in range(NT)]
        sk1T = sb("sk1T", [64, 8, 128], BF16)
        sk1Tb = Buf()
        KX = [sb("KX%d" % i, [128, 8, 512], BF16) for i in range(2)]
        KXlo = [Buf() for _ in range(2)]
        KXhi = Buf()

        with ExitStack() as pes:
            pb_ = lambda name, shape, dt: pes.enter_context(nc.sbuf_tensor(name + "_%d" % l, list(shape), dt))
            Wq = pb_("Wq", [128, 8, D], BF16)
            Wqb = Buf()
            c.dma("pool", "ld0", Wq[:], dr["peer_w_q"][l].rearrange("(dc p) f -> p dc f", p=128), writes=[Wqb])
            skn = pb_("skn", [128, 16, 64], BF16)
            sknb = Buf()
            c.dma("pool", "ld1", skn[:], dr["peer_sub_keys"][l].rearrange("h p n k -> n (h p) k"), writes=[sknb])
            SKBD = pb_("SKBD", [128, 8, 256], BF16)
            SKBDb = Buf()
            c.op("pool", lambda e: e.memset(SKBD[:], 0.0), writes=[SKBDb])
            for h in range(8):
                bk = h % 2
                c.op("pe", lambda e, h=h, bk=bk: e.transpose(out=self.bank_bf(bk)[:, 0:128],
                                                            in_=skn[:, 2 * h:2 * h + 2, :].rearrange("p a b -> p (a b)"),
                                                            identity=self.ident[:]),
                     reads=[sknb, self.identb], writes=[self.pb[bk]])
                c.op("dve", lambda e, h=h, bk=bk: e.tensor_copy(out=SKBD[0:64, h, 0:128], in_=self.bank_bf(bk)[0:64, 0:128]),
                     reads=[self.pb[bk]], writes=[SKBDb])
                c.op("dve", lambda e, h=h, bk=bk: e.tensor_copy(out=SKBD[64:128, h, 128:256],
                                                               in_=self.bank_bf(bk)[64:128, 0:128]),
                     reads=[self.pb[bk]], writes=[SKBDb])
                c.op("act", lambda e, h=h, bk=bk: e.activation(out=sk1T[:, h, :], in_=self.bank_bf(bk)[0:64, 0:128],
                                                              func=AF.Copy),
                     reads=[self.pb[bk]], writes=[sk1Tb])
                for kb in range(2):
                    c.op("act", lambda e, h=h, bk=bk, kb=kb: e.activation(
                        out=KX[kb][64:128, h, :].rearrange("p (a b) -> p a b", a=4),
                        in_=self.bank_bf(bk)[64:128, 0:128].unsqueeze(1).to_broadcast([64, 4, 128]), func=AF.Copy),
                         reads=[self.pb[bk]], writes=[KXhi])
            for h in range(8):
                for tb in range(4):
                    bk = 2 + (h * 4 + tb) % 4
                    for dc in range(8):
                        c.op("pe", lambda e, h=h, tb=tb, dc=dc, bk=bk: e.matmul(
                            self.bank(bk), lhsT=Wq[:, dc, h * 128:(h + 1) * 128], rhs=self.hnT[:, dc, tb * 512:(tb + 1) * 512],
                            start=(dc == 0), stop=(dc == 7)),
                             reads=[Wqb] + self.hnTb[tb * 4:tb * 4 + 4], writes=[self.pb[bk]])
                    eng = "act" if tb % 2 == 0 else "dve"
                    if eng == "act":
                        c.op("act", lambda e, h=h, tb=tb, bk=bk: e.activation(out=qT[:, h, tb * 512:(tb + 1) * 512],
                                                                             in_=self.bank(bk), func=AF.Copy),
                             reads=[self.pb[bk]], writes=[qTb[h]])
                    else:
                        c.op("dve", lambda e, h=h, tb=tb, bk=bk: e.tensor_copy(out=qT[:, h, tb * 512:(tb + 1) * 512],
                                                                              in_=self.bank(bk)),
                             reads=[self.pb[bk]], writes=[qTb[h]])
            SC = [pb_("SC%d" % i, [128, 16, 128], F32) for i in range(2)]
            SCb = [Buf() for _ in range(2)]
            TOP = pb_("TOP", [128, 16, 24], F32)
            TOPb = Buf()
            CAND = pb_("CAND", [128, 8, 289], F32)
            CANDb = Buf()
            CT = pb_("CT", [128, 8, 24], F32)
            CTb = Buf()
            sm = pb_("sm", [128, 8, 16], F32)
            smb = Buf()
            for tt in range(NT):
                k = tt % 2
                sc = SC[k]
                for h in range(8):
                    bk = 2 * k + h // 4
                    bk = 4 * k + h // 2
                    c.op("pe", lambda e, h=h, tt=tt, bk=bk: e.matmul(
                        self.ps[:, bk * 512 + (h % 2) * 256: bk * 512 + (h % 2) * 256 + 256],
                        lhsT=qT[:, h, tt * 128:(tt + 1) * 128], rhs=SKBD[:, h, :], start=True, stop=True),
                         reads=[qTb[h], SKBDb], writes=[self.pb[bk]])
                for q in range(4):
                    bk = 4 * k + q
                    c.op("act", lambda e, q=q, bk=bk, sc=sc: e.activation(
                        out=sc[:, 4 * q:4 * q + 4, :].rearrange("p a b -> p (a b)"), in_=self.bank(bk), func=AF.Copy),
                         reads=[self.pb[bk]], writes=[SCb[k]])
                for g in range(16):
                    c.op("dve", lambda e, g=g, sc=sc: e.max(out=TOP[:, g, 0:8], in_=sc[:, g, :]), reads=[SCb[k]], writes=[TOPb])
                    c.op("dve", lambda e, g=g, sc=sc: e.match_replace(out=sc[:, g, :], in_to_replace=TOP[:, g, 0:8],
                                                                     in_values=sc[:, g, :], imm_value=NEG),
                         reads=[TOPb, SCb[k]], writes=[SCb[k]])
                    c.op("dve", lambda e, g=g, sc=sc: e.max(out=TOP[:, g, 8:16], in_=sc[:, g, :]), reads=[SCb[k]], writes=[TOPb])
                    c.op("dve", lambda e, g=g, sc=sc: e.match_replace(out=sc[:, g, :], in_to_replace=TOP[:, g, 8:16],
                                                                     in_values=sc[:, g, :], imm_value=NEG),
                         reads=[TOPb, SCb[k]], writes=[SCb[k]])
                c.op("dve", lambda e, sc=sc: e.tensor_reduce(out=TOP[:, :, 16:17], in_=sc[:], axis=AX.X, op=ALU.max),
                     reads=[SCb[k]], writes=[TOPb])
                in0 = AP(TOP, 0, [[384, 128], [48, 8], [1, 17], [0, 17]])
                in1 = AP(TOP, 24, [[384, 128], [48, 8], [0, 17], [1, 17]])
                c.op("dve", lambda e, in0=in0, in1=in1: e.tensor_tensor(
                    out=CAND[:].rearrange("p h (a b) -> p h a b", a=17), in0=in0, in1=in1, op=ALU.add),
                     reads=[TOPb], writes=[CANDb])
                for h in range(8):
                    c.op("dve", lambda e, h=h: e.max(out=CT[:, h, 0:8], in_=CAND[:, h, :]), reads=[CANDb], writes=[CTb])
                    c.op("dve", lambda e, h=h: e.match_replace(out=CAND[:, h, :], in_to_replace=CT[:, h, 0:8],
                                                              in_values=CAND[:, h, :], imm_value=NEG),
                         reads=[CTb, CANDb], writes=[CANDb])
                    c.op("dve", lambda e, h=h: e.max(out=CT[:, h, 8:16], in_=CAND[:, h, :]), reads=[CANDb], writes=[CTb])
                    c.op("dve", lambda e, h=h: e.match_replace(out=CAND[:, h, :], in_to_replace=CT[:, h, 8:16],
                                                              in_values=CAND[:, h, :], imm_value=NEG),
                         reads=[CTb, CANDb], writes=[CANDb])
                c.op("dve", lambda e: e.tensor_reduce(out=CT[:, :, 16:17], in_=CAND[:], axis=AX.X, op=ALU.max),
                     reads=[CANDb], writes=[CTb])
                c.op("dve", lambda e, tt=tt: e.tensor_tensor(out=thr[:, tt, :].unsqueeze(2), in0=CT[:, :, 15:16],
                                                            in1=CT[:, :, 16:17], op=ALU.add),
                     reads=[CTb], writes=[thrb[tt]])
                c.op("dve", lambda e, tt=tt: e.tensor_scalar(out=thr[:, tt, :], in0=thr[:, tt, :], scalar1=0.5,
                                                            scalar2=None, op0=ALU.mult),
                     reads=[thrb[tt]], writes=[thrb[tt]])
                c.op("dve", lambda e: e.tensor_tensor(out=sm[:], in0=CT[:, :, 0:16],
                                                      in1=CT[:, :, 0:1].to_broadcast([128, 8, 16]), op=ALU.subtract),
                     reads=[CTb], writes=[smb])
                c.op("act", lambda e: e.activation(out=sm[:], in_=sm[:], func=AF.Exp), reads=[smb], writes=[smb])
                c.op("dve", lambda e: e.tensor_reduce(out=CT[:, :, 17:18], in_=sm[:], axis=AX.X, op=ALU.add),
                     reads=[smb], writes=[CTb])
                c.op("act", lambda e: e.activation(out=CT[:, :, 18:19], in_=CT[:, :, 17:18], func=AF.Ln),
                     reads=[CTb], writes=[CTb])
                c.op("dve", lambda e, tt=tt: e.scalar_tensor_tensor(out=negc[:, tt, :].unsqueeze(2), in0=CT[:, :, 18:19],
                                                                   scalar=-1.0, in1=CT[:, :, 0:1], op0=ALU.mult,
                                                                   op1=ALU.subtract),
                     reads=[CTb], writes=[thrb[tt]])
            c.barrier()
        if PEER_NEB == 0:
            c.barrier()
            es.close()
            return
        u = dr["peer_u"][l]
        v = dr["peer_v"][l]
        Ubf = sb("Ubf", [128, 4, D], BF16)
        Ubfb = Buf()
        UT1 = sb("UT", [128, 8, 512], BF16)
        UT = [UT1, UT1]
        UTb1 = Buf()
        UTb = [UTb1, UTb1]
        Vb = [sb("Vb%d" % i, [128, 4, D], BF16) for i in range(2)]
        Vbb = [Buf() for _ in range(2)]
        GA = sb("GA", [128, NT, 512], BF16)
        GAb = [Buf() for _ in range(NT)]
        NS = 3
        E = [sb("E%d" % i, [128, 512], BF16) for i in range(2)]
        Eb = [Buf() for _ in range(2)]
        G = [sb("G%d" % i, [128, 512], BF16) for i in range(4)]
        Gb = [Buf() for _ in range(4)]
        PT = [sb("PT%d" % i, [128, 512], BF16) for i in range(2)]
        PTb = [Buf() for _ in range(2)]
        NEB = PEER_NEB
        NU = NEB * NT
        B_S = (0, 1, 2)
        B_AT = 3
        B_WT = (4, 5)
        B_O = (6, 7)

        def load_U(eb):
            c.dma("pool", "ld2", Ubf[:], u[eb * 512:(eb + 1) * 512, :].rearrange("(ec p) d -> p ec d", p=128),
                  writes=[Ubfb])

        def load_V(eb):
            c.dma("pool", "ld3" if eb % 2 == 0 else "ld4", Vb[eb % 2][:],
                  v[eb * 512:(eb + 1) * 512, :].rearrange("(ec p) d -> p ec d", p=128), writes=[Vbb[eb % 2]])

        def build_KX(eb):
            kx = KX[eb % 2]
            c.op("pool", lambda e: e.tensor_copy(
                out=kx[0:64, :, :].rearrange("p h (a b) -> p h a b", a=4),
                in_=sk1T[:, :, eb * 4:eb * 4 + 4].unsqueeze(3).broadcast_to([64, 8, 4, 128])),
                 reads=[sk1Tb], writes=[KXlo[eb % 2]])

        def transp_eb(eb):
            ut = UT[eb % 2]
            for dcp in range(4):
                for jj in range(2):
                    dc = dcp * 2 + jj
                    for ec in range(4):
                        c.op("pe", lambda e: e.transpose(
                            out=self.bank_bf(B_AT)[:, jj * 512 + ec * 128: jj * 512 + (ec + 1) * 128],
                            in_=Ubf[:, ec, dc * 128:(dc + 1) * 128], identity=self.ident[:]),
                             reads=[Ubfb, self.identb], writes=[self.pb[B_AT]])
                c.op("act", lambda e: e.activation(
                    out=ut[:, 2 * dcp:2 * dcp + 2, :].rearrange("p a b -> p (a b)"), in_=self.bank_bf(B_AT), func=AF.Copy),
                     reads=[self.pb[B_AT]], writes=[UTb[eb % 2]])

        def emit_AT(eb, tt, only=None):
            ut = UT[eb % 2]
            for ec in (range(4) if only is None else (only,)):
                for dc in range(8):
                    c.op("pe", lambda e: e.matmul(
                        self.ps[:, B_AT * 512 + ec * 128: B_AT * 512 + (ec + 1) * 128],
                        lhsT=ut[:, dc, ec * 128:(ec + 1) * 128], rhs=self.hnT[:, dc, tt * 128:(tt + 1) * 128],
                        start=(dc == 0), stop=(dc == 7)),
                         reads=[UTb[eb % 2], self.hnTb[tt]], writes=[self.pb[B_AT]])

        def emit_ATq(eb, tt, j):
            ut = UT[eb % 2]
            ec = j // 2
            for dc in range((j % 2) * 4, (j % 2) * 4 + 4):
                c.op("pe", lambda e: e.matmul(
                    self.ps[:, B_AT * 512 + ec * 128: B_AT * 512 + (ec + 1) * 128],
                    lhsT=ut[:, dc, ec * 128:(ec + 1) * 128], rhs=self.hnT[:, dc, tt * 128:(tt + 1) * 128],
                    start=(dc == 0), stop=(dc == 7)),
                     reads=[UTb[eb % 2], self.hnTb[tt]], writes=[self.pb[B_AT]])

        def emit_V1(uu, i):
            eb, tt = uu // NT, uu % NT
            vb = Vb[eb % 2]
            dh, ec = i // 4, i % 4
            c.op("pe", lambda e: e.matmul(self.bank(B_O[dh]), lhsT=PT[uu % 2][:, ec * 128:(ec + 1) * 128],
                                          rhs=vb[:, ec, dh * 512:(dh + 1) * 512], start=(ec == 0), stop=(ec == 3)),
                 reads=[PTb[uu % 2], Vbb[eb % 2]], writes=[self.pb[B_O[dh]]])

        def emit_Acopy(tt):
            c.op("act", lambda e: e.activation(out=GA[:, tt, :], in_=self.bank(B_AT), func=AF.Copy),
                 reads=[self.pb[B_AT]], writes=[GAb[tt]])

        def emit_gelu_batch():
            for tt in range(NT):
                c.op("act", lambda e: e.activation(out=GA[:, tt, :], in_=GA[:, tt, :], func=self.gelu_func),
                     reads=[GAb[tt]], writes=[GAb[tt]])

        def unit_of(n):
            uu = n // 8
            return uu, uu // NT, uu % NT, n % 8

        def emit_S(n):
            uu, eb, tt, h = unit_of(n)
            bk = B_S[n % NS]
            c.op("pe", lambda e: e.matmul(self.bank(bk), lhsT=qT[:, h, tt * 128:(tt + 1) * 128], rhs=KX[eb % 2][:, h, :],
                                          start=True, stop=True),
                 reads=[qTb[h], KXlo[eb % 2], KXhi], writes=[self.pb[bk]])

        def emit_EG(n):
            uu, eb, tt, h = unit_of(n)
            bk = B_S[n % NS]
            e2 = n % 2
            g4 = n % 4
            c.op("act", lambda e: e.activation(out=E[e2][:], in_=self.bank(bk), func=AF.Exp, bias=negc[:, tt, h:h + 1]),
                 reads=[self.pb[bk], thrb[tt]], writes=[Eb[e2]])
            c.op("dve", lambda e: e.scalar_tensor_tensor(out=G[g4][:], in0=self.bank(bk), scalar=thr[:, tt, h:h + 1],
                                                         in1=E[e2][:], op0=ALU.is_gt, op1=ALU.mult),
                 reads=[self.pb[bk], thrb[tt], Eb[e2]], writes=[Gb[g4]])

        def emit_T(n):
            uu, eb, tt, h = unit_of(n)
            wtb = B_WT[uu % 2]
            g4 = n % 4
            for ec in range(4):
                c.op("pe", lambda e: e.matmul(self.ps[:, wtb * 512 + ec * 128: wtb * 512 + (ec + 1) * 128],
                                              lhsT=G[g4][:, ec * 128:(ec + 1) * 128], rhs=self.ident[:],
                                              start=(h == 0 and ec == 0), stop=(h == 7), skip_group_check=True),
                     reads=[Gb[g4], self.identb], writes=[self.pb[wtb]])

        def emit_PT(uu):
            eb, tt = uu // NT, uu % NT
            wtb = B_WT[uu % 2]
            c.op("dve", lambda e: e.tensor_tensor(out=PT[uu % 2][:], in0=self.bank(wtb), in1=GA[:, tt, :], op=ALU.mult),
                 reads=[self.pb[wtb], GAb[tt]], writes=[PTb[uu % 2]])

        def emit_V(uu, only=None):
            eb, tt = uu // NT, uu % NT
            vb = Vb[eb % 2]
            for dh in (range(2) if only is None else (only,)):
                for ec in range(4):
                    c.op("pe", lambda e: e.matmul(self.bank(B_O[dh]), lhsT=PT[uu % 2][:, ec * 128:(ec + 1) * 128],
                                                  rhs=vb[:, ec, dh * 512:(dh + 1) * 512], start=(ec == 0), stop=(ec == 3)),
                         reads=[PTb[uu % 2], Vbb[eb % 2]], writes=[self.pb[B_O[dh]]])

        def emit_xacc(uu, only=None):
            eb, tt = uu // NT, uu % NT
            for dh in (range(2) if only is None else (only,)):
                xs = self.xres[:, tt, dh * 512:(dh + 1) * 512]
                c.op("dve", lambda e: e.tensor_tensor(out=xs, in0=self.bank(B_O[dh]), in1=xs, op=ALU.add),
                     reads=[self.pb[B_O[dh]], self.xb[tt]], writes=[self.xb[tt]])

        if NEB == 0:
            c.barrier()
            es.close()
            return
        load_U(0)
        load_V(0)
        build_KX(0)
        transp_eb(0)
        for tt in range(NT):
            emit_AT(0, tt)
            emit_Acopy(tt)
        emit_gelu_batch()
        load_U(1)
        transp_eb(1)
        NH = NU * 8
        LT = 2
        TAILS = [(LT, "PT", 0)] + [(3 + i, "V", i) for i in range(8)] + [(7, "X", 0), (11, "X", 1)] + \
                [(2 + jj, "AT", jj) for jj in range(8)] + [(10, "AC", 0)]
        TAILS.sort(key=lambda t: -t[0])
        for k in range(min(NH + 14, PEER_MAXSTEPS)):
            if k < NH:
                uu, eb, tt, h = unit_of(k)
                if h == 0 and tt == 2:
                    if eb + 1 < NEB:
                        load_V(eb + 1)
                        build_KX(eb + 1)
                    if eb + 2 < NEB:
                        load_U(eb + 2)
                emit_S(k)
                emit_EG(k)
            if 0 <= k - LT < NH:
                emit_T(k - LT)
            for (lag, what, arg) in TAILS:
                n = k - lag
                if not (n >= 7 and n < NH and n % 8 == 7):
                    continue
                uu = n // 8
                eb, tt = uu // NT, uu % NT
                if what == "PT":
                    emit_PT(uu)
                elif what == "V":
                    emit_V1(uu, arg)
                elif what == "X":
                    emit_xacc(uu, arg)
                elif what == "AT":
                    if eb + 1 < NEB:
                        emit_ATq(eb + 1, tt, arg)
                else:
                    if eb + 1 < NEB:
                        emit_Acopy(tt)
                        if tt == NT - 1:
                            if eb + 2 < NEB:
                                transp_eb(eb + 2)
                            emit_gelu_batch()
        c.barrier()
        es.close()

    gelu_func = AF.Gelu

    def ring_open(self, es, nslots, tag):
        self.ring = [es.enter_context(self.nc.sbuf_tensor("ring%s_%d" % (tag, i), [128, 8, 256], BF16)) for i in range(nslots)]
        self.ringb = [Buf() for _ in range(nslots)]
        self.ringsem = []
        for i in range(nslots):
            self.ringsem.append("rg%d" % i)
        self.ringi = 0

    def wload_cols(self, w2d, cols):
        i = self.ringi % len(self.ring)
        self.ringi += 1
        t, b = self.ring[i], self.ringb[i]
        off = 0
        for (c0, n) in cols:
            self.c.dma("pool", self.ringsem[i], t[:, :, off:off + n],
                       w2d[:, c0:c0 + n].rearrange("(dc p) f -> p dc f", p=128), writes=[b])
            off += n
        return t, b

    def wload_rows(self, w2d, row0):
        i = self.ringi % len(self.ring)
        self.ringi += 1
        t, b = self.ring[i], self.ringb[i]
        v = t[:].rearrange("p a b -> p (a b)").rearrange("p (a b) -> p a b", a=2)
        self.c.dma("pool", self.ringsem[i], v, w2d[row0:row0 + 256, :].rearrange("(a p) d -> p a d", p=128), writes=[b])
        return v, b

    def even_mixer(self, j=0):
        c = self.c
        nc = self.nc
        dr = self.dr
        es = ExitStack()
        self.es.enter_context(es)
        sb = lambda name, shape, dt: es.enter_context(nc.sbuf_tensor(name + "_e%d" % j, list(shape), dt))
        es.enter_context(self.norm_to_hnT(dr["mix_norm"][2 * j]))
        win = dr["even_w_in"][j]
        wout = dr["even_w_out"][j]
        self.ring_open(es, 6, "e")
        PB = self.pb

        prm = sb("prm", [128, 8, 8], F32)
        prmb = Buf()
        kw = dict(allow_slow_non_contiguous=True)
        for k in range(4):
            c.dma("sp", "ld0", prm[:, :, k:k + 1], dr["lru_conv_w"][j][k].rearrange("(c p o) -> p c o", p=128, o=1),
                  writes=[prmb], **kw)
        for idx, nm in ((4, "lru_conv_b"), (5, "lru_b_a"), (6, "lru_b_i"), (7, "lru_lambda")):
            c.dma("sp", "ld0", prm[:, :, idx:idx + 1], dr[nm][j].rearrange("(c p o) -> p c o", p=128, o=1), writes=[prmb], **kw)
        c.op("act", lambda e: e.activation(out=prm[:, :, 7:8], in_=prm[:, :, 7:8], func=AF.Exp, scale=-1.0), reads=[prmb], writes=[prmb])
        c.op("act", lambda e: e.activation(out=prm[:, :, 7:8], in_=prm[:, :, 7:8], func=AF.Ln, bias=1.0), reads=[prmb], writes=[prmb])
        c.op("dve", lambda e: e.tensor_scalar(out=prm[:, :, 7:8], in0=prm[:, :, 7:8], scalar1=-8.0, scalar2=None, op0=ALU.mult),
             reads=[prmb], writes=[prmb])
        WBD = sb("WBD", [128, 2, 8, 128], BF16)
        WBDb = Buf()
        c.op("pool", lambda e: e.memset(WBD[:], 0.0), writes=[WBDb])
        for gi, nm in enumerate(("lru_w_a", "lru_w_i")):
            for hl in range(2):
                c.dma("pool", "ld1", WBD[hl * 64:(hl + 1) * 64, gi, :, hl * 64:(hl + 1) * 64],
                      dr[nm][j].rearrange("(c two) i j -> two i c j", two=2)[hl], writes=[WBDb])

        YT = sb("YT", [128, 8, S], BF16)
        YTb = [Buf() for _ in range(8)]
        with ExitStack() as les:
            lb = lambda name, shape, dt: les.enter_context(nc.sbuf_tensor(name + "_e%d" % j, list(shape), dt))
            T0 = lb("T0", [128, 3 + S], F32)
            T0b = Buf()
            T = [lb("T%d" % i, [128, 1024], F32) for i in range(1, 5)]
            Tb = [Buf() for _ in range(4)]
            XCb = lb("XCb", [128, 1024], BF16)
            XCbb = Buf()
            GAg = lb("GAg", [128, 1024], BF16)
            GAgb = Buf()
            hl_ = lb("hlast", [128, 2], F32)
            hlb = Buf()
            c.op("pool", lambda e: e.memset(T0[:, 0:3], 0.0), writes=[T0b])
            for ch in range(8):
                wt, wb = self.wload_cols(win, [(ch * 128, 128), (1024 + ch * 128, 128)])
                for hf in range(2):
                    t0 = hf * 1024
                    for q in range(2):
                        for dc in range(8):
                            c.op("pe", lambda e: e.matmul(self.bank(q), lhsT=wt[:, dc, 128:256],
                                                          rhs=self.hnT[:, dc, t0 + q * 512:t0 + (q + 1) * 512],
                                                          start=(dc == 0), stop=(dc == 7)),
                                 reads=[wb] + self.hnTb[hf * 8 + q * 4: hf * 8 + q * 4 + 4], writes=[PB[q]])
                    for q in range(2):
                        for dc in range(8):
                            c.op("pe", lambda e: e.matmul(self.bank(2 + q), lhsT=wt[:, dc, 0:128],
                                                          rhs=self.hnT[:, dc, t0 + q * 512:t0 + (q + 1) * 512],
                                                          start=(dc == 0), stop=(dc == 7)),
                                 reads=[wb] + self.hnTb[hf * 8 + q * 4: hf * 8 + q * 4 + 4], writes=[PB[2 + q]])
                    c.op("act", lambda e: e.activation(out=T0[:, 3 + t0:3 + t0 + 1024], in_=self.ps[:, 0:1024], func=AF.Copy),
                         reads=[PB[0], PB[1]], writes=[T0b])
                    XC = T[0]
                    c.op("dve", lambda e: e.tensor_scalar(out=XC[:], in0=T0[:, 3 + t0:3 + t0 + 1024], scalar1=prm[:, ch, 3:4],
                                                          scalar2=prm[:, ch, 4:5], op0=ALU.mult, op1=ALU.add),
                         reads=[T0b, prmb], writes=[Tb[0]])
                    for k in range(3):
                        c.op("dve", lambda e: e.scalar_tensor_tensor(out=XC[:], in0=T0[:, k + t0:k + t0 + 1024],
                                                                     scalar=prm[:, ch, k:k + 1], in1=XC[:], op0=ALU.mult,
                                                                     op1=ALU.add), reads=[T0b, prmb, Tb[0]], writes=[Tb[0]])
                    c.op("pool", lambda e: e.tensor_copy(out=XCb[:], in_=XC[:]), reads=[Tb[0]], writes=[XCbb])
                    for gi in range(2):
                        for q in range(2):
                            bk = 4 + gi * 2 + q
                            c.op("pe", lambda e: e.matmul(self.bank(bk), lhsT=WBD[:, gi, ch, :], rhs=XCb[:, q * 512:(q + 1) * 512],
                                                          start=True, stop=True), reads=[WBDb, XCbb], writes=[PB[bk]])
                    A, M, IS = T[1], T[2], T[3]
                    c.op("act", lambda e: e.activation(out=A[:], in_=self.ps[:, 2048:3072], func=AF.Sigmoid, bias=prm[:, ch, 5:6]),
                         reads=[PB[4], PB[5], prmb], writes=[Tb[1]])
                    c.op("act", lambda e: e.activation(out=IS[:], in_=self.ps[:, 3072:4096], func=AF.Sigmoid, bias=prm[:, ch, 6:7]),
                         reads=[PB[6], PB[7], prmb], writes=[Tb[3]])
                    c.op("act", lambda e: e.activation(out=A[:], in_=A[:], func=AF.Exp, scale=prm[:, ch, 7:8]),
                         reads=[Tb[1], prmb], writes=[Tb[1]])
                    c.op("act", lambda e: e.activation(out=M[:], in_=A[:], func=AF.Square), reads=[Tb[1]], writes=[Tb[2]])
                    c.op("act", lambda e: e.activation(out=M[:], in_=M[:], func=AF.Sqrt, scale=-1.0, bias=1.0), reads=[Tb[2]], writes=[Tb[2]])
                    c.op("dve", lambda e: e.tensor_tensor(out=IS[:], in0=IS[:], in1=M[:], op=ALU.mult), reads=[Tb[3], Tb[2]], writes=[Tb[3]])
                    c.op("dve", lambda e: e.tensor_tensor(out=IS[:], in0=IS[:], in1=XC[:], op=ALU.mult), reads=[Tb[3], Tb[0]], writes=[Tb[3]])
                    init = 0.0 if hf == 0 else hl_[:, 0:1]
                    c.op("dve", lambda e: e.tensor_tensor_scan(out=M[:], data0=A[:], data1=IS[:], initial=init, op0=ALU.mult,
                                                               op1=ALU.add), reads=[Tb[1], Tb[3], hlb], writes=[Tb[2]])
                    if hf == 0:
                        c.op("dve", lambda e: e.tensor_copy(out=hl_[:, 0:1], in_=M[:, 1023:1024]), reads=[Tb[2]], writes=[hlb])
                    c.op("act", lambda e: e.activation(out=GAg[:], in_=self.ps[:, 1024:2048], func=self.gelu_func),
                         reads=[PB[2], PB[3]], writes=[GAgb])
                    c.op("dve", lambda e: e.tensor_tensor(out=YT[:, ch, t0:t0 + 1024], in0=M[:], in1=GAg[:], op=ALU.mult),
                         reads=[Tb[2], GAgb], writes=[YTb[ch]])
        wo = [self.wload_rows(wout, r * 256) for r in range(4)]
        self.out_proj(YT, YTb, wo, 8)
        c.barrier()
        es.close()
        self.retention(j)

    def retention(self, j):
        import math
        c = self.c
        nc = self.nc
        dr = self.dr
        PB = self.pb
        win = dr["even_w_in"][j]
        wout = dr["even_w_out"][j]
        es = ExitStack()
        self.es.enter_context(es)
        rb = lambda name, shape, dt: es.enter_context(nc.sbuf_tensor(name + "_r%d" % j, list(shape), dt))
        self.ring_open(es, 6, "r")
        COS = rb("COS", [128, S], F32)
        SIN = rb("SIN", [128, S], F32)
        tabb = Buf()
        DB = rb("DB", [128, 4, 128], F32)
        DM = rb("DM", [128, 4, 128], F32)
        decb = Buf()
        PI = math.pi
        with ExitStack() as tes:
            tb_ = lambda name, shape, dt: tes.enter_context(nc.sbuf_tensor(name + "_r%d" % j, list(shape), dt))
            invf = tb_("invf", [128, 4], F32)
            ANG = tb_("ANG", [128, S], F32)
            KI = tb_("KI", [128, S], mybir.dt.int32)
            KF = tb_("KF", [128, S], F32)
            tmb = Buf()
            c.op("pool", lambda e: e.iota(out=invf[:, 0:1], pattern=[[0, 1]], base=0, channel_multiplier=1,
                                          allow_small_or_imprecise_dtypes=True), writes=[tmb])
            c.op("dve", lambda e: e.tensor_scalar(out=invf[:, 1:2], in0=invf[:, 0:1], scalar1=-1.0 / 128, scalar2=None,
                                                  op0=ALU.mult), reads=[tmb], writes=[tmb])
            c.op("pool", lambda e: e.memset(invf[:, 2:3], 10000.0), reads=[tmb], writes=[tmb])
            c.op("pool", lambda e: e.tensor_tensor(out=invf[:, 3:4], in0=invf[:, 2:3], in1=invf[:, 1:2], op=ALU.pow),
                 reads=[tmb], writes=[tmb])
            c.op("pool", lambda e: e.iota(out=ANG[:], pattern=[[1, S]], base=0, channel_multiplier=0,
                                          allow_small_or_imprecise_dtypes=True), reads=[tmb], writes=[tmb])
            c.op("dve", lambda e: e.tensor_scalar(out=ANG[:], in0=ANG[:], scalar1=invf[:, 3:4], scalar2=None, op0=ALU.mult),
                 reads=[tmb], writes=[tmb])
            for dst, shift in ((SIN, 0.0), (COS, PI / 2)):
                c.op("dve", lambda e: e.tensor_scalar(out=KF[:], in0=ANG[:], scalar1=shift + PI, scalar2=1.0 / (2 * PI),
                                                      op0=ALU.add, op1=ALU.mult), reads=[tmb], writes=[tmb])
                c.op("dve", lambda e: e.tensor_copy(out=KI[:], in_=KF[:]), reads=[tmb], writes=[tmb])
                c.op("dve", lambda e: e.tensor_copy(out=KF[:], in_=KI[:]), reads=[tmb], writes=[tmb])
                c.op("dve", lambda e: e.scalar_tensor_tensor(out=KF[:], in0=KF[:], scalar=-2 * PI, in1=ANG[:], op0=ALU.mult,
                                                             op1=ALU.add), reads=[tmb], writes=[tmb])
                if shift != 0.0:
                    c.op("dve", lambda e: e.tensor_scalar(out=KF[:], in0=KF[:], scalar1=shift, scalar2=None, op0=ALU.add),
                         reads=[tmb], writes=[tmb])
                c.op("dve", lambda e: e.tensor_scalar(out=dst[:], in0=KF[:], scalar1=-PI, scalar2=2 * PI, op0=ALU.is_lt,
                                                      op1=ALU.mult), reads=[tmb], writes=[tabb])
                c.op("dve", lambda e: e.tensor_tensor(out=dst[:], in0=dst[:], in1=KF[:], op=ALU.add), reads=[tmb, tabb], writes=[tabb])
                c.op("dve", lambda e: e.tensor_scalar(out=dst[:], in0=dst[:], scalar1=-3.14159, scalar2=3.14159, op0=ALU.max,
                                                      op1=ALU.min), reads=[tabb], writes=[tabb])
                c.op("act", lambda e: e.activation(out=dst[:], in_=dst[:], func=AF.Sin), reads=[tabb], writes=[tabb])
            c.op("pool", lambda e: e.iota(out=KF[:, 0:128], pattern=[[1, 128]], base=0, channel_multiplier=-1,
                                          allow_small_or_imprecise_dtypes=True), reads=[tmb], writes=[tmb])
            for h in range(4):
                lg = math.log1p(-2.0 ** (-5 - h))
                c.op("act", lambda e: e.activation(out=DB[:, h, :], in_=KF[:, 0:128], func=AF.Exp, scale=lg), reads=[tmb], writes=[decb])
                c.op("pool", lambda e: e.affine_select(out=DM[:, h, :], in_=DB[:, h, :], pattern=[[1, 128]], compare_op=ALU.is_ge,
                                                       fill=0.0, base=0, channel_multiplier=-1), reads=[decb], writes=[decb])
            c.barrier()
        qT = rb("qTr", [128, 2, S], BF16)
        kT = rb("kTr", [128, 2, S], BF16)
        qkb = [Buf(), Buf()]
        VH = rb("VH", [128, NT, 256], BF16)
        VHb = Buf()
        YBT = rb("YBT", [128, 2, S], BF16)
        YBTb = [Buf(), Buf()]
        R = [rb("R%d" % i, [128, 512], F32) for i in range(4)]
        Rb = [Buf() for _ in range(4)]
        SC = [rb("SCr%d" % i, [128, 128], BF16) for i in range(4)]
        SCb = [Buf() for _ in range(4)]
        SG = [rb("SG%d" % i, [128, 256], F32) for i in range(2)]
        SGb = [Buf() for _ in range(2)]
        yb = [rb("yb%d" % i, [128, 256], BF16) for i in range(2)]
        ybb = [Buf() for _ in range(2)]
        junk = rb("junkr", [128, 256], BF16)
        junkb = Buf()
        ssr = rb("ssr", [128, 4], F32)
        ssrb = Buf()
        sreg = [Buf() for _ in range(8)]
        cnt = 0
        pj = 0
        for h in range(4):
            lg = math.log1p(-2.0 ** (-5 - h))
            wq = self.wload_cols(win, [(2048 + h * 256, 256)])
            wk = self.wload_cols(win, [(3072 + h * 256, 256)])
            wv = self.wload_cols(win, [(4096 + h * 256, 256)])
            wg = self.wload_cols(win, [(5120 + h * 256, 256)])
            wo = self.wload_rows(wout, 1024 + h * 256)
            for jq, ((wt, wb), dst) in enumerate(((wq, qT), (wk, kT))):
                for tb in range(4):
                    ba, bb = (4, 5) if pj % 2 == 0 else (6, 7)
                    pj += 1
                    ts = slice(tb * 512, (tb + 1) * 512)
                    for half, bk in ((0, ba), (1, bb)):
                        for dc in range(8):
                            c.op("pe", lambda e: e.matmul(self.bank(bk), lhsT=wt[:, dc, half * 128:(half + 1) * 128],
                                                          rhs=self.hnT[:, dc, ts], start=(dc == 0), stop=(dc == 7)),
                                 reads=[wb] + self.hnTb[tb * 4:tb * 4 + 4], writes=[PB[bk]])
                    c.op("dve", lambda e: e.tensor_tensor(out=R[0][:], in0=self.bank(ba), in1=COS[:, ts], op=ALU.mult),
                         reads=[PB[ba], tabb], writes=[Rb[0]])
                    c.op("dve", lambda e: e.tensor_tensor(out=R[1][:], in0=self.bank(bb), in1=SIN[:, ts], op=ALU.mult),
                         reads=[PB[bb], tabb], writes=[Rb[1]])
                    c.op("pool", lambda e: e.tensor_tensor(out=dst[:, 0, ts], in0=R[0][:], in1=R[1][:], op=ALU.subtract),
                         reads=[Rb[0], Rb[1]], writes=[qkb[jq]])
                    c.op("dve", lambda e: e.tensor_tensor(out=R[2][:], in0=self.bank(ba), in1=SIN[:, ts], op=ALU.mult),
                         reads=[PB[ba], tabb], writes=[Rb[2]])
                    c.op("dve", lambda e: e.tensor_tensor(out=R[3][:], in0=self.bank(bb), in1=COS[:, ts], op=ALU.mult),
                         reads=[PB[bb], tabb], writes=[Rb[3]])
                    c.op("pool", lambda e: e.tensor_tensor(out=dst[:, 1, ts], in0=R[2][:], in1=R[3][:], op=ALU.add),
                         reads=[Rb[2], Rb[3]], writes=[qkb[jq]])
            for tt in range(NT):
                bk = 4 + (tt // 2) % 4
                reg = self.ps[:, bk * 512 + (tt % 2) * 256: bk * 512 + (tt % 2) * 256 + 256]
                for dc in range(8):
                    c.op("pe", lambda e: e.matmul(reg, lhsT=self.hnT[:, dc, tt * 128:(tt + 1) * 128], rhs=wv[0][:, dc, :],
                                                  start=(dc == 0), stop=(dc == 7)), reads=[wv[1], self.hnTb[tt]], writes=[PB[bk]])
                c.op("act", lambda e: e.activation(out=VH[:, tt, :], in_=reg, func=AF.Copy), reads=[PB[bk]], writes=[VHb])
            for ci in range(NT):
                k2 = ci % 2
                ybk = 2 + k2
                Y = self.ps[:, ybk * 512: ybk * 512 + 256]
                GP = self.ps[:, ybk * 512 + 256: ybk * 512 + 512]
                for cj in range(ci + 1):
                    r = cnt % 8
                    cnt += 1
                    st = self.ps[:, r * 128:(r + 1) * 128]
                    for dch in range(2):
                        c.op("pe", lambda e: e.matmul(st, lhsT=kT[:, dch, cj * 128:(cj + 1) * 128],
                                                      rhs=qT[:, dch, ci * 128:(ci + 1) * 128], start=(dch == 0), stop=(dch == 1)),
                             reads=[qkb[0], qkb[1]], writes=[sreg[r]])
                    const = math.exp(lg * 128.0 * (ci - cj)) / 16.0
                    dmat = DM[:, h, :] if cj == ci else DB[:, h, :]
                    sc = SC[r % 4]
                    c.op("dve", lambda e: e.scalar_tensor_tensor(out=sc[:], in0=st, scalar=const, in1=dmat, op0=ALU.mult,
                                                                 op1=ALU.mult), reads=[sreg[r], decb], writes=[SCb[r % 4]])
                    c.op("pe", lambda e: e.matmul(Y, lhsT=sc[:], rhs=VH[:, cj, :], start=(cj == 0), stop=(cj == ci)),
                         reads=[SCb[r % 4], VHb], writes=[PB[ybk]])
                for dc in range(8):
                    c.op("pe", lambda e: e.matmul(GP, lhsT=self.hnT[:, dc, ci * 128:(ci + 1) * 128], rhs=wg[0][:, dc, :],
                                                  start=(dc == 0), stop=(dc == 7), skip_group_check=True),
                         reads=[wg[1], self.hnTb[ci]], writes=[PB[ybk]])
                c.op("act", lambda e: e.activation(out=SG[k2][:], in_=GP, func=AF.Silu), reads=[PB[ybk]], writes=[SGb[k2]])
                c.op("act", lambda e: e.activation(out=junk[:], in_=Y, func=AF.Square, accum_out=ssr[:, 0:1]),
                     reads=[PB[ybk]], writes=[junkb, ssrb])
                c.op("act", lambda e: e.activation(out=ssr[:, 1:2], in_=ssr[:, 0:1], func=AF.Sqrt, bias=EPS, scale=1.0 / 256),
                     reads=[ssrb], writes=[ssrb])
                c.op("dve", lambda e: e.reciprocal(out=ssr[:, 2:3], in_=ssr[:, 1:2]), reads=[ssrb], writes=[ssrb])
                c.op("dve", lambda e: e.scalar_tensor_tensor(out=yb[k2][:], in0=Y, scalar=ssr[:, 2:3], in1=SG[k2][:],
                                                             op0=ALU.mult, op1=ALU.mult),
                     reads=[PB[ybk], ssrb, SGb[k2]], writes=[ybb[k2]])
                tbk = 6 + k2
                for eh in range(2):
                    c.op("pe", lambda e: e.transpose(out=self.bank_bf(tbk)[:, eh * 128:(eh + 1) * 128],
                                                     in_=yb[k2][:, eh * 128:(eh + 1) * 128], identity=self.ident[:]),
                         reads=[ybb[k2], self.identb], writes=[PB[tbk]])
                c.op("act", lambda e: e.activation(out=YBT[:, :, ci * 128:(ci + 1) * 128],
                                                   in_=self.bank_bf(tbk)[:, 0:256].rearrange("p (a b) -> p a b", a=2), func=AF.Copy),
                     reads=[PB[tbk]], writes=[YBTb[0], YBTb[1]])
            self.out_proj(YBT, YBTb, [wo], 2)
        c.barrier()
        es.close()

    def out_proj(self, YT, YTb, wo, nchunks, rstd=None, rstdb=None):
        c = self.c
        for tt in range(NT):
            for dh in range(2):
                bk = (tt * 2 + dh) % 4
                for ch in range(nchunks):
                    wv, wb = wo[ch // 2]
                    c.op("pe", lambda e: e.matmul(self.bank(bk), lhsT=YT[:, ch, tt * 128:(tt + 1) * 128],
                                                  rhs=wv[:, ch % 2, dh * 512:(dh + 1) * 512], start=(ch == 0),
                                                  stop=(ch == nchunks - 1)), reads=[YTb[ch], wb], writes=[self.pb[bk]])
                xs = self.xres[:, tt, dh * 512:(dh + 1) * 512]
                if rstd is None:
                    c.op("dve", lambda e: e.tensor_tensor(out=xs, in0=self.bank(bk), in1=xs, op=ALU.add),
                         reads=[self.pb[bk], self.xb[tt]], writes=[self.xb[tt]])
                else:
                    c.op("dve", lambda e: e.scalar_tensor_tensor(out=xs, in0=self.bank(bk), scalar=rstd[:, tt:tt + 1], in1=xs,
                                                                 op0=ALU.mult, op1=ALU.add),
                         reads=[self.pb[bk], self.xb[tt], rstdb], writes=[self.xb[tt]])

    def odd_mixer(self, j=0):
        c = self.c
        nc = self.nc
        dr = self.dr
        PB = self.pb
        win = dr["ssm_w_in"][j]
        wout = dr["ssm_w_out"][j]
        ygs = dr["ygs"]
        es = ExitStack()
        self.es.enter_context(es)
        ob = lambda name, shape, dt: es.enter_context(nc.sbuf_tensor(name + "_o%d" % j, list(shape), dt))
        es.enter_context(self.norm_to_hnT(dr["mix_norm"][2 * j + 1]))
        self.ring_open(es, 4, "o")
        kw = dict(allow_slow_non_contiguous=True)
        TRI = ob("TRI", [128, 128], F32)
        ONES = ob("ONES", [128, 128], F32)
        MNEG = ob("MNEG", [128, 128], F32)
        cstb = Buf()
        c.op("pool", lambda e: e.memset(ONES[:], 1.0), writes=[cstb])
        c.op("pool", lambda e: e.affine_select(out=TRI[:], in_=ONES[:], pattern=[[1, 128]], compare_op=ALU.is_ge, fill=0.0,
                                               base=0, channel_multiplier=-1), reads=[cstb], writes=[cstb])
        c.op("pool", lambda e: e.memset(MNEG[:], 0.0), reads=[cstb], writes=[cstb])
        c.op("pool", lambda e: e.affine_select(out=MNEG[:], in_=MNEG[:], pattern=[[1, 128]], compare_op=ALU.is_ge, fill=NEG,
                                               base=0, channel_multiplier=-1), reads=[cstb], writes=[cstb])
        prm = ob("prm", [128, 24, 8], F32)
        prmb = Buf()
        for k in range(4):
            c.dma("sp", "ld0", prm[:, :, k:k + 1], dr["ssm_conv_w"][j][k].rearrange("(c p o) -> p c o", p=128, o=1), writes=[prmb], **kw)
        c.dma("sp", "ld0", prm[:, :, 4:5], dr["ssm_conv_b"][j].rearrange("(c p o) -> p c o", p=128, o=1), writes=[prmb], **kw)
        GN = ob("GN", [128, 16], F32)
        c.dma("sp", "ld0", GN[:].unsqueeze(2), dr["ssm_norm"][j].rearrange("(c p o) -> p c o", p=128, o=1), writes=[prmb], **kw)
        hp = ob("hp", [128, 4, 32], F32)
        hpb = Buf()
        for idx, nm in ((0, "ssm_dt_bias"), (1, "ssm_a_log"), (2, "ssm_d")):
            c.dma("sp", "ld1", hp[:, idx, :], dr[nm][j].partition_broadcast(128), writes=[hpb])
        c.op("act", lambda e: e.activation(out=hp[:, 1, :], in_=hp[:, 1, :], func=AF.Exp), reads=[hpb], writes=[hpb])
        c.op("dve", lambda e: e.tensor_scalar(out=hp[:, 1, :], in0=hp[:, 1, :], scalar1=-1.0, scalar2=None, op0=ALU.mult),
             reads=[hpb], writes=[hpb])
        fam = {n: ob(n, [128, NT, 32], F32) for n in ("DT", "ADT", "ACUM", "NACUM", "EA", "DEC", "CD")}
        famb = Buf()
        SS = ob("SS", [128, NT, 4], F32)
        SSb = Buf()
        wdt, wdtb = self.wload_cols(win, [(5120, 32)])
        for tt in range(NT):
            for dc in range(8):
                c.op("pe", lambda e: e.matmul(self.ps[:, tt * 32:(tt + 1) * 32], lhsT=self.hnT[:, dc, tt * 128:(tt + 1) * 128],
                                              rhs=wdt[:, dc, 0:32], start=(dc == 0), stop=(dc == 7)),
                     reads=[wdtb, self.hnTb[tt]], writes=[PB[0]])
        f3 = lambda t: t[:]
        flat = lambda t: t[:].rearrange("p a b -> p (a b)")
        bc_h = lambda ap2: ap2.unsqueeze(1).to_broadcast([128, NT, 32])
        DT, ADT, ACUM, NACUM, EA, DEC, CD = (fam[n] for n in ("DT", "ADT", "ACUM", "NACUM", "EA", "DEC", "CD"))
        c.op("dve", lambda e: e.tensor_tensor(out=ADT[:], in0=self.bank(0).rearrange("p (a b) -> p a b", a=NT), in1=bc_h(hp[:, 0, :]),
                                              op=ALU.add), reads=[PB[0], hpb], writes=[famb])
        c.op("act", lambda e: e.activation(out=flat(EA), in_=flat(ADT), func=AF.Abs), reads=[famb], writes=[famb])
        c.op("act", lambda e: e.activation(out=flat(EA), in_=flat(EA), func=AF.Exp, scale=-1.0), reads=[famb], writes=[famb])
        c.op("act", lambda e: e.activation(out=flat(EA), in_=flat(EA), func=AF.Ln, bias=1.0), reads=[famb], writes=[famb])
        c.op("dve", lambda e: e.scalar_tensor_tensor(out=flat(DT), in0=flat(ADT), scalar=0.0, in1=flat(EA), op0=ALU.max, op1=ALU.add),
             reads=[famb], writes=[famb])
        c.op("dve", lambda e: e.tensor_tensor(out=ADT[:], in0=DT[:], in1=bc_h(hp[:, 1, :]), op=ALU.mult), reads=[famb, hpb], writes=[famb])
        c.op("pe", lambda e: e.matmul(self.bank(1), lhsT=TRI[:], rhs=flat(ADT), start=True, stop=True), reads=[cstb, famb], writes=[PB[1]])
        c.op("pe", lambda e: e.matmul(self.bank(2), lhsT=ONES[:], rhs=flat(ADT), start=True, stop=True), reads=[cstb, famb], writes=[PB[2]])
        c.op("act", lambda e: e.activation(out=flat(ACUM), in_=self.bank(1), func=AF.Copy), reads=[PB[1]], writes=[famb])
        c.op("act", lambda e: e.activation(out=flat(EA), in_=self.bank(1), func=AF.Exp), reads=[PB[1]], writes=[famb])
        c.op("act", lambda e: e.activation(out=flat(CD), in_=self.bank(2), func=AF.Exp), reads=[PB[2]], writes=[famb])
        c.op("dve", lambda e: e.tensor_scalar(out=flat(NACUM), in0=flat(ACUM), scalar1=-1.0, scalar2=None, op0=ALU.mult),
             reads=[famb], writes=[famb])
        c.op("dve", lambda e: e.tensor_tensor(out=flat(DEC), in0=self.bank(2), in1=flat(ACUM), op=ALU.subtract),
             reads=[PB[2], famb], writes=[famb])
        c.op("act", lambda e: e.activation(out=flat(DEC), in_=flat(DEC), func=AF.Exp), reads=[famb], writes=[famb])
        c.barrier()
        with ExitStack() as ges:
            gb_ = lambda name, shape, dt: ges.enter_context(nc.sbuf_tensor(name + "_o%d" % j, list(shape), dt))
            BT = gb_("BT", [128, S], BF16)
            CT = gb_("CT", [128, S], BF16)
            BCb = [Buf(), Buf()]
            XS = gb_("XS", [128, NT, 512], BF16)
            XSb = Buf()
            STG = gb_("STG", [128, 3 + S], F32)
            STGb = Buf()
            TC = gb_("TC", [128, 1024], F32)
            TCb = Buf()
            XSF = gb_("XSF", [128, 1024], BF16)
            XSFb = Buf()
            STATE = gb_("STATE", [128, 8, 64], F32)
            STb = Buf()
            STATEb = gb_("STATEb", [128, 512], BF16)
            STbb = Buf()
            LA = gb_("LA", [128, 8, 128], BF16)
            LAb = Buf()
            MT = gb_("MT", [128, 8, 128], BF16)
            MTb = Buf()
            CBm = gb_("CBm", [128, 128], BF16)
            CBmb = Buf()
            BTOK = gb_("BTOK", [128, 128], BF16)
            BTOKb = Buf()
            RH = [gb_("RH%d" % i, [128, 128], F32) for i in range(2)]
            RHb = [Buf(), Buf()]
            XDT = gb_("XDT", [128, 8, 64], BF16)
            XDTb = Buf()
            XDD = gb_("XDD", [128, 8, 64], BF16)
            XDDb = Buf()
            SZ = gb_("SZ", [128, 512], F32)
            SZb = Buf()
            W1 = gb_("W1", [128, 8, 64], F32)
            W1b = Buf()
            W2 = gb_("W2", [128, 8, 64], F32)
            W2b = Buf()
            YG = [gb_("YG%d" % i, [128, 512], BF16) for i in range(2)]
            YGb = [Buf(), Buf()]
            junk = gb_("junko", [128, 512], BF16)
            junkb = Buf()
            c.op("pool", lambda e: e.memset(STG[:, 0:3], 0.0), writes=[STGb])
            ygi = 0
            for g in range(4):
                wbc = self.wload_cols(win, [(4096 + g * 128, 128), (4608 + g * 128, 128)])
                wxs = [self.wload_cols(win, [(2048 + g * 512 + i * 256, 256)]) for i in range(2)]
                hs = slice(8 * g, 8 * g + 8)
                for fc in range(6):
                    if fc < 2:
                        wt, wb, co, q = wbc[0], wbc[1], fc * 128, 16 + 4 * fc + g
                    else:
                        jx = fc - 2
                        wt, wb, co, q = wxs[jx // 2][0], wxs[jx // 2][1], (jx % 2) * 128, g * 4 + jx
                    for hf in range(2):
                        t0 = hf * 1024
                        for qq in range(2):
                            for dc in range(8):
                                c.op("pe", lambda e: e.matmul(self.bank(qq), lhsT=wt[:, dc, co:co + 128],
                                                              rhs=self.hnT[:, dc, t0 + qq * 512:t0 + (qq + 1) * 512],
                                                              start=(dc == 0), stop=(dc == 7)),
                                     reads=[wb] + self.hnTb[hf * 8 + qq * 4:hf * 8 + qq * 4 + 4], writes=[PB[qq]])
                        c.op("act", lambda e: e.activation(out=STG[:, 3 + t0:3 + t0 + 1024], in_=self.ps[:, 0:1024], func=AF.Copy),
                             reads=[PB[0], PB[1]], writes=[STGb])
                        c.op("dve", lambda e: e.tensor_scalar(out=TC[:], in0=STG[:, 3 + t0:3 + t0 + 1024], scalar1=prm[:, q, 3:4],
                                                              scalar2=prm[:, q, 4:5], op0=ALU.mult, op1=ALU.add),
                             reads=[STGb, prmb], writes=[TCb])
                        for k in range(3):
                            c.op("dve", lambda e: e.scalar_tensor_tensor(out=TC[:], in0=STG[:, k + t0:k + t0 + 1024],
                                                                         scalar=prm[:, q, k:k + 1], in1=TC[:], op0=ALU.mult,
                                                                         op1=ALU.add), reads=[STGb, prmb, TCb], writes=[TCb])
                        if fc < 2:
                            dst = (BT, CT)[fc]
                            c.op("act", lambda e: e.activation(out=dst[:, t0:t0 + 1024], in_=TC[:], func=AF.Silu),
                                 reads=[TCb], writes=[BCb[fc]])
                        else:
                            c.op("act", lambda e: e.activation(out=XSF[:], in_=TC[:], func=AF.Silu), reads=[TCb], writes=[XSFb])
                            tbk = 2 + (fc * 2 + hf) % 2
                            for k in range(8):
                                c.op("pe", lambda e: e.transpose(out=self.bank_bf(tbk)[:, k * 128:(k + 1) * 128],
                                                                 in_=XSF[:, k * 128:(k + 1) * 128], identity=self.ident[:]),
                                     reads=[XSFb, self.identb], writes=[PB[tbk]])
                            c.op("act", lambda e: e.activation(out=XS[:, hf * 8:hf * 8 + 8, (fc - 2) * 128:(fc - 1) * 128],
                                                               in_=self.bank_bf(tbk).rearrange("p (a b) -> p a b", a=8), func=AF.Copy),
                                 reads=[PB[tbk]], writes=[XSb])
                wz = [self.wload_cols(win, [(g * 512 + i * 256, 256)]) for i in range(2)]
                c.op("pool", lambda e: e.memset(STATE[:], 0.0), reads=[STb], writes=[STb])
                c.op("pool", lambda e: e.memset(STATEb[:], 0.0), reads=[STbb], writes=[STbb])
                for tt in range(NT):
                    tsl = slice(tt * 128, (tt + 1) * 128)
                    c.op("pe", lambda e: e.matmul(self.ps[:, 4 * 512:4 * 512 + 128], lhsT=BT[:, tsl], rhs=CT[:, tsl], start=True, stop=True),
                         reads=BCb, writes=[PB[4]])
                    c.op("act", lambda e: e.activation(out=CBm[:], in_=self.ps[:, 4 * 512:4 * 512 + 128], func=AF.Copy),
                         reads=[PB[4]], writes=[CBmb])
                    c.op("pe", lambda e: e.transpose(out=self.bank_bf(5)[:, 0:128], in_=BT[:, tsl], identity=self.ident[:]),
                         reads=[BCb[0], self.identb], writes=[PB[5]])
                    c.op("act", lambda e: e.activation(out=BTOK[:], in_=self.bank_bf(5)[:, 0:128], func=AF.Copy),
                         reads=[PB[5]], writes=[BTOKb])
                    for h in range(8):
                        hh = 8 * g + h
                        rh = RH[h % 2]
                        bk = 6 + (h // 4) % 2
                        reg = self.ps[:, bk * 512 + (h % 4) * 128: bk * 512 + (h % 4 + 1) * 128]
                        c.op("dve", lambda e: e.tensor_scalar(out=rh[:], in0=TRI[:], scalar1=ADT[:, tt, hh:hh + 1], scalar2=None,
                                                              op0=ALU.mult), reads=[cstb, famb], writes=[RHb[h % 2]])
                        c.op("pe", lambda e: e.matmul(reg, lhsT=ONES[:], rhs=rh[:], start=True, stop=False),
                             reads=[cstb, RHb[h % 2]], writes=[PB[bk]])
                        c.op("pe", lambda e: e.matmul(reg, lhsT=self.identf[:], rhs=MNEG[:], start=False, stop=True),
                             reads=[cstb, self.identfb], writes=[PB[bk]])
                        c.op("act", lambda e: e.activation(out=LA[:, h, :], in_=reg, func=AF.Exp, bias=NACUM[:, tt, hh:hh + 1]),
                             reads=[PB[bk], famb], writes=[LAb])
                    c.op("dve", lambda e: e.tensor_tensor(out=MT[:], in0=LA[:], in1=CBm[:].unsqueeze(1).to_broadcast([128, 8, 128]),
                                                          op=ALU.mult), reads=[LAb, CBmb], writes=[MTb])
                    xs3 = XS[:, tt, :].rearrange("p (a b) -> p a b", a=8)
                    bc_p = lambda ap2: ap2.unsqueeze(2).to_broadcast([128, 8, 64])
                    c.op("dve", lambda e: e.tensor_tensor(out=XDT[:], in0=xs3, in1=bc_p(DT[:, tt, hs]), op=ALU.mult),
                         reads=[XSb, famb], writes=[XDTb])
                    c.op("dve", lambda e: e.tensor_tensor(out=XDD[:], in0=XDT[:], in1=bc_p(DEC[:, tt, hs]), op=ALU.mult),
                         reads=[XDTb, famb], writes=[XDDb])
                    for h in range(8):
                        c.op("pe", lambda e: e.matmul(self.ps[:, 2 * 512 + h * 64: 2 * 512 + (h + 1) * 64], lhsT=MT[:, h, :],
                                                      rhs=XDT[:, h, :], start=True, stop=True), reads=[MTb, XDTb], writes=[PB[2]])
                    c.op("pe", lambda e: e.matmul(self.bank(3), lhsT=CT[:, tsl], rhs=STATEb[:], start=True, stop=True),
                         reads=[BCb[1], STbb], writes=[PB[3]])
                    c.op("pe", lambda e: e.matmul(self.bank(0), lhsT=BTOK[:], rhs=XDD[:].rearrange("p a b -> p (a b)"), start=True, stop=True),
                         reads=[BTOKb, XDDb], writes=[PB[0]])
                    for i in range(2):
                        for dc in range(8):
                            c.op("pe", lambda e: e.matmul(self.ps[:, 512 + i * 256:512 + (i + 1) * 256], lhsT=self.hnT[:, dc, tsl],
                                                          rhs=wz[i][0][:, dc, :], start=(dc == 0), stop=(dc == 7)),
                                 reads=[wz[i][1], self.hnTb[tt]], writes=[PB[1]])
                    c.op("act", lambda e: e.activation(out=SZ[:], in_=self.bank(1), func=AF.Silu), reads=[PB[1]], writes=[SZb])
                    c.op("dve", lambda e: e.tensor_tensor(out=W1[:], in0=self.bank(3).rearrange("p (a b) -> p a b", a=8),
                                                          in1=bc_p(EA[:, tt, hs]), op=ALU.mult), reads=[PB[3], famb], writes=[W1b])
                    c.op("dve", lambda e: e.tensor_tensor(out=W1[:], in0=self.bank(2).rearrange("p (a b) -> p a b", a=8), in1=W1[:],
                                                          op=ALU.add), reads=[PB[2], W1b], writes=[W1b])
                    c.op("dve", lambda e: e.tensor_tensor(out=W2[:], in0=xs3, in1=bc_p(hp[:, 2, hs]), op=ALU.mult),
                         reads=[XSb, hpb], writes=[W2b])
                    c.op("pool", lambda e: e.tensor_tensor(out=W1[:], in0=W1[:], in1=W2[:], op=ALU.add), reads=[W1b, W2b], writes=[W1b])
                    yg = YG[ygi % 2]
                    c.op("dve", lambda e: e.tensor_tensor(out=yg[:], in0=W1[:].rearrange("p a b -> p (a b)"), in1=SZ[:], op=ALU.mult),
                         reads=[W1b, SZb], writes=[YGb[ygi % 2]])
                    c.op("act", lambda e: e.activation(out=junk[:], in_=yg[:], func=AF.Square, accum_out=SS[:, tt, g:g + 1]),
                         reads=[YGb[ygi % 2]], writes=[junkb, SSb])
                    c.dma("sp", "ld2" if ygi % 2 == 0 else "ld3", ygs[tt * 128:(tt + 1) * 128, g * 512:(g + 1) * 512], yg[:],
                          reads=[YGb[ygi % 2]])
                    ygi += 1
                    c.op("dve", lambda e: e.tensor_tensor(out=STATE[:], in0=STATE[:], in1=bc_p(CD[:, tt, hs]), op=ALU.mult),
                         reads=[STb, famb], writes=[STb])
                    c.op("dve", lambda e: e.tensor_tensor(out=STATE[:], in0=self.bank(0).rearrange("p (a b) -> p a b", a=8), in1=STATE[:],
                                                          op=ALU.add), reads=[PB[0], STb], writes=[STb])
                    c.op("act", lambda e: e.activation(out=STATEb[:], in_=STATE[:].rearrange("p a b -> p (a b)"), func=AF.Copy),
                         reads=[STb], writes=[STbb])
            c.barrier()
        RS = ob("RS", [128, 3, NT], F32)
        RSb = Buf()
        c.op("dve", lambda e: e.tensor_reduce(out=RS[:, 0, :], in_=SS[:], axis=AX.X, op=ALU.add), reads=[SSb], writes=[RSb])
        c.op("act", lambda e: e.activation(out=RS[:, 1, :], in_=RS[:, 0, :], func=AF.Sqrt, bias=EPS, scale=1.0 / 2048), reads=[RSb], writes=[RSb])
        c.op("dve", lambda e: e.reciprocal(out=RS[:, 2, :], in_=RS[:, 1, :]), reads=[RSb], writes=[RSb])
        WO = ob("WO", [128, 16, D], BF16)
        WOb = Buf()
        for r in range(8):
            c.dma("pool", "ld4", WO[:, 2 * r:2 * r + 2, :], wout[r * 256:(r + 1) * 256, :].rearrange("(a p) d -> p a d", p=128), writes=[WOb])
        YGF = [ob("YGF%d" % i, [128, 2048], BF16) for i in range(2)]
        YGFb = [Buf(), Buf()]
        YGT = [ob("YGT%d" % i, [128, 16, 128], BF16) for i in range(2)]
        YGTb = [Buf(), Buf()]
        for tt in range(NT):
            k = tt % 2
            c.dma("sp", "ld2" if k == 0 else "ld3", YGF[k][:], ygs[tt * 128:(tt + 1) * 128, :], writes=[YGFb[k]])
            b0 = 4 + 2 * k
            for ch in range(16):
                bk = b0 + ch // 8
                c.op("pe", lambda e: e.transpose(out=self.bank_bf(bk)[:, (ch % 8) * 128:(ch % 8 + 1) * 128],
                                                 in_=YGF[k][:, ch * 128:(ch + 1) * 128], identity=self.ident[:]),
                     reads=[YGFb[k], self.identb], writes=[PB[bk]])
            src = self.ps[:, b0 * 512:(b0 + 2) * 512].bitcast(BF16).rearrange("p (a b) -> p a b", a=16)
            c.op("dve", lambda e: e.tensor_tensor(out=YGT[k][:], in0=src, in1=GN[:].unsqueeze(2).to_broadcast([128, 16, 128]), op=ALU.mult),
                 reads=[PB[b0], PB[b0 + 1], prmb], writes=[YGTb[k]])
            for dh in range(2):
                bk = 2 * k + dh
                for ch in range(16):
                    c.op("pe", lambda e: e.matmul(self.bank(bk), lhsT=YGT[k][:, ch, :], rhs=WO[:, ch, dh * 512:(dh + 1) * 512],
                                                  start=(ch == 0), stop=(ch == 15)), reads=[YGTb[k], WOb], writes=[PB[bk]])
                xs_ = self.xres[:, tt, dh * 512:(dh + 1) * 512]
                c.op("dve", lambda e: e.scalar_tensor_tensor(out=xs_, in0=self.bank(bk), scalar=RS[:, 2, tt:tt + 1], in1=xs_,
                                                             op0=ALU.mult, op1=ALU.add),
                     reads=[PB[bk], RSb, self.xb[tt]], writes=[self.xb[tt]])
        c.barrier()
        es.close()

    def final_norm(self):
        c = self.c
        nes = self.norm_open()
        self.es.enter_context(nes)
        self.load_gain(self.dr["final_norm"])
        self.norm_stats()
        for tt in range(NT):
            c.op("dve", lambda e: e.scalar_tensor_tensor(out=self.xres[:, tt, :], in0=self.xres[:, tt, :],
                                                         scalar=self.stat[:, 16 + tt:17 + tt], in1=self.gbc[:], op0=ALU.mult,
                                                         op1=ALU.mult), reads=[self.xb[tt], self.statb, self.gbcb], writes=[self.xb[tt]])


LAST_INPUT_NAMES = []


def _build_once(stages, needed):
    nc = bass.Bass("TRN2", target_bir_lowering=False)
    dram = {}

    def din(name, shape):
        dram[name] = nc.dram_tensor(name, list(shape), F32, kind="ExternalInput").ap()
        if name not in LAST_INPUT_NAMES:
            LAST_INPUT_NAMES.append(name)

    din("x", [S, D])
    din("ffn_norm", [2, D])
    din("mix_norm", [2, D])
    din("final_norm", [D])
    din("even_w_in", [1, D, 6144])
    din("lru_conv_w", [1, 4, D])
    din("lru_conv_b", [1, D])
    din("lru_w_a", [1, 16, 64, 64])
    din("lru_b_a", [1, D])
    din("lru_w_i", [1, 16, 64, 64])
    din("lru_b_i", [1, D])
    din("lru_lambda", [1, D])
    din("even_w_out", [1, 2048, D])
    din("ssm_w_in", [1, D, 5152])
    din("ssm_conv_w", [1, 4, 3072])
    din("ssm_conv_b", [1, 3072])
    din("ssm_dt_bias", [1, 32])
    din("ssm_a_log", [1, 32])
    din("ssm_d", [1, 32])
    din("ssm_norm", [1, 2048])
    din("ssm_w_out", [1, 2048, D])
    dram["ygs"] = nc.dram_tensor("ygs", [S, 2048], BF16, kind="Internal").ap()
    din("peer_w_q", [2, D, D])
    din("peer_sub_keys", [2, 8, 2, 128, 64])
    din("peer_u", [2, 16384, D])
    din("peer_v", [2, 16384, D])
    dram["y"] = nc.dram_tensor("y", [S, D], F32, kind="ExternalOutput").ap()
    with ExitStack() as es:
        k = Kern(nc, es, dram, needed)
        k.load_x()
        for st in stages:
            if st == "peer0":
                k.peer(0)
            elif st == "peer1":
                k.peer(1)
            elif st == "odd":
                k.odd_mixer(0)
            elif st == "even":
                k.even_mixer(0)
            elif st == "final":
                k.final_norm()
        k.store_x("y")
    return nc, k.c.waited


def build(stages=("peer0",), out_name="y"):
    _, waited = _build_once(stages, None)
    nc, waited2 = _build_once(stages, waited)
    assert waited == waited2
    return nc


ALL_STAGES = ("even", "peer0", "odd", "peer1", "final")


def kernel(**inputs):
    nc = build(ALL_STAGES)
    names = [n for n in LAST_INPUT_NAMES if n != "x"]
    shared = {n: np.ascontiguousarray(np.asarray(inputs[n], dtype=np.float32)) for n in names}
    x = np.asarray(inputs["x"], dtype=np.float32)
    nb = x.shape[0]
    maps = [dict(shared, x=np.ascontiguousarray(x[b])) for b in range(nb)]
    res = run_bass_kernel_spmd(nc, maps, core_ids=list(range(nb)))
    return np.stack([np.asarray(res.results[b]["y"], dtype=np.float32) for b in range(nb)], axis=0)
```

```python
import numpy as np
from contextlib import ExitStack
import concourse.bass as bass
import concourse.mybir as mybir
from concourse.bass_types import AP
from concourse.bass_utils import run_bass_kernel_spmd

F32 = mybir.dt.float32
BF16 = mybir.dt.bfloat16
AF = mybir.ActivationFunctionType
ALU = mybir.AluOpType
AX = mybir.AxisListType

S = 2048
D = 1024
NT = 16
EPS = 1e-6
NEG = -1.0e30
PEER_NEB = 32
NORM_SCOPED = True
PEER_MAXSTEPS = 10 ** 9
NORM_BARRIER = False


class Buf:
    __slots__ = ("w", "r", "name")

    def __init__(self, name=""):
        self.w = None
        self.r = {}
        self.name = name


class Ctx:
    def __init__(self, nc, es, needed=None):
        self.nc = nc
        self.es = es
        self.eng = {"pe": nc.tensor, "act": nc.scalar, "dve": nc.vector, "pool": nc.gpsimd, "sp": nc.sync}
        self.semobj = {}
        self.val = {}
        self.real = {}
        self.v2r = {}
        self.needed = needed
        self.waited = set()
        self.seen = {k: {} for k in self.eng}
        for k in self.eng:
            self.newsem(k)

    def newsem(self, name):
        self.semobj[name] = self.es.enter_context(self.nc.semaphore("s_" + name))
        self.val[name] = 0
        self.real[name] = 0
        return name

    def sbuf(self, name, shape, dt):
        return self.es.enter_context(self.nc.sbuf_tensor(name, list(shape), dt))

    def _wait(self, e, s, v):
        self.waited.add((s, v))
        if self.needed is None or s not in self.eng:
            rv = v
        else:
            rv = self.v2r[(s, v)]
        self.eng[e].wait_ge(self.semobj[s], rv)
        self.seen[e][s] = v

    def _deps(self, e, reads, writes):
        need = {}

        def add(s, v):
            if s == "pe" and e == "pe":
                return
            if need.get(s, 0) < v:
                need[s] = v

        for b in reads:
            if b.w is not None:
                add(*b.w)
        for b in writes:
            if b.w is not None:
                add(*b.w)
            for s, v in b.r.items():
                add(s, v)
        for s, v in need.items():
            if self.seen[e].get(s, 0) < v:
                self._wait(e, s, v)

    def op(self, e, fn, reads=(), writes=()):
        self._deps(e, reads, writes)
        self.val[e] += 1
        v = self.val[e]
        ins = fn(self.eng[e])
        if self.needed is None:
            ins.then_inc(self.semobj[e], 1)
        elif (e, v) in self.needed:
            self.real[e] += 1
            self.v2r[(e, v)] = self.real[e]
            ins.then_inc(self.semobj[e], 1)
        for b in reads:
            if b.r.get(e, 0) < v:
                b.r[e] = v
        for b in writes:
            b.w = (e, v)
            b.r = {}

    def dma(self, q, sem, out, in_, reads=(), writes=(), **kw):
        sem = q + "_" + sem
        if sem not in self.semobj:
            self.newsem(sem)
        self._deps(q, reads, writes)
        self.val[sem] += 16
        v = self.val[sem]
        self.eng[q].dma_start(out=out, in_=in_, **kw).then_inc(self.semobj[sem], 16)
        for b in reads:
            b.r[sem] = v
        for b in writes:
            b.w = (sem, v)
            b.r = {}

    def barrier(self, engines=None):
        for e in (engines or self.eng):
            for s, v in self.val.items():
                if s == e or v == 0:
                    continue
                if self.seen[e].get(s, 0) < v:
                    self._wait(e, s, v)


def bcast_last(ap2d, n):
    return ap2d.unsqueeze(2).broadcast_to([ap2d.shape[0], ap2d.shape[1], n])


class Kern:
    def __init__(self, nc, es, dram, needed=None):
        self.nc = nc
        self.c = Ctx(nc, es, needed)
        self.es = es
        self.dr = dram
        c = self.c
        self.xres = c.sbuf("xres", [128, NT, D], F32)
        self.xb = [Buf("x%d" % i) for i in range(NT)]
        self.hnT = c.sbuf("hnT", [128, 8, S], BF16)
        self.hnTb = [Buf("hnT%d" % i) for i in range(NT)]
        self.ident = c.sbuf("ident", [128, 128], BF16)
        self.identb = Buf("ident")
        self.ps = es.enter_context(nc.psum_tensor("ps", [128, 4096], F32))
        self.pb = [Buf("bank%d" % i) for i in range(8)]
        self.stat = c.sbuf("stat", [128, 64], F32)
        self.statb = Buf("stat")
        self.nscope = 0
        for n in ("sp_io", "sp_ld0", "sp_ld1", "sp_ld2", "sp_ld3", "sp_ld5", "pool_ld0", "pool_ld1", "pool_ld2", "pool_ld3",
                  "pool_ld4", "pool_rg0", "pool_rg1", "pool_rg2", "pool_rg3", "pool_rg4", "pool_rg5"):
            c.newsem(n)
        self.make_ident()

    def bank(self, i, n=512):
        return self.ps[:, i * 512:i * 512 + n]

    def bank_bf(self, i):
        return self.ps[:, i * 512:(i + 1) * 512].bitcast(BF16)

    def make_ident(self):
        c = self.c
        tmp = c.sbuf("identf", [128, 128], F32)
        tb = Buf()
        c.op("pool", lambda e: e.memset(tmp[:], 1.0), writes=[tb])
        c.op("pool", lambda e: e.affine_select(out=tmp[:], in_=tmp[:], pattern=[[1, 128]], compare_op=ALU.is_equal,
                                               fill=0.0, base=0, channel_multiplier=-1), reads=[tb], writes=[tb])
        c.op("dve", lambda e: e.tensor_copy(out=self.ident[:], in_=tmp[:]), reads=[tb], writes=[self.identb])
        self.identf = tmp
        self.identfb = tb

    def load_x(self):
        c = self.c
        x = self.dr["x"]
        for tt in range(NT):
            c.dma("sp", "io", self.xres[:, tt, :], x[tt * 128:(tt + 1) * 128, :], writes=[self.xb[tt]])
        for tt in range(NT):
            self.xb[tt].w = ("sp_io", c.val["sp_io"])

    def store_x(self, name="y"):
        c = self.c
        y = self.dr[name]
        for tt in range(NT):
            c.dma("sp", "io", y[tt * 128:(tt + 1) * 128, :], self.xres[:, tt, :], reads=[self.xb[tt]])
        c._wait("sp", "sp_io", c.val["sp_io"])

    def norm_open(self):
        es = ExitStack()
        self.nscope += 1
        t = "_n%d" % self.nscope
        self.gbc = es.enter_context(self.nc.sbuf_tensor("gbc" + t, [128, D], F32))
        self.gbcb = Buf("gbc")
        self.hn = [es.enter_context(self.nc.sbuf_tensor("hn%d%s" % (i, t), [128, D], BF16)) for i in range(2)]
        self.hnb = [Buf("hn%d" % i) for i in range(2)]
        return es

    def norm_stats(self):
        c = self.c
        junk = self.hn[0]
        for tt in range(NT):
            c.op("act", lambda e, tt=tt: e.activation(out=junk[:], in_=self.xres[:, tt, :], func=AF.Square,
                                                     accum_out=self.stat[:, tt:tt + 1]),
                 reads=[self.xb[tt]], writes=[self.hnb[0], self.statb])
        c.op("act", lambda e: e.activation(out=self.stat[:, 32:48], in_=self.stat[:, 0:16], func=AF.Sqrt, bias=EPS,
                                           scale=1.0 / D), reads=[self.statb], writes=[self.statb])
        c.op("dve", lambda e: e.reciprocal(out=self.stat[:, 16:32], in_=self.stat[:, 32:48]), reads=[self.statb],
             writes=[self.statb])

    def load_gain(self, gain_ap):
        self.c.dma("sp", "ld5", self.gbc[:], gain_ap.partition_broadcast(128), writes=[self.gbcb])

    def norm_to_hnT(self, gain_ap, free=False):
        c = self.c
        nes = self.norm_open()
        self.load_gain(gain_ap)
        self.norm_stats()
        for tt in range(NT):
            k = tt % 2
            hn = self.hn[k]
            c.op("dve", lambda e, tt=tt, hn=hn: e.scalar_tensor_tensor(out=hn[:], in0=self.xres[:, tt, :],
                                                                      scalar=self.stat[:, 16 + tt:17 + tt],
                                                                      in1=self.gbc[:], op0=ALU.mult, op1=ALU.mult),
                 reads=[self.xb[tt], self.statb, self.gbcb], writes=[self.hnb[k]])
            bk = 6 + k
            for dc in range(8):
                c.op("pe", lambda e, dc=dc, hn=hn, bk=bk: e.transpose(out=self.bank_bf(bk)[:, dc * 128:(dc + 1) * 128],
                                                                     in_=hn[:, dc * 128:(dc + 1) * 128],
                                                                     identity=self.ident[:]),
                     reads=[self.hnb[k], self.identb], writes=[self.pb[bk]])
            src = self.bank_bf(bk).rearrange("p (a b) -> p a b", a=8)
            c.op("act", lambda e, tt=tt, src=src: e.activation(out=self.hnT[:, :, tt * 128:(tt + 1) * 128], in_=src,
                                                              func=AF.Copy),
                 reads=[self.pb[bk]], writes=[self.hnTb[tt]])
        if free:
            nes.close()
        else:
            return nes
        return None

    def peer(self, l):
        c = self.c
        nc = self.nc
        es = ExitStack()
        self.es.enter_context(es)
        sb = lambda name, shape, dt: es.enter_context(nc.sbuf_tensor(name + "_%d" % l, list(shape), dt))
        dr = self.dr
        self.norm_to_hnT(dr["ffn_norm"][l], free=True)

        qT = sb("qT", [128, 8, S], BF16)
        qTb = [Buf() for _ in range(8)]
        thr = sb("thr", [128, NT, 8], F32)
        negc = sb("negc", [128, NT, 8], F32)
        thrb = [Buf() for _ in range(NT)]
        sk1T = sb("sk1T", [64, 8, 128], BF16)
        sk1Tb = Buf()
        KX = [sb("KX%d" % i, [128, 8, 512], BF16) for i in range(2)]
        KXlo = [Buf() for _ in range(2)]
        KXhi = Buf()

        with ExitStack() as pes:
            pb_ = lambda name, shape, dt: pes.enter_context(nc.sbuf_tensor(name + "_%d" % l, list(shape), dt))
            Wq = pb_("Wq", [128, 8, D], BF16)
            Wqb = Buf()
            c.dma("pool", "ld0", Wq[:], dr["peer_w_q"][l].rearrange("(dc p) f -> p dc f", p=128), writes=[Wqb])
            skn = pb_("skn", [128, 16, 64], BF16)
            sknb = Buf()
            c.dma("pool", "ld1", skn[:], dr["peer_sub_keys"][l].rearrange("h p n k -> n (h p) k"), writes=[sknb])
            SKBD = pb_("SKBD", [128, 8, 256], BF16)
            SKBDb = Buf()
            c.op("pool", lambda e: e.memset(SKBD[:], 0.0), writes=[SKBDb])
            for h in range(8):
                bk = h % 2
                c.op("pe", lambda e, h=h, bk=bk: e.transpose(out=self.bank_bf(bk)[:, 0:128],
                                                            in_=skn[:, 2 * h:2 * h + 2, :].rearrange("p a b -> p (a b)"),
                                                            identity=self.ident[:]),
                     reads=[sknb, self.identb], writes=[self.pb[bk]])
                c.op("dve", lambda e, h=h, bk=bk: e.tensor_copy(out=SKBD[0:64, h, 0:128], in_=self.bank_bf(bk)[0:64, 0:128]),
                     reads=[self.pb[bk]], writes=[SKBDb])
                c.op("dve", lambda e, h=h, bk=bk: e.tensor_copy(out=SKBD[64:128, h, 128:256],
                                                               in_=self.bank_bf(bk)[64:128, 0:128]),
                     reads=[self.pb[bk]], writes=[SKBDb])
                c.op("act", lambda e, h=h, bk=bk: e.activation(out=sk1T[:, h, :], in_=self.bank_bf(bk)[0:64, 0:128],
                                                              func=AF.Copy),
                     reads=[self.pb[bk]], writes=[sk1Tb])
                for kb in range(2):
                    c.op("act", lambda e, h=h, bk=bk, kb=kb: e.activation(
                        out=KX[kb][64:128, h, :].rearrange("p (a b) -> p a b", a=4),
                        in_=self.bank_bf(bk)[64:128, 0:128].unsqueeze(1).to_broadcast([64, 4, 128]), func=AF.Copy),
                         reads=[self.pb[bk]], writes=[KXhi])
            for h in range(8):
                for tb in range(4):
                    bk = 2 + (h * 4 + tb) % 4
                    for dc in range(8):
                        c.op("pe", lambda e, h=h, tb=tb, dc=dc, bk=bk: e.matmul(
                            self.bank(bk), lhsT=Wq[:, dc, h * 128:(h + 1) * 128], rhs=self.hnT[:, dc, tb * 512:(tb + 1) * 512],
                            start=(dc == 0), stop=(dc == 7)),
                             reads=[Wqb] + self.hnTb[tb * 4:tb * 4 + 4], writes=[self.pb[bk]])
                    eng = "act" if tb % 2 == 0 else "dve"
                    if eng == "act":
                        c.op("act", lambda e, h=h, tb=tb, bk=bk: e.activation(out=qT[:, h, tb * 512:(tb + 1) * 512],
                                                                             in_=self.bank(bk), func=AF.Copy),
                             reads=[self.pb[bk]], writes=[qTb[h]])
                    else:
                        c.op("dve", lambda e, h=h, tb=tb, bk=bk: e.tensor_copy(out=qT[:, h, tb * 512:(tb + 1) * 512],
                                                                              in_=self.bank(bk)),
                             reads=[self.pb[bk]], writes=[qTb[h]])
            SC = [pb_("SC%d" % i, [128, 16, 128], F32) for i in range(2)]
            SCb = [Buf() for _ in range(2)]
            TOP = pb_("TOP", [128, 16, 24], F32)
            TOPb = Buf()
            CAND = pb_("CAND", [128, 8, 289], F32)
            CANDb = Buf()
            CT = pb_("CT", [128, 8, 24], F32)
            CTb = Buf()
            sm = pb_("sm", [128, 8, 16], F32)
            smb = Buf()
            for tt in range(NT):
                k = tt % 2
                sc = SC[k]
                for h in range(8):
                    bk = 2 * k + h // 4
                    bk = 4 * k + h // 2
                    c.op("pe", lambda e, h=h, tt=tt, bk=bk: e.matmul(
                        self.ps[:, bk * 512 + (h % 2) * 256: bk * 512 + (h % 2) * 256 + 256],
                        lhsT=qT[:, h, tt * 128:(tt + 1) * 128], rhs=SKBD[:, h, :], start=True, stop=True),
                         reads=[qTb[h], SKBDb], writes=[self.pb[bk]])
                for q in range(4):
                    bk = 4 * k + q
                    c.op("act", lambda e, q=q, bk=bk, sc=sc: e.activation(
                        out=sc[:, 4 * q:4 * q + 4, :].rearrange("p a b -> p (a b)"), in_=self.bank(bk), func=AF.Copy),
                         reads=[self.pb[bk]], writes=[SCb[k]])
                for g in range(16):
                    c.op("dve", lambda e, g=g, sc=sc: e.max(out=TOP[:, g, 0:8], in_=sc[:, g, :]), reads=[SCb[k]], writes=[TOPb])
                    c.op("dve", lambda e, g=g, sc=sc: e.match_replace(out=sc[:, g, :], in_to_replace=TOP[:, g, 0:8],
                                                                     in_values=sc[:, g, :], imm_value=NEG),
                         reads=[TOPb, SCb[k]], writes=[SCb[k]])
                    c.op("dve", lambda e, g=g, sc=sc: e.max(out=TOP[:, g, 8:16], in_=sc[:, g, :]), reads=[SCb[k]], writes=[TOPb])
                    c.op("dve", lambda e, g=g, sc=sc: e.match_replace(out=sc[:, g, :], in_to_replace=TOP[:, g, 8:16],
                                                                     in_values=sc[:, g, :], imm_value=NEG),
                         reads=[TOPb, SCb[k]], writes=[SCb[k]])
                c.op("dve", lambda e, sc=sc: e.tensor_reduce(out=TOP[:, :, 16:17], in_=sc[:], axis=AX.X, op=ALU.max),
                     reads=[SCb[k]], writes=[TOPb])
                in0 = AP(TOP, 0, [[384, 128], [48, 8], [1, 17], [0, 17]])
                in1 = AP(TOP, 24, [[384, 128], [48, 8], [0, 17], [1, 17]])
                c.op("dve", lambda e, in0=in0, in1=in1: e.tensor_tensor(
                    out=CAND[:].rearrange("p h (a b) -> p h a b", a=17), in0=in0, in1=in1, op=ALU.add),
                     reads=[TOPb], writes=[CANDb])
                for h in range(8):
                    c.op("dve", lambda e, h=h: e.max(out=CT[:, h, 0:8], in_=CAND[:, h, :]), reads=[CANDb], writes=[CTb])
                    c.op("dve", lambda e, h=h: e.match_replace(out=CAND[:, h, :], in_to_replace=CT[:, h, 0:8],
                                                              in_values=CAND[:, h, :], imm_value=NEG),
                         reads=[CTb, CANDb], writes=[CANDb])
                    c.op("dve", lambda e, h=h: e.max(out=CT[:, h, 8:16], in_=CAND[:, h, :]), reads=[CANDb], writes=[CTb])
                    c.op("dve", lambda e, h=h: e.match_replace(out=CAND[:, h, :], in_to_replace=CT[:, h, 8:16],
                                                              in_values=CAND[:, h, :], imm_value=NEG),
                         reads=[CTb, CANDb], writes=[CANDb])
                c.op("dve", lambda e: e.tensor_reduce(out=CT[:, :, 16:17], in_=CAND[:], axis=AX.X, op=ALU.max),
                     reads=[CANDb], writes=[CTb])
                c.op("dve", lambda e, tt=tt: e.tensor_tensor(out=thr[:, tt, :].unsqueeze(2), in0=CT[:, :, 15:16],
                                                            in1=CT[:, :, 16:17], op=ALU.add),
                     reads=[CTb], writes=[thrb[tt]])
                c.op("dve", lambda e, tt=tt: e.tensor_scalar(out=thr[:, tt, :], in0=thr[:, tt, :], scalar1=0.5,
                                                            scalar2=None, op0=ALU.mult),
                     reads=[thrb[tt]], writes=[thrb[tt]])
                c.op("dve", lambda e: e.tensor_tensor(out=sm[:], in0=CT[:, :, 0:16],
                                                      in1=CT[:, :, 0:1].to_broadcast([128, 8, 16]), op=ALU.subtract),
                     reads=[CTb], writes=[smb])
                c.op("act", lambda e: e.activation(out=sm[:], in_=sm[:], func=AF.Exp), reads=[smb], writes=[smb])
                c.op("dve", lambda e: e.tensor_reduce(out=CT[:, :, 17:18], in_=sm[:], axis=AX.X, op=ALU.add),
                     reads=[smb], writes=[CTb])
                c.op("act", lambda e: e.activation(out=CT[:, :, 18:19], in_=CT[:, :, 17:18], func=AF.Ln),
                     reads=[CTb], writes=[CTb])
                c.op("dve", lambda e, tt=tt: e.scalar_tensor_tensor(out=negc[:, tt, :].unsqueeze(2), in0=CT[:, :, 18:19],
                                                                   scalar=-1.0, in1=CT[:, :, 0:1], op0=ALU.mult,
                                                                   op1=ALU.subtract),
                     reads=[CTb], writes=[thrb[tt]])
            c.barrier()
        if PEER_NEB == 0:
            c.barrier()
            es.close()
            return
        u = dr["peer_u"][l]
        v = dr["peer_v"][l]
        Ubf = sb("Ubf", [128, 4, D], BF16)
        Ubfb = Buf()
        UT1 = sb("UT", [128, 8, 512], BF16)
        UT = [UT1, UT1]
        UTb1 = Buf()
        UTb = [UTb1, UTb1]
        Vb = [sb("Vb%d" % i, [128, 4, D], BF16) for i in range(2)]
        Vbb = [Buf() for _ in range(2)]
        GA = sb("GA", [128, NT, 512], BF16)
        GAb = [Buf() for _ in range(NT)]
        NS = 3
        NE = 3
        E = [sb("E%d" % i, [128, 512], BF16) for i in range(NE)]
        Eb = [Buf() for _ in range(NE)]
        G = [sb("G%d" % i, [128, 512], BF16) for i in range(4)]
        Gb = [Buf() for _ in range(4)]
        PT = [sb("PT%d" % i, [128, 512], BF16) for i in range(2)]
        PTb = [Buf() for _ in range(2)]
        NEB = PEER_NEB
        NU = NEB * NT
        B_S = (0, 1, 2)
        B_AT = 3
        B_WT = (4, 5)
        B_O = (6, 7)

        def load_U(eb):
            c.dma("pool", "ld2", Ubf[:], u[eb * 512:(eb + 1) * 512, :].rearrange("(ec p) d -> p ec d", p=128),
                  writes=[Ubfb])

        def load_V(eb):
            c.dma("pool", "ld3" if eb % 2 == 0 else "ld4", Vb[eb % 2][:],
                  v[eb * 512:(eb + 1) * 512, :].rearrange("(ec p) d -> p ec d", p=128), writes=[Vbb[eb % 2]])

        def build_KX(eb):
            kx = KX[eb % 2]
            c.op("pool", lambda e: e.tensor_copy(
                out=kx[0:64, :, :].rearrange("p h (a b) -> p h a b", a=4),
                in_=sk1T[:, :, eb * 4:eb * 4 + 4].unsqueeze(3).broadcast_to([64, 8, 4, 128])),
                 reads=[sk1Tb], writes=[KXlo[eb % 2]])

        def transp_eb(eb):
            ut = UT[eb % 2]
            for dcp in range(4):
                for jj in range(2):
                    dc = dcp * 2 + jj
                    for ec in range(4):
                        c.op("pe", lambda e: e.transpose(
                            out=self.bank_bf(B_AT)[:, jj * 512 + ec * 128: jj * 512 + (ec + 1) * 128],
                            in_=Ubf[:, ec, dc * 128:(dc + 1) * 128], identity=self.ident[:]),
                             reads=[Ubfb, self.identb], writes=[self.pb[B_AT]])
                c.op("act", lambda e: e.activation(
                    out=ut[:, 2 * dcp:2 * dcp + 2, :].rearrange("p a b -> p (a b)"), in_=self.bank_bf(B_AT), func=AF.Copy),
                     reads=[self.pb[B_AT]], writes=[UTb[eb % 2]])

        def emit_AT(eb, tt, only=None):
            ut = UT[eb % 2]
            for ec in (range(4) if only is None else (only,)):
                for dc in range(8):
                    c.op("pe", lambda e: e.matmul(
                        self.ps[:, B_AT * 512 + ec * 128: B_AT * 512 + (ec + 1) * 128],
                        lhsT=ut[:, dc, ec * 128:(ec + 1) * 128], rhs=self.hnT[:, dc, tt * 128:(tt + 1) * 128],
                        start=(dc == 0), stop=(dc == 7)),
                         reads=[UTb[eb % 2], self.hnTb[tt]], writes=[self.pb[B_AT]])

        def emit_ATq(eb, tt, j):
            ut = UT[eb % 2]
            ec = j // 2
            for dc in range((j % 2) * 4, (j % 2) * 4 + 4):
                c.op("pe", lambda e: e.matmul(
                    self.ps[:, B_AT * 512 + ec * 128: B_AT * 512 + (ec + 1) * 128],
                    lhsT=ut[:, dc, ec * 128:(ec + 1) * 128], rhs=self.hnT[:, dc, tt * 128:(tt + 1) * 128],
                    start=(dc == 0), stop=(dc == 7)),
                     reads=[UTb[eb % 2], self.hnTb[tt]], writes=[self.pb[B_AT]])

        def emit_V1(uu, i):
            eb, tt = uu // NT, uu % NT
            vb = Vb[eb % 2]
            dh, ec = i // 4, i % 4
            c.op("pe", lambda e: e.matmul(self.bank(B_O[dh]), lhsT=PT[uu % 2][:, ec * 128:(ec + 1) * 128],
                                          rhs=vb[:, ec, dh * 512:(dh + 1) * 512], start=(ec == 0), stop=(ec == 3)),
                 reads=[PTb[uu % 2], Vbb[eb % 2]], writes=[self.pb[B_O[dh]]])

        def emit_Acopy(tt):
            c.op("act", lambda e: e.activation(out=GA[:, tt, :], in_=self.bank(B_AT), func=AF.Copy),
                 reads=[self.pb[B_AT]], writes=[GAb[tt]])

        def emit_gelu_batch():
            for tt in range(NT):
                c.op("act", lambda e: e.activation(out=GA[:, tt, :], in_=GA[:, tt, :], func=self.gelu_func),
                     reads=[GAb[tt]], writes=[GAb[tt]])

        def unit_of(n):
            uu = n // 8
            return uu, uu // NT, uu % NT, n % 8

        def emit_S(n):
            uu, eb, tt, h = unit_of(n)
            bk = B_S[n % NS]
            c.op("pe", lambda e: e.matmul(self.bank(bk), lhsT=qT[:, h, tt * 128:(tt + 1) * 128], rhs=KX[eb % 2][:, h, :],
                                          start=True, stop=True),
                 reads=[qTb[h], KXlo[eb % 2], KXhi], writes=[self.pb[bk]])

        def emit_EG(n):
            uu, eb, tt, h = unit_of(n)
            bk = B_S[n % NS]
            e2 = n % NE
            g4 = n % 4
            c.op("act", lambda e: e.activation(out=E[e2][:], in_=self.bank(bk), func=AF.Exp, bias=negc[:, tt, h:h + 1]),
                 reads=[self.pb[bk], thrb[tt]], writes=[Eb[e2]])
            c.op("dve", lambda e: e.scalar_tensor_tensor(out=G[g4][:], in0=self.bank(bk), scalar=thr[:, tt, h:h + 1],
                                                         in1=E[e2][:], op0=ALU.is_gt, op1=ALU.mult),
                 reads=[self.pb[bk], thrb[tt], Eb[e2]], writes=[Gb[g4]])

        def emit_T(n):
            uu, eb, tt, h = unit_of(n)
            wtb = B_WT[uu % 2]
            g4 = n % 4
            for ec in range(4):
                c.op("pe", lambda e: e.matmul(self.ps[:, wtb * 512 + ec * 128: wtb * 512 + (ec + 1) * 128],
                                              lhsT=G[g4][:, ec * 128:(ec + 1) * 128], rhs=self.ident[:],
                                              start=(h == 0 and ec == 0), stop=(h == 7), skip_group_check=True),
                     reads=[Gb[g4], self.identb], writes=[self.pb[wtb]])

        def emit_PT(uu):
            eb, tt = uu // NT, uu % NT
            wtb = B_WT[uu % 2]
            c.op("dve", lambda e: e.tensor_tensor(out=PT[uu % 2][:], in0=self.bank(wtb), in1=GA[:, tt, :], op=ALU.mult),
                 reads=[self.pb[wtb], GAb[tt]], writes=[PTb[uu % 2]])

        def emit_V(uu, only=None):
            eb, tt = uu // NT, uu % NT
            vb = Vb[eb % 2]
            for dh in (range(2) if only is None else (only,)):
                for ec in range(4):
                    c.op("pe", lambda e: e.matmul(self.bank(B_O[dh]), lhsT=PT[uu % 2][:, ec * 128:(ec + 1) * 128],
                                                  rhs=vb[:, ec, dh * 512:(dh + 1) * 512], start=(ec == 0), stop=(ec == 3)),
                         reads=[PTb[uu % 2], Vbb[eb % 2]], writes=[self.pb[B_O[dh]]])

        def emit_xacc(uu, only=None):
            eb, tt = uu // NT, uu % NT
            for dh in (range(2) if only is None else (only,)):
                xs = self.xres[:, tt, dh * 512:(dh + 1) * 512]
                c.op("dve", lambda e: e.tensor_tensor(out=xs, in0=self.bank(B_O[dh]), in1=xs, op=ALU.add),
                     reads=[self.pb[B_O[dh]], self.xb[tt]], writes=[self.xb[tt]])

        if NEB == 0:
            c.barrier()
            es.close()
            return
        load_U(0)
        load_V(0)
        build_KX(0)
        transp_eb(0)
        for tt in range(NT):
            emit_AT(0, tt)
            emit_Acopy(tt)
        emit_gelu_batch()
        load_U(1)
        transp_eb(1)
        NH = NU * 8
        LT = 3
        TAILS = [(LT, "PT", 0)] + [(4 + i, "V", i) for i in range(8)] + [(8, "X", 0), (12, "X", 1)] + \
                [(3 + jj, "AT", jj) for jj in range(8)] + [(11, "AC", 0)]
        TAILS.sort(key=lambda t: -t[0])
        for k in range(min(NH + 16, PEER_MAXSTEPS)):
            if k < NH:
                uu, eb, tt, h = unit_of(k)
                if h == 0 and tt == 2:
                    if eb + 1 < NEB:
                        load_V(eb + 1)
                        build_KX(eb + 1)
                    if eb + 2 < NEB:
                        load_U(eb + 2)
                emit_S(k)
                emit_EG(k)
            if 0 <= k - LT < NH:
                emit_T(k - LT)
            for (lag, what, arg) in TAILS:
                n = k - lag
                if not (n >= 7 and n < NH and n % 8 == 7):
                    continue
                uu = n // 8
                eb, tt = uu // NT, uu % NT
                if what == "PT":
                    emit_PT(uu)
                elif what == "V":
                    emit_V1(uu, arg)
                elif what == "X":
                    emit_xacc(uu, arg)
                elif what == "AT":
                    if eb + 1 < NEB:
                        emit_ATq(eb + 1, tt, arg)
                else:
                    if eb + 1 < NEB:
                        emit_Acopy(tt)
                        if tt == NT - 1:
                            if eb + 2 < NEB:
                                transp_eb(eb + 2)
                            emit_gelu_batch()
        c.barrier()
        es.close()

    gelu_func = AF.Gelu

    def ring_open(self, es, nslots, tag):
        self.ring = [es.enter_context(self.nc.sbuf_tensor("ring%s_%d" % (tag, i), [128, 8, 256], BF16)) for i in range(nslots)]
        self.ringb = [Buf() for _ in range(nslots)]
        self.ringsem = []
        for i in range(nslots):
            self.ringsem.append("rg%d" % i)
        self.ringi = 0

    def wload_cols(self, w2d, cols):
        i = self.ringi % len(self.ring)
        self.ringi += 1
        t, b = self.ring[i], self.ringb[i]
        off = 0
        for (c0, n) in cols:
            self.c.dma("pool", self.ringsem[i], t[:, :, off:off + n],
                       w2d[:, c0:c0 + n].rearrange("(dc p) f -> p dc f", p=128), writes=[b])
            off += n
        return t, b

    def wload_rows(self, w2d, row0):
        i = self.ringi % len(self.ring)
        self.ringi += 1
        t, b = self.ring[i], self.ringb[i]
        v = t[:].rearrange("p a b -> p (a b)").rearrange("p (a b) -> p a b", a=2)
        self.c.dma("pool", self.ringsem[i], v, w2d[row0:row0 + 256, :].rearrange("(a p) d -> p a d", p=128), writes=[b])
        return v, b

    def even_mixer(self, j=0):
        c = self.c
        nc = self.nc
        dr = self.dr
        es = ExitStack()
        self.es.enter_context(es)
        sb = lambda name, shape, dt: es.enter_context(nc.sbuf_tensor(name + "_e%d" % j, list(shape), dt))
        es.enter_context(self.norm_to_hnT(dr["mix_norm"][2 * j]))
        win = dr["even_w_in"][j]
        wout = dr["even_w_out"][j]
        self.ring_open(es, 6, "e")
        PB = self.pb

        prm = sb("prm", [128, 8, 8], F32)
        prmb = Buf()
        kw = dict(allow_slow_non_contiguous=True)
        for k in range(4):
            c.dma("sp", "ld0", prm[:, :, k:k + 1], dr["lru_conv_w"][j][k].rearrange("(c p o) -> p c o", p=128, o=1),
                  writes=[prmb], **kw)
        for idx, nm in ((4, "lru_conv_b"), (5, "lru_b_a"), (6, "lru_b_i"), (7, "lru_lambda")):
            c.dma("sp", "ld0", prm[:, :, idx:idx + 1], dr[nm][j].rearrange("(c p o) -> p c o", p=128, o=1), writes=[prmb], **kw)
        c.op("act", lambda e: e.activation(out=prm[:, :, 7:8], in_=prm[:, :, 7:8], func=AF.Exp, scale=-1.0), reads=[prmb], writes=[prmb])
        c.op("act", lambda e: e.activation(out=prm[:, :, 7:8], in_=prm[:, :, 7:8], func=AF.Ln, bias=1.0), reads=[prmb], writes=[prmb])
        c.op("dve", lambda e: e.tensor_scalar(out=prm[:, :, 7:8], in0=prm[:, :, 7:8], scalar1=-8.0, scalar2=None, op0=ALU.mult),
             reads=[prmb], writes=[prmb])
        WBD = sb("WBD", [128, 2, 8, 128], BF16)
        WBDb = Buf()
        c.op("pool", lambda e: e.memset(WBD[:], 0.0), writes=[WBDb])
        for gi, nm in enumerate(("lru_w_a", "lru_w_i")):
            for hl in range(2):
                c.dma("pool", "ld1", WBD[hl * 64:(hl + 1) * 64, gi, :, hl * 64:(hl + 1) * 64],
                      dr[nm][j].rearrange("(c two) i j -> two i c j", two=2)[hl], writes=[WBDb])

        YT = sb("YT", [128, 8, S], BF16)
        YTb = [Buf() for _ in range(8)]
        with ExitStack() as les:
            lb = lambda name, shape, dt: les.enter_context(nc.sbuf_tensor(name + "_e%d" % j, list(shape), dt))
            T0 = lb("T0", [128, 3 + S], F32)
            T0b = Buf()
            T = [lb("T%d" % i, [128, 1024], F32) for i in range(1, 5)]
            Tb = [Buf() for _ in range(4)]
            XCb = lb("XCb", [128, 1024], BF16)
            XCbb = Buf()
            GAg = lb("GAg", [128, 1024], BF16)
            GAgb = Buf()
            hl_ = lb("hlast", [128, 2], F32)
            hlb = Buf()
            c.op("pool", lambda e: e.memset(T0[:, 0:3], 0.0), writes=[T0b])
            for ch in range(8):
                wt, wb = self.wload_cols(win, [(ch * 128, 128), (1024 + ch * 128, 128)])
                for hf in range(2):
                    t0 = hf * 1024
                    for q in range(2):
                        for dc in range(8):
                            c.op("pe", lambda e: e.matmul(self.bank(q), lhsT=wt[:, dc, 128:256],
                                                          rhs=self.hnT[:, dc, t0 + q * 512:t0 + (q + 1) * 512],
                                                          start=(dc == 0), stop=(dc == 7)),
                                 reads=[wb] + self.hnTb[hf * 8 + q * 4: hf * 8 + q * 4 + 4], writes=[PB[q]])
                    for q in range(2):
                        for dc in range(8):
                            c.op("pe", lambda e: e.matmul(self.bank(2 + q), lhsT=wt[:, dc, 0:128],
                                                          rhs=self.hnT[:, dc, t0 + q * 512:t0 + (q + 1) * 512],
                                                          start=(dc == 0), stop=(dc == 7)),
                                 reads=[wb] + self.hnTb[hf * 8 + q * 4: hf * 8 + q * 4 + 4], writes=[PB[2 + q]])
                    c.op("act", lambda e: e.activation(out=T0[:, 3 + t0:3 + t0 + 1024], in_=self.ps[:, 0:1024], func=AF.Copy),
                         reads=[PB[0], PB[1]], writes=[T0b])
                    XC = T[0]
                    c.op("dve", lambda e: e.tensor_scalar(out=XC[:], in0=T0[:, 3 + t0:3 + t0 + 1024], scalar1=prm[:, ch, 3:4],
                                                          scalar2=prm[:, ch, 4:5], op0=ALU.mult, op1=ALU.add),
                         reads=[T0b, prmb], writes=[Tb[0]])
                    for k in range(3):
                        c.op("dve", lambda e: e.scalar_tensor_tensor(out=XC[:], in0=T0[:, k + t0:k + t0 + 1024],
                                                                     scalar=prm[:, ch, k:k + 1], in1=XC[:], op0=ALU.mult,
                                                                     op1=ALU.add), reads=[T0b, prmb, Tb[0]], writes=[Tb[0]])
                    c.op("pool", lambda e: e.tensor_copy(out=XCb[:], in_=XC[:]), reads=[Tb[0]], writes=[XCbb])
                    for gi in range(2):
                        for q in range(2):
                            bk = 4 + gi * 2 + q
                            c.op("pe", lambda e: e.matmul(self.bank(bk), lhsT=WBD[:, gi, ch, :], rhs=XCb[:, q * 512:(q + 1) * 512],
                                                          start=True, stop=True), reads=[WBDb, XCbb], writes=[PB[bk]])
                    A, M, IS = T[1], T[2], T[3]
                    c.op("act", lambda e: e.activation(out=A[:], in_=self.ps[:, 2048:3072], func=AF.Sigmoid, bias=prm[:, ch, 5:6]),
                         reads=[PB[4], PB[5], prmb], writes=[Tb[1]])
                    c.op("act", lambda e: e.activation(out=IS[:], in_=self.ps[:, 3072:4096], func=AF.Sigmoid, bias=prm[:, ch, 6:7]),
                         reads=[PB[6], PB[7], prmb], writes=[Tb[3]])
                    c.op("act", lambda e: e.activation(out=A[:], in_=A[:], func=AF.Exp, scale=prm[:, ch, 7:8]),
                         reads=[Tb[1], prmb], writes=[Tb[1]])
                    c.op("act", lambda e: e.activation(out=M[:], in_=A[:], func=AF.Square), reads=[Tb[1]], writes=[Tb[2]])
                    c.op("act", lambda e: e.activation(out=M[:], in_=M[:], func=AF.Sqrt, scale=-1.0, bias=1.0), reads=[Tb[2]], writes=[Tb[2]])
                    c.op("dve", lambda e: e.tensor_tensor(out=IS[:], in0=IS[:], in1=M[:], op=ALU.mult), reads=[Tb[3], Tb[2]], writes=[Tb[3]])
                    c.op("dve", lambda e: e.tensor_tensor(out=IS[:], in0=IS[:], in1=XC[:], op=ALU.mult), reads=[Tb[3], Tb[0]], writes=[Tb[3]])
                    init = 0.0 if hf == 0 else hl_[:, 0:1]
                    c.op("dve", lambda e: e.tensor_tensor_scan(out=M[:], data0=A[:], data1=IS[:], initial=init, op0=ALU.mult,
                                                               op1=ALU.add), reads=[Tb[1], Tb[3], hlb], writes=[Tb[2]])
                    if hf == 0:
                        c.op("dve", lambda e: e.tensor_copy(out=hl_[:, 0:1], in_=M[:, 1023:1024]), reads=[Tb[2]], writes=[hlb])
                    c.op("act", lambda e: e.activation(out=GAg[:], in_=self.ps[:, 1024:2048], func=self.gelu_func),
                         reads=[PB[2], PB[3]], writes=[GAgb])
                    c.op("dve", lambda e: e.tensor_tensor(out=YT[:, ch, t0:t0 + 1024], in0=M[:], in1=GAg[:], op=ALU.mult),
                         reads=[Tb[2], GAgb], writes=[YTb[ch]])
        wo = [self.wload_rows(wout, r * 256) for r in range(4)]
        self.out_proj(YT, YTb, wo, 8)
        c.barrier()
        es.close()
        self.retention(j)

    def retention(self, j):
        import math
        c = self.c
        nc = self.nc
        dr = self.dr
        PB = self.pb
        win = dr["even_w_in"][j]
        wout = dr["even_w_out"][j]
        es = ExitStack()
        self.es.enter_context(es)
        rb = lambda name, shape, dt: es.enter_context(nc.sbuf_tensor(name + "_r%d" % j, list(shape), dt))
        self.ring_open(es, 6, "r")
        COS = rb("COS", [128, S], F32)
        SIN = rb("SIN", [128, S], F32)
        tabb = Buf()
        DB = rb("DB", [128, 4, 128], F32)
        DM = rb("DM", [128, 4, 128], F32)
        decb = Buf()
        PI = math.pi
        with ExitStack() as tes:
            tb_ = lambda name, shape, dt: tes.enter_context(nc.sbuf_tensor(name + "_r%d" % j, list(shape), dt))
            invf = tb_("invf", [128, 4], F32)
            ANG = tb_("ANG", [128, S], F32)
            KI = tb_("KI", [128, S], mybir.dt.int32)
            KF = tb_("KF", [128, S], F32)
            tmb = Buf()
            c.op("pool", lambda e: e.iota(out=invf[:, 0:1], pattern=[[0, 1]], base=0, channel_multiplier=1,
                                          allow_small_or_imprecise_dtypes=True), writes=[tmb])
            c.op("dve", lambda e: e.tensor_scalar(out=invf[:, 1:2], in0=invf[:, 0:1], scalar1=-1.0 / 128, scalar2=None,
                                                  op0=ALU.mult), reads=[tmb], writes=[tmb])
            c.op("pool", lambda e: e.memset(invf[:, 2:3], 10000.0), reads=[tmb], writes=[tmb])
            c.op("pool", lambda e: e.tensor_tensor(out=invf[:, 3:4], in0=invf[:, 2:3], in1=invf[:, 1:2], op=ALU.pow),
                 reads=[tmb], writes=[tmb])
            c.op("pool", lambda e: e.iota(out=ANG[:], pattern=[[1, S]], base=0, channel_multiplier=0,
                                          allow_small_or_imprecise_dtypes=True), reads=[tmb], writes=[tmb])
            c.op("dve", lambda e: e.tensor_scalar(out=ANG[:], in0=ANG[:], scalar1=invf[:, 3:4], scalar2=None, op0=ALU.mult),
                 reads=[tmb], writes=[tmb])
            for dst, shift in ((SIN, 0.0), (COS, PI / 2)):
                c.op("dve", lambda e: e.tensor_scalar(out=KF[:], in0=ANG[:], scalar1=shift + PI, scalar2=1.0 / (2 * PI),
                                                      op0=ALU.add, op1=ALU.mult), reads=[tmb], writes=[tmb])
                c.op("dve", lambda e: e.tensor_copy(out=KI[:], in_=KF[:]), reads=[tmb], writes=[tmb])
                c.op("dve", lambda e: e.tensor_copy(out=KF[:], in_=KI[:]), reads=[tmb], writes=[tmb])
                c.op("dve", lambda e: e.scalar_tensor_tensor(out=KF[:], in0=KF[:], scalar=-2 * PI, in1=ANG[:], op0=ALU.mult,
                                                             op1=ALU.add), reads=[tmb], writes=[tmb])
                if shift != 0.0:
                    c.op("dve", lambda e: e.tensor_scalar(out=KF[:], in0=KF[:], scalar1=shift, scalar2=None, op0=ALU.add),
                         reads=[tmb], writes=[tmb])
                c.op("dve", lambda e: e.tensor_scalar(out=dst[:], in0=KF[:], scalar1=-PI, scalar2=2 * PI, op0=ALU.is_lt,
                                                      op1=ALU.mult), reads=[tmb], writes=[tabb])
                c.op("dve", lambda e: e.tensor_tensor(out=dst[:], in0=dst[:], in1=KF[:], op=ALU.add), reads=[tmb, tabb], writes=[tabb])
                c.op("dve", lambda e: e.tensor_scalar(out=dst[:], in0=dst[:], scalar1=-3.14159, scalar2=3.14159, op0=ALU.max,
                                                      op1=ALU.min), reads=[tabb], writes=[tabb])
                c.op("act", lambda e: e.activation(out=dst[:], in_=dst[:], func=AF.Sin), reads=[tabb], writes=[tabb])
            c.op("pool", lambda e: e.iota(out=KF[:, 0:128], pattern=[[1, 128]], base=0, channel_multiplier=-1,
                                          allow_small_or_imprecise_dtypes=True), reads=[tmb], writes=[tmb])
            for h in range(4):
                lg = math.log1p(-2.0 ** (-5 - h))
                c.op("act", lambda e: e.activation(out=DB[:, h, :], in_=KF[:, 0:128], func=AF.Exp, scale=lg), reads=[tmb], writes=[decb])
                c.op("pool", lambda e: e.affine_select(out=DM[:, h, :], in_=DB[:, h, :], pattern=[[1, 128]], compare_op=ALU.is_ge,
                                                       fill=0.0, base=0, channel_multiplier=-1), reads=[decb], writes=[decb])
            c.barrier()
        qT = rb("qTr", [128, 2, S], BF16)
        kT = rb("kTr", [128, 2, S], BF16)
        qkb = [Buf(), Buf()]
        VH = rb("VH", [128, NT, 256], BF16)
        VHb = Buf()
        YBT = rb("YBT", [128, 2, S], BF16)
        YBTb = [Buf(), Buf()]
        R = [rb("R%d" % i, [128, 512], F32) for i in range(4)]
        Rb = [Buf() for _ in range(4)]
        SC = [rb("SCr%d" % i, [128, 128], BF16) for i in range(4)]
        SCb = [Buf() for _ in range(4)]
        SG = [rb("SG%d" % i, [128, 256], F32) for i in range(2)]
        SGb = [Buf() for _ in range(2)]
        yb = [rb("yb%d" % i, [128, 256], BF16) for i in range(2)]
        ybb = [Buf() for _ in range(2)]
        junk = rb("junkr", [128, 256], BF16)
        junkb = Buf()
        ssr = rb("ssr", [128, 4], F32)
        ssrb = Buf()
        sreg = [Buf() for _ in range(8)]
        cnt = 0
        pj = 0
        for h in range(4):
            lg = math.log1p(-2.0 ** (-5 - h))
            wq = self.wload_cols(win, [(2048 + h * 256, 256)])
            wk = self.wload_cols(win, [(3072 + h * 256, 256)])
            wv = self.wload_cols(win, [(4096 + h * 256, 256)])
            wg = self.wload_cols(win, [(5120 + h * 256, 256)])
            wo = self.wload_rows(wout, 1024 + h * 256)
            for jq, ((wt, wb), dst) in enumerate(((wq, qT), (wk, kT))):
                for tb in range(4):
                    ba, bb = (4, 5) if pj % 2 == 0 else (6, 7)
                    pj += 1
                    ts = slice(tb * 512, (tb + 1) * 512)
                    for half, bk in ((0, ba), (1, bb)):
                        for dc in range(8):
                            c.op("pe", lambda e: e.matmul(self.bank(bk), lhsT=wt[:, dc, half * 128:(half + 1) * 128],
                                                          rhs=self.hnT[:, dc, ts], start=(dc == 0), stop=(dc == 7)),
                                 reads=[wb] + self.hnTb[tb * 4:tb * 4 + 4], writes=[PB[bk]])
                    c.op("dve", lambda e: e.tensor_tensor(out=R[0][:], in0=self.bank(ba), in1=COS[:, ts], op=ALU.mult),
                         reads=[PB[ba], tabb], writes=[Rb[0]])
                    c.op("dve", lambda e: e.tensor_tensor(out=R[1][:], in0=self.bank(bb), in1=SIN[:, ts], op=ALU.mult),
                         reads=[PB[bb], tabb], writes=[Rb[1]])
                    c.op("pool", lambda e: e.tensor_tensor(out=dst[:, 0, ts], in0=R[0][:], in1=R[1][:], op=ALU.subtract),
                         reads=[Rb[0], Rb[1]], writes=[qkb[jq]])
                    c.op("dve", lambda e: e.tensor_tensor(out=R[2][:], in0=self.bank(ba), in1=SIN[:, ts], op=ALU.mult),
                         reads=[PB[ba], tabb], writes=[Rb[2]])
                    c.op("dve", lambda e: e.tensor_tensor(out=R[3][:], in0=self.bank(bb), in1=COS[:, ts], op=ALU.mult),
                         reads=[PB[bb], tabb], writes=[Rb[3]])
                    c.op("pool", lambda e: e.tensor_tensor(out=dst[:, 1, ts], in0=R[2][:], in1=R[3][:], op=ALU.add),
                         reads=[Rb[2], Rb[3]], writes=[qkb[jq]])
            for tt in range(NT):
                bk = 4 + (tt // 2) % 4
                reg = self.ps[:, bk * 512 + (tt % 2) * 256: bk * 512 + (tt % 2) * 256 + 256]
                for dc in range(8):
                    c.op("pe", lambda e: e.matmul(reg, lhsT=self.hnT[:, dc, tt * 128:(tt + 1) * 128], rhs=wv[0][:, dc, :],
                                                  start=(dc == 0), stop=(dc == 7)), reads=[wv[1], self.hnTb[tt]], writes=[PB[bk]])
                c.op("act", lambda e: e.activation(out=VH[:, tt, :], in_=reg, func=AF.Copy), reads=[PB[bk]], writes=[VHb])
            for ci in range(NT):
                k2 = ci % 2
                ybk = 2 + k2
                Y = self.ps[:, ybk * 512: ybk * 512 + 256]
                GP = self.ps[:, ybk * 512 + 256: ybk * 512 + 512]
                for cj in range(ci + 1):
                    r = cnt % 8
                    cnt += 1
                    st = self.ps[:, r * 128:(r + 1) * 128]
                    for dch in range(2):
                        c.op("pe", lambda e: e.matmul(st, lhsT=kT[:, dch, cj * 128:(cj + 1) * 128],
                                                      rhs=qT[:, dch, ci * 128:(ci + 1) * 128], start=(dch == 0), stop=(dch == 1)),
                             reads=[qkb[0], qkb[1]], writes=[sreg[r]])
                    const = math.exp(lg * 128.0 * (ci - cj)) / 16.0
                    dmat = DM[:, h, :] if cj == ci else DB[:, h, :]
                    sc = SC[r % 4]
                    c.op("dve", lambda e: e.scalar_tensor_tensor(out=sc[:], in0=st, scalar=const, in1=dmat, op0=ALU.mult,
                                                                 op1=ALU.mult), reads=[sreg[r], decb], writes=[SCb[r % 4]])
                    c.op("pe", lambda e: e.matmul(Y, lhsT=sc[:], rhs=VH[:, cj, :], start=(cj == 0), stop=(cj == ci)),
                         reads=[SCb[r % 4], VHb], writes=[PB[ybk]])
                for dc in range(8):
                    c.op("pe", lambda e: e.matmul(GP, lhsT=self.hnT[:, dc, ci * 128:(ci + 1) * 128], rhs=wg[0][:, dc, :],
                                                  start=(dc == 0), stop=(dc == 7), skip_group_check=True),
                         reads=[wg[1], self.hnTb[ci]], writes=[PB[ybk]])
                c.op("act", lambda e: e.activation(out=SG[k2][:], in_=GP, func=AF.Silu), reads=[PB[ybk]], writes=[SGb[k2]])
                c.op("act", lambda e: e.activation(out=junk[:], in_=Y, func=AF.Square, accum_out=ssr[:, 0:1]),
                     reads=[PB[ybk]], writes=[junkb, ssrb])
                c.op("act", lambda e: e.activation(out=ssr[:, 1:2], in_=ssr[:, 0:1], func=AF.Sqrt, bias=EPS, scale=1.0 / 256),
                     reads=[ssrb], writes=[ssrb])
                c.op("dve", lambda e: e.reciprocal(out=ssr[:, 2:3], in_=ssr[:, 1:2]), reads=[ssrb], writes=[ssrb])
                c.op("dve", lambda e: e.scalar_tensor_tensor(out=yb[k2][:], in0=Y, scalar=ssr[:, 2:3], in1=SG[k2][:],
                                                             op0=ALU.mult, op1=ALU.mult),
                     reads=[PB[ybk], ssrb, SGb[k2]], writes=[ybb[k2]])
                tbk = 6 + k2
                for eh in range(2):
                    c.op("pe", lambda e: e.transpose(out=self.bank_bf(tbk)[:, eh * 128:(eh + 1) * 128],
                                                     in_=yb[k2][:, eh * 128:(eh + 1) * 128], identity=self.ident[:]),
                         reads=[ybb[k2], self.identb], writes=[PB[tbk]])
                c.op("act", lambda e: e.activation(out=YBT[:, :, ci * 128:(ci + 1) * 128],
                                                   in_=self.bank_bf(tbk)[:, 0:256].rearrange("p (a b) -> p a b", a=2), func=AF.Copy),
                     reads=[PB[tbk]], writes=[YBTb[0], YBTb[1]])
            self.out_proj(YBT, YBTb, [wo], 2)
        c.barrier()
        es.close()

    def out_proj(self, YT, YTb, wo, nchunks, rstd=None, rstdb=None):
        c = self.c
        for tt in range(NT):
            for dh in range(2):
                bk = (tt * 2 + dh) % 4
                for ch in range(nchunks):
                    wv, wb = wo[ch // 2]
                    c.op("pe", lambda e: e.matmul(self.bank(bk), lhsT=YT[:, ch, tt * 128:(tt + 1) * 128],
                                                  rhs=wv[:, ch % 2, dh * 512:(dh + 1) * 512], start=(ch == 0),
                                                  stop=(ch == nchunks - 1)), reads=[YTb[ch], wb], writes=[self.pb[bk]])
                xs = self.xres[:, tt, dh * 512:(dh + 1) * 512]
                if rstd is None:
                    c.op("dve", lambda e: e.tensor_tensor(out=xs, in0=self.bank(bk), in1=xs, op=ALU.add),
                         reads=[self.pb[bk], self.xb[tt]], writes=[self.xb[tt]])
                else:
                    c.op("dve", lambda e: e.scalar_tensor_tensor(out=xs, in0=self.bank(bk), scalar=rstd[:, tt:tt + 1], in1=xs,
                                                                 op0=ALU.mult, op1=ALU.add),
                         reads=[self.pb[bk], self.xb[tt], rstdb], writes=[self.xb[tt]])

    def odd_mixer(self, j=0):
        c = self.c
        nc = self.nc
        dr = self.dr
        PB = self.pb
        win = dr["ssm_w_in"][j]
        wout = dr["ssm_w_out"][j]
        ygs = dr["ygs"]
        es = ExitStack()
        self.es.enter_context(es)
        ob = lambda name, shape, dt: es.enter_context(nc.sbuf_tensor(name + "_o%d" % j, list(shape), dt))
        es.enter_context(self.norm_to_hnT(dr["mix_norm"][2 * j + 1]))
        self.ring_open(es, 4, "o")
        kw = dict(allow_slow_non_contiguous=True)
        TRI = ob("TRI", [128, 128], F32)
        ONES = ob("ONES", [128, 128], F32)
        MNEG = ob("MNEG", [128, 128], F32)
        cstb = Buf()
        c.op("pool", lambda e: e.memset(ONES[:], 1.0), writes=[cstb])
        c.op("pool", lambda e: e.affine_select(out=TRI[:], in_=ONES[:], pattern=[[1, 128]], compare_op=ALU.is_ge, fill=0.0,
                                               base=0, channel_multiplier=-1), reads=[cstb], writes=[cstb])
        c.op("pool", lambda e: e.memset(MNEG[:], 0.0), reads=[cstb], writes=[cstb])
        c.op("pool", lambda e: e.affine_select(out=MNEG[:], in_=MNEG[:], pattern=[[1, 128]], compare_op=ALU.is_ge, fill=NEG,
                                               base=0, channel_multiplier=-1), reads=[cstb], writes=[cstb])
        prm = ob("prm", [128, 24, 8], F32)
        prmb = Buf()
        for k in range(4):
            c.dma("sp", "ld0", prm[:, :, k:k + 1], dr["ssm_conv_w"][j][k].rearrange("(c p o) -> p c o", p=128, o=1), writes=[prmb], **kw)
        c.dma("sp", "ld0", prm[:, :, 4:5], dr["ssm_conv_b"][j].rearrange("(c p o) -> p c o", p=128, o=1), writes=[prmb], **kw)
        GN = ob("GN", [128, 16], F32)
        c.dma("sp", "ld0", GN[:].unsqueeze(2), dr["ssm_norm"][j].rearrange("(c p o) -> p c o", p=128, o=1), writes=[prmb], **kw)
        hp = ob("hp", [128, 4, 32], F32)
        hpb = Buf()
        for idx, nm in ((0, "ssm_dt_bias"), (1, "ssm_a_log"), (2, "ssm_d")):
            c.dma("sp", "ld1", hp[:, idx, :], dr[nm][j].partition_broadcast(128), writes=[hpb])
        c.op("act", lambda e: e.activation(out=hp[:, 1, :], in_=hp[:, 1, :], func=AF.Exp), reads=[hpb], writes=[hpb])
        c.op("dve", lambda e: e.tensor_scalar(out=hp[:, 1, :], in0=hp[:, 1, :], scalar1=-1.0, scalar2=None, op0=ALU.mult),
             reads=[hpb], writes=[hpb])
        fam = {n: ob(n, [128, NT, 32], F32) for n in ("DT", "ADT", "ACUM", "NACUM", "EA", "DEC", "CD")}
        famb = Buf()
        SS = ob("SS", [128, NT, 4], F32)
        SSb = Buf()
        wdt, wdtb = self.wload_cols(win, [(5120, 32)])
        for tt in range(NT):
            for dc in range(8):
                c.op("pe", lambda e: e.matmul(self.ps[:, tt * 32:(tt + 1) * 32], lhsT=self.hnT[:, dc, tt * 128:(tt + 1) * 128],
                                              rhs=wdt[:, dc, 0:32], start=(dc == 0), stop=(dc == 7)),
                     reads=[wdtb, self.hnTb[tt]], writes=[PB[0]])
        f3 = lambda t: t[:]
        flat = lambda t: t[:].rearrange("p a b -> p (a b)")
        bc_h = lambda ap2: ap2.unsqueeze(1).to_broadcast([128, NT, 32])
        DT, ADT, ACUM, NACUM, EA, DEC, CD = (fam[n] for n in ("DT", "ADT", "ACUM", "NACUM", "EA", "DEC", "CD"))
        c.op("dve", lambda e: e.tensor_tensor(out=ADT[:], in0=self.bank(0).rearrange("p (a b) -> p a b", a=NT), in1=bc_h(hp[:, 0, :]),
                                              op=ALU.add), reads=[PB[0], hpb], writes=[famb])
        c.op("act", lambda e: e.activation(out=flat(EA), in_=flat(ADT), func=AF.Abs), reads=[famb], writes=[famb])
        c.op("act", lambda e: e.activation(out=flat(EA), in_=flat(EA), func=AF.Exp, scale=-1.0), reads=[famb], writes=[famb])
        c.op("act", lambda e: e.activation(out=flat(EA), in_=flat(EA), func=AF.Ln, bias=1.0), reads=[famb], writes=[famb])
        c.op("dve", lambda e: e.scalar_tensor_tensor(out=flat(DT), in0=flat(ADT), scalar=0.0, in1=flat(EA), op0=ALU.max, op1=ALU.add),
             reads=[famb], writes=[famb])
        c.op("dve", lambda e: e.tensor_tensor(out=ADT[:], in0=DT[:], in1=bc_h(hp[:, 1, :]), op=ALU.mult), reads=[famb, hpb], writes=[famb])
        c.op("pe", lambda e: e.matmul(self.bank(1), lhsT=TRI[:], rhs=flat(ADT), start=True, stop=True), reads=[cstb, famb], writes=[PB[1]])
        c.op("pe", lambda e: e.matmul(self.bank(2), lhsT=ONES[:], rhs=flat(ADT), start=True, stop=True), reads=[cstb, famb], writes=[PB[2]])
        c.op("act", lambda e: e.activation(out=flat(ACUM), in_=self.bank(1), func=AF.Copy), reads=[PB[1]], writes=[famb])
        c.op("act", lambda e: e.activation(out=flat(EA), in_=self.bank(1), func=AF.Exp), reads=[PB[1]], writes=[famb])
        c.op("act", lambda e: e.activation(out=flat(CD), in_=self.bank(2), func=AF.Exp), reads=[PB[2]], writes=[famb])
        c.op("dve", lambda e: e.tensor_scalar(out=flat(NACUM), in0=flat(ACUM), scalar1=-1.0, scalar2=None, op0=ALU.mult),
             reads=[famb], writes=[famb])
        c.op("dve", lambda e: e.tensor_tensor(out=flat(DEC), in0=self.bank(2), in1=flat(ACUM), op=ALU.subtract),
             reads=[PB[2], famb], writes=[famb])
        c.op("act", lambda e: e.activation(out=flat(DEC), in_=flat(DEC), func=AF.Exp), reads=[famb], writes=[famb])
        c.barrier()
        with ExitStack() as ges:
            gb_ = lambda name, shape, dt: ges.enter_context(nc.sbuf_tensor(name + "_o%d" % j, list(shape), dt))
            BT = gb_("BT", [128, S], BF16)
            CT = gb_("CT", [128, S], BF16)
            BCb = [Buf(), Buf()]
            XS = gb_("XS", [128, NT, 512], BF16)
            XSb = Buf()
            STG = gb_("STG", [128, 3 + S], F32)
            STGb = Buf()
            TC = gb_("TC", [128, 1024], F32)
            TCb = Buf()
            XSF = gb_("XSF", [128, 1024], BF16)
            XSFb = Buf()
            STATE = gb_("STATE", [128, 8, 64], F32)
            STb = Buf()
            STATEb = gb_("STATEb", [128, 512], BF16)
            STbb = Buf()
            LA = gb_("LA", [128, 8, 128], BF16)
            LAb = Buf()
            MT = gb_("MT", [128, 8, 128], BF16)
            MTb = Buf()
            CBm = gb_("CBm", [128, 128], BF16)
            CBmb = Buf()
            BTOK = gb_("BTOK", [128, 128], BF16)
            BTOKb = Buf()
            RH = [gb_("RH%d" % i, [128, 128], F32) for i in range(2)]
            RHb = [Buf(), Buf()]
            XDT = gb_("XDT", [128, 8, 64], BF16)
            XDTb = Buf()
            XDD = gb_("XDD", [128, 8, 64], BF16)
            XDDb = Buf()
            SZ = gb_("SZ", [128, 512], F32)
            SZb = Buf()
            W1 = gb_("W1", [128, 8, 64], F32)
            W1b = Buf()
            W2 = gb_("W2", [128, 8, 64], F32)
            W2b = Buf()
            YG = [gb_("YG%d" % i, [128, 512], BF16) for i in range(2)]
            YGb = [Buf(), Buf()]
            junk = gb_("junko", [128, 512], BF16)
            junkb = Buf()
            c.op("pool", lambda e: e.memset(STG[:, 0:3], 0.0), writes=[STGb])
            ygi = 0
            for g in range(4):
                wbc = self.wload_cols(win, [(4096 + g * 128, 128), (4608 + g * 128, 128)])
                wxs = [self.wload_cols(win, [(2048 + g * 512 + i * 256, 256)]) for i in range(2)]
                hs = slice(8 * g, 8 * g + 8)
                for fc in range(6):
                    if fc < 2:
                        wt, wb, co, q = wbc[0], wbc[1], fc * 128, 16 + 4 * fc + g
                    else:
                        jx = fc - 2
                        wt, wb, co, q = wxs[jx // 2][0], wxs[jx // 2][1], (jx % 2) * 128, g * 4 + jx
                    for hf in range(2):
                        t0 = hf * 1024
                        for qq in range(2):
                            for dc in range(8):
                                c.op("pe", lambda e: e.matmul(self.bank(qq), lhsT=wt[:, dc, co:co + 128],
                                                              rhs=self.hnT[:, dc, t0 + qq * 512:t0 + (qq + 1) * 512],
                                                              start=(dc == 0), stop=(dc == 7)),
                                     reads=[wb] + self.hnTb[hf * 8 + qq * 4:hf * 8 + qq * 4 + 4], writes=[PB[qq]])
                        c.op("act", lambda e: e.activation(out=STG[:, 3 + t0:3 + t0 + 1024], in_=self.ps[:, 0:1024], func=AF.Copy),
                             reads=[PB[0], PB[1]], writes=[STGb])
                        c.op("dve", lambda e: e.tensor_scalar(out=TC[:], in0=STG[:, 3 + t0:3 + t0 + 1024], scalar1=prm[:, q, 3:4],
                                                              scalar2=prm[:, q, 4:5], op0=ALU.mult, op1=ALU.add),
                             reads=[STGb, prmb], writes=[TCb])
                        for k in range(3):
                            c.op("dve", lambda e: e.scalar_tensor_tensor(out=TC[:], in0=STG[:, k + t0:k + t0 + 1024],
                                                                         scalar=prm[:, q, k:k + 1], in1=TC[:], op0=ALU.mult,
                                                                         op1=ALU.add), reads=[STGb, prmb, TCb], writes=[TCb])
                        if fc < 2:
                            dst = (BT, CT)[fc]
                            c.op("act", lambda e: e.activation(out=dst[:, t0:t0 + 1024], in_=TC[:], func=AF.Silu),
                                 reads=[TCb], writes=[BCb[fc]])
                        else:
                            c.op("act", lambda e: e.activation(out=XSF[:], in_=TC[:], func=AF.Silu), reads=[TCb], writes=[XSFb])
                            tbk = 2 + (fc * 2 + hf) % 2
                            for k in range(8):
                                c.op("pe", lambda e: e.transpose(out=self.bank_bf(tbk)[:, k * 128:(k + 1) * 128],
                                                                 in_=XSF[:, k * 128:(k + 1) * 128], identity=self.ident[:]),
                                     reads=[XSFb, self.identb], writes=[PB[tbk]])
                            c.op("act", lambda e: e.activation(out=XS[:, hf * 8:hf * 8 + 8, (fc - 2) * 128:(fc - 1) * 128],
                                                               in_=self.bank_bf(tbk).rearrange("p (a b) -> p a b", a=8), func=AF.Copy),
                                 reads=[PB[tbk]], writes=[XSb])
                wz = [self.wload_cols(win, [(g * 512 + i * 256, 256)]) for i in range(2)]
                c.op("pool", lambda e: e.memset(STATE[:], 0.0), reads=[STb], writes=[STb])
                c.op("pool", lambda e: e.memset(STATEb[:], 0.0), reads=[STbb], writes=[STbb])
                for tt in range(NT):
                    tsl = slice(tt * 128, (tt + 1) * 128)
                    c.op("pe", lambda e: e.matmul(self.ps[:, 4 * 512:4 * 512 + 128], lhsT=BT[:, tsl], rhs=CT[:, tsl], start=True, stop=True),
                         reads=BCb, writes=[PB[4]])
                    c.op("act", lambda e: e.activation(out=CBm[:], in_=self.ps[:, 4 * 512:4 * 512 + 128], func=AF.Copy),
                         reads=[PB[4]], writes=[CBmb])
                    c.op("pe", lambda e: e.transpose(out=self.bank_bf(5)[:, 0:128], in_=BT[:, tsl], identity=self.ident[:]),
                         reads=[BCb[0], self.identb], writes=[PB[5]])
                    c.op("act", lambda e: e.activation(out=BTOK[:], in_=self.bank_bf(5)[:, 0:128], func=AF.Copy),
                         reads=[PB[5]], writes=[BTOKb])
                    for h in range(8):
                        hh = 8 * g + h
                        rh = RH[h % 2]
                        bk = 6 + (h // 4) % 2
                        reg = self.ps[:, bk * 512 + (h % 4) * 128: bk * 512 + (h % 4 + 1) * 128]
                        c.op("dve", lambda e: e.tensor_scalar(out=rh[:], in0=TRI[:], scalar1=ADT[:, tt, hh:hh + 1], scalar2=None,
                                                              op0=ALU.mult), reads=[cstb, famb], writes=[RHb[h % 2]])
                        c.op("pe", lambda e: e.matmul(reg, lhsT=ONES[:], rhs=rh[:], start=True, stop=False),
                             reads=[cstb, RHb[h % 2]], writes=[PB[bk]])
                        c.op("pe", lambda e: e.matmul(reg, lhsT=self.identf[:], rhs=MNEG[:], start=False, stop=True),
                             reads=[cstb, self.identfb], writes=[PB[bk]])
                        c.op("act", lambda e: e.activation(out=LA[:, h, :], in_=reg, func=AF.Exp, bias=NACUM[:, tt, hh:hh + 1]),
                             reads=[PB[bk], famb], writes=[LAb])
                    c.op("dve", lambda e: e.tensor_tensor(out=MT[:], in0=LA[:], in1=CBm[:].unsqueeze(1).to_broadcast([128, 8, 128]),
                                                          op=ALU.mult), reads=[LAb, CBmb], writes=[MTb])
                    xs3 = XS[:, tt, :].rearrange("p (a b) -> p a b", a=8)
                    bc_p = lambda ap2: ap2.unsqueeze(2).to_broadcast([128, 8, 64])
                    c.op("dve", lambda e: e.tensor_tensor(out=XDT[:], in0=xs3, in1=bc_p(DT[:, tt, hs]), op=ALU.mult),
                         reads=[XSb, famb], writes=[XDTb])
                    c.op("dve", lambda e: e.tensor_tensor(out=XDD[:], in0=XDT[:], in1=bc_p(DEC[:, tt, hs]), op=ALU.mult),
                         reads=[XDTb, famb], writes=[XDDb])
                    for h in range(8):
                        c.op("pe", lambda e: e.matmul(self.ps[:, 2 * 512 + h * 64: 2 * 512 + (h + 1) * 64], lhsT=MT[:, h, :],
                                                      rhs=XDT[:, h, :], start=True, stop=True), reads=[MTb, XDTb], writes=[PB[2]])
                    c.op("pe", lambda e: e.matmul(self.bank(3), lhsT=CT[:, tsl], rhs=STATEb[:], start=True, stop=True),
                         reads=[BCb[1], STbb], writes=[PB[3]])
                    c.op("pe", lambda e: e.matmul(self.bank(0), lhsT=BTOK[:], rhs=XDD[:].rearrange("p a b -> p (a b)"), start=True, stop=True),
                         reads=[BTOKb, XDDb], writes=[PB[0]])
                    for i in range(2):
                        for dc in range(8):
                            c.op("pe", lambda e: e.matmul(self.ps[:, 512 + i * 256:512 + (i + 1) * 256], lhsT=self.hnT[:, dc, tsl],
                                                          rhs=wz[i][0][:, dc, :], start=(dc == 0), stop=(dc == 7)),
                                 reads=[wz[i][1], self.hnTb[tt]], writes=[PB[1]])
                    c.op("act", lambda e: e.activation(out=SZ[:], in_=self.bank(1), func=AF.Silu), reads=[PB[1]], writes=[SZb])
                    c.op("dve", lambda e: e.tensor_tensor(out=W1[:], in0=self.bank(3).rearrange("p (a b) -> p a b", a=8),
                                                          in1=bc_p(EA[:, tt, hs]), op=ALU.mult), reads=[PB[3], famb], writes=[W1b])
                    c.op("dve", lambda e: e.tensor_tensor(out=W1[:], in0=self.bank(2).rearrange("p (a b) -> p a b", a=8), in1=W1[:],
                                                          op=ALU.add), reads=[PB[2], W1b], writes=[W1b])
                    c.op("dve", lambda e: e.tensor_tensor(out=W2[:], in0=xs3, in1=bc_p(hp[:, 2, hs]), op=ALU.mult),
                         reads=[XSb, hpb], writes=[W2b])
                    c.op("pool", lambda e: e.tensor_tensor(out=W1[:], in0=W1[:], in1=W2[:], op=ALU.add), reads=[W1b, W2b], writes=[W1b])
                    yg = YG[ygi % 2]
                    c.op("dve", lambda e: e.tensor_tensor(out=yg[:], in0=W1[:].rearrange("p a b -> p (a b)"), in1=SZ[:], op=ALU.mult),
                         reads=[W1b, SZb], writes=[YGb[ygi % 2]])
                    c.op("act", lambda e: e.activation(out=junk[:], in_=yg[:], func=AF.Square, accum_out=SS[:, tt, g:g + 1]),
                         reads=[YGb[ygi % 2]], writes=[junkb, SSb])
                    c.dma("sp", "ld2" if ygi % 2 == 0 else "ld3", ygs[tt * 128:(tt + 1) * 128, g * 512:(g + 1) * 512], yg[:],
                          reads=[YGb[ygi % 2]])
                    ygi += 1
                    c.op("dve", lambda e: e.tensor_tensor(out=STATE[:], in0=STATE[:], in1=bc_p(CD[:, tt, hs]), op=ALU.mult),
                         reads=[STb, famb], writes=[STb])
                    c.op("dve", lambda e: e.tensor_tensor(out=STATE[:], in0=self.bank(0).rearrange("p (a b) -> p a b", a=8), in1=STATE[:],
                                                          op=ALU.add), reads=[PB[0], STb], writes=[STb])
                    c.op("act", lambda e: e.activation(out=STATEb[:], in_=STATE[:].rearrange("p a b -> p (a b)"), func=AF.Copy),
                         reads=[STb], writes=[STbb])
            c.barrier()
        RS = ob("RS", [128, 3, NT], F32)
        RSb = Buf()
        c.op("dve", lambda e: e.tensor_reduce(out=RS[:, 0, :], in_=SS[:], axis=AX.X, op=ALU.add), reads=[SSb], writes=[RSb])
        c.op("act", lambda e: e.activation(out=RS[:, 1, :], in_=RS[:, 0, :], func=AF.Sqrt, bias=EPS, scale=1.0 / 2048), reads=[RSb], writes=[RSb])
        c.op("dve", lambda e: e.reciprocal(out=RS[:, 2, :], in_=RS[:, 1, :]), reads=[RSb], writes=[RSb])
        WO = ob("WO", [128, 16, D], BF16)
        WOb = Buf()
        for r in range(8):
            c.dma("pool", "ld4", WO[:, 2 * r:2 * r + 2, :], wout[r * 256:(r + 1) * 256, :].rearrange("(a p) d -> p a d", p=128), writes=[WOb])
        YGF = [ob("YGF%d" % i, [128, 2048], BF16) for i in range(2)]
        YGFb = [Buf(), Buf()]
        YGT = [ob("YGT%d" % i, [128, 16, 128], BF16) for i in range(2)]
        YGTb = [Buf(), Buf()]
        for tt in range(NT):
            k = tt % 2
            c.dma("sp", "ld2" if k == 0 else "ld3", YGF[k][:], ygs[tt * 128:(tt + 1) * 128, :], writes=[YGFb[k]])
            b0 = 4 + 2 * k
            for ch in range(16):
                bk = b0 + ch // 8
                c.op("pe", lambda e: e.transpose(out=self.bank_bf(bk)[:, (ch % 8) * 128:(ch % 8 + 1) * 128],
                                                 in_=YGF[k][:, ch * 128:(ch + 1) * 128], identity=self.ident[:]),
                     reads=[YGFb[k], self.identb], writes=[PB[bk]])
            src = self.ps[:, b0 * 512:(b0 + 2) * 512].bitcast(BF16).rearrange("p (a b) -> p a b", a=16)
            c.op("dve", lambda e: e.tensor_tensor(out=YGT[k][:], in0=src, in1=GN[:].unsqueeze(2).to_broadcast([128, 16, 128]), op=ALU.mult),
                 reads=[PB[b0], PB[b0 + 1], prmb], writes=[YGTb[k]])
            for dh in range(2):
                bk = 2 * k + dh
                for ch in range(16):
                    c.op("pe", lambda e: e.matmul(self.bank(bk), lhsT=YGT[k][:, ch, :], rhs=WO[:, ch, dh * 512:(dh + 1) * 512],
                                                  start=(ch == 0), stop=(ch == 15)), reads=[YGTb[k], WOb], writes=[PB[bk]])
                xs_ = self.xres[:, tt, dh * 512:(dh + 1) * 512]
                c.op("dve", lambda e: e.scalar_tensor_tensor(out=xs_, in0=self.bank(bk), scalar=RS[:, 2, tt:tt + 1], in1=xs_,
                                                             op0=ALU.mult, op1=ALU.add),
                     reads=[PB[bk], RSb, self.xb[tt]], writes=[self.xb[tt]])
        c.barrier()
        es.close()

    def final_norm(self):
        c = self.c
        nes = self.norm_open()
        self.es.enter_context(nes)
        self.load_gain(self.dr["final_norm"])
        self.norm_stats()
        for tt in range(NT):
            c.op("dve", lambda e: e.scalar_tensor_tensor(out=self.xres[:, tt, :], in0=self.xres[:, tt, :],
                                                         scalar=self.stat[:, 16 + tt:17 + tt], in1=self.gbc[:], op0=ALU.mult,
                                                         op1=ALU.mult), reads=[self.xb[tt], self.statb, self.gbcb], writes=[self.xb[tt]])


LAST_INPUT_NAMES = []


def _build_once(stages, needed):
    nc = bass.Bass("TRN2", target_bir_lowering=False)
    dram = {}

    def din(name, shape):
        dram[name] = nc.dram_tensor(name, list(shape), F32, kind="ExternalInput").ap()
        if name not in LAST_INPUT_NAMES:
            LAST_INPUT_NAMES.append(name)

    din("x", [S, D])
    din("ffn_norm", [2, D])
    din("mix_norm", [2, D])
    din("final_norm", [D])
    din("even_w_in", [1, D, 6144])
    din("lru_conv_w", [1, 4, D])
    din("lru_conv_b", [1, D])
    din("lru_w_a", [1, 16, 64, 64])
    din("lru_b_a", [1, D])
    din("lru_w_i", [1, 16, 64, 64])
    din("lru_b_i", [1, D])
    din("lru_lambda", [1, D])
    din("even_w_out", [1, 2048, D])
    din("ssm_w_in", [1, D, 5152])
    din("ssm_conv_w", [1, 4, 3072])
    din("ssm_conv_b", [1, 3072])
    din("ssm_dt_bias", [1, 32])
    din("ssm_a_log", [1, 32])
    din("ssm_d", [1, 32])
    din("ssm_norm", [1, 2048])
    din("ssm_w_out", [1, 2048, D])
    dram["ygs"] = nc.dram_tensor("ygs", [S, 2048], BF16, kind="Internal").ap()
    din("peer_w_q", [2, D, D])
    din("peer_sub_keys", [2, 8, 2, 128, 64])
    din("peer_u", [2, 16384, D])
    din("peer_v", [2, 16384, D])
    dram["y"] = nc.dram_tensor("y", [S, D], F32, kind="ExternalOutput").ap()
    with ExitStack() as es:
        k = Kern(nc, es, dram, needed)
        k.load_x()
        for st in stages:
            if st == "peer0":
                k.peer(0)
            elif st == "peer1":
                k.peer(1)
            elif st == "odd":
                k.odd_mixer(0)
            elif st == "even":
                k.even_mixer(0)
            elif st == "final":
                k.final_norm()
        k.store_x("y")
    return nc, k.c.waited


def build(stages=("peer0",), out_name="y"):
    _, waited = _build_once(stages, None)
    nc, waited2 = _build_once(stages, waited)
    assert waited == waited2
    return nc


ALL_STAGES = ("even", "peer0", "odd", "peer1", "final")


def kernel(**inputs):
    nc = build(ALL_STAGES)
    names = [n for n in LAST_INPUT_NAMES if n != "x"]
    shared = {n: np.ascontiguousarray(np.asarray(inputs[n], dtype=np.float32)) for n in names}
    x = np.asarray(inputs["x"], dtype=np.float32)
    nb = x.shape[0]
    maps = [dict(shared, x=np.ascontiguousarray(x[b])) for b in range(nb)]
    res = run_bass_kernel_spmd(nc, maps, core_ids=list(range(nb)))
    return np.stack([np.asarray(res.results[b]["y"], dtype=np.float32) for b in range(nb)], axis=0)
```
